# Optimizing a Trainium2 kernel written in Bass

```python
import math
import jax, jax.numpy as jnp
from jax import lax
import numpy as np

D_MODEL = 1024
BATCH = 4
SEQ = 8192
DEPTH = 4

N_MIXERS = 2
N_CONV_LAYERS = (DEPTH + 1) // 2
N_ATTN_LAYERS = DEPTH // 2
WIDTH = D_MODEL
N_PROJ = 4
CONV_K = 3
N_DIFF_HEADS = 8
HEAD_DIM = 64
V_DIM = 2 * HEAD_DIM
N_MAPS = 2 * N_DIFF_HEADS
NUM_BUCKETS = 32
MAX_EXACT = NUM_BUCKETS // 2
REL_MAX_DIST = 128
Q_BLOCK = 128
RMS_EPS = 1e-6

kernel_name = "hybrid_shortconv_diffattn_trunk"


def rmsnorm(x, g):
    xf = x.astype(jnp.float32)
    y = xf * lax.rsqrt(jnp.mean(xf * xf, axis=-1, keepdims=True) + RMS_EPS)
    return (y * g.astype(jnp.float32)).astype(x.dtype)


def t5_causal_bucket(dist):
    d_safe = jnp.maximum(dist, 1).astype(jnp.float32)
    large = MAX_EXACT + (jnp.log(d_safe / MAX_EXACT) / math.log(REL_MAX_DIST / MAX_EXACT)
                         * (NUM_BUCKETS - MAX_EXACT)).astype(jnp.int32)
    large = jnp.minimum(large, NUM_BUCKETS - 1)
    return jnp.where(dist < MAX_EXACT, dist, large)


def lambda_init_for(layer_idx):
    return 0.8 - 0.6 * math.exp(-0.3 * layer_idx)


def short_conv_mixer(h, w_in, w_out, conv_w):
    s = h.shape[1]
    proj = h @ w_in
    b_gate, c_gate, u, z = jnp.split(proj, N_PROJ, axis=-1)
    v = c_gate * u
    vp = jnp.pad(v, ((0, 0), (CONV_K - 1, 0), (0, 0)))
    conv = (vp[:, 0:s] * conv_w[:, 0] + vp[:, 1:s + 1] * conv_w[:, 1]
            + vp[:, 2:s + 2] * conv_w[:, 2])
    y = b_gate * conv * jax.nn.silu(z)
    return y @ w_out


def diff_attn_mixer(h, w_in, w_out, lq1, lk1, lq2, lk2, subln_g, bias_dist, lambda_init):
    bsz, s = h.shape[0], h.shape[1]
    proj = h @ w_in
    q, k, v, z = jnp.split(proj, N_PROJ, axis=-1)
    q = q.reshape(bsz, s, N_MAPS, HEAD_DIM)
    k = k.reshape(bsz, s, N_MAPS, HEAD_DIM)
    v = v.reshape(bsz, s, N_DIFF_HEADS, V_DIM)
    lam = (jnp.exp(jnp.sum(lq1.astype(jnp.float32) * lk1.astype(jnp.float32)))
           - jnp.exp(jnp.sum(lq2.astype(jnp.float32) * lk2.astype(jnp.float32)))
           + lambda_init)
    scale = HEAD_DIM ** -0.5
    n_blk = s // Q_BLOCK
    q_blocks = q.reshape(bsz, n_blk, Q_BLOCK, N_MAPS, HEAD_DIM).transpose(1, 0, 2, 3, 4)
    k_pos = jnp.arange(s, dtype=jnp.int32)

    def block(args):
        q_blk, blk_idx = args
        q_pos = blk_idx * Q_BLOCK + jnp.arange(Q_BLOCK, dtype=jnp.int32)
        logits = jnp.einsum('bqhd,bkhd->bhqk', q_blk, k,
                            preferred_element_type=jnp.float32) * scale
        dist = q_pos[:, None] - k_pos[None, :]
        bias = bias_dist[:, jnp.maximum(dist, 0)].astype(jnp.float32)
        logits = jnp.where(dist[None, None] >= 0, logits + bias[None], -jnp.inf)
        p = jax.nn.softmax(logits, axis=-1).reshape(bsz, N_DIFF_HEADS, 2, Q_BLOCK, s)
        a = p[:, :, 0] - lam * p[:, :, 1]
        o = jnp.einsum('bhqk,bkhe->bqhe', a.astype(v.dtype), v)
        return rmsnorm(o, subln_g) * (1.0 - lambda_init)

    o = lax.map(block, (q_blocks, jnp.arange(n_blk, dtype=jnp.int32)))
    o = o.transpose(1, 0, 2, 3, 4).reshape(bsz, s, N_DIFF_HEADS * V_DIM)
    return (o * jax.nn.silu(z)) @ w_out


def setup_inputs(seed: int = 0) -> dict:
    key = jax.random.key(seed)
    ks = jax.random.split(key, 13)
    f32 = jnp.float32
    x = jax.random.normal(ks[0], (BATCH, SEQ, D_MODEL), f32)
    norm_g = 1.0 + 0.02 * jax.random.normal(ks[1], (DEPTH, D_MODEL), f32)
    w_in = jax.random.normal(ks[2], (DEPTH, D_MODEL, N_PROJ * WIDTH), f32) * D_MODEL ** -0.5
    w_out = jax.random.normal(ks[3], (DEPTH, WIDTH, D_MODEL), f32) * WIDTH ** -0.5
    conv_w = jax.random.normal(ks[4], (N_CONV_LAYERS, WIDTH, CONV_K), f32) * CONV_K ** -0.5
    lambda_q1 = 0.1 * jax.random.normal(ks[5], (N_ATTN_LAYERS, HEAD_DIM), f32)
    lambda_k1 = 0.1 * jax.random.normal(ks[6], (N_ATTN_LAYERS, HEAD_DIM), f32)
    lambda_q2 = 0.1 * jax.random.normal(ks[7], (N_ATTN_LAYERS, HEAD_DIM), f32)
    lambda_k2 = 0.1 * jax.random.normal(ks[8], (N_ATTN_LAYERS, HEAD_DIM), f32)
    subln_g = 1.0 + 0.02 * jax.random.normal(ks[9], (N_ATTN_LAYERS, V_DIM), f32)
    rel_bias = 0.5 * jax.random.normal(ks[10], (NUM_BUCKETS, N_MAPS), f32)
    final_g = 1.0 + 0.02 * jax.random.normal(ks[11], (D_MODEL,), f32)
    return {"x": x, "norm_g": norm_g, "w_in": w_in, "w_out": w_out, "conv_w": conv_w,
            "lambda_q1": lambda_q1, "lambda_k1": lambda_k1, "lambda_q2": lambda_q2,
            "lambda_k2": lambda_k2, "subln_g": subln_g, "rel_bias": rel_bias,
            "final_g": final_g}


def reference(x, norm_g, w_in, w_out, conv_w, lambda_q1, lambda_k1, lambda_q2, lambda_k2,
              subln_g, rel_bias, final_g):
    s = x.shape[1]
    bias_dist = rel_bias[t5_causal_bucket(jnp.arange(s, dtype=jnp.int32))].T
    for i in range(DEPTH):
        h = rmsnorm(x, norm_g[i])
        j = i // N_MIXERS
        if i % N_MIXERS == 0:
            out = short_conv_mixer(h, w_in[i], w_out[i], conv_w[j])
        else:
            out = diff_attn_mixer(h, w_in[i], w_out[i], lambda_q1[j], lambda_k1[j],
                                  lambda_q2[j], lambda_k2[j], subln_g[j], bias_dist,
                                  lambda_init_for(i))
        x = x + out
    return rmsnorm(x, final_g)
```

```python
import math
import numpy as np
import ml_dtypes
import concourse.bass as bass
import concourse.mybir as mybir
from concourse.bass_utils import run_bass_kernel_spmd

F32 = mybir.dt.float32
BF16 = mybir.dt.bfloat16
AF = mybir.ActivationFunctionType
ALU = mybir.AluOpType

D = 1024
S = 8192
TOK = 4096
NT = TOK // 512
EPS = 1e-6
MASKV = -30000.0
ENGS = ("tensor", "vector", "scalar", "gpsimd", "sync")


class Sched:
    def __init__(self):
        self.ops = {e: [] for e in ENGS}
        self.cnt = {}
        self.sems = {}
        self.bar = []

    def op(self, eng, fn, waits=(), signal=True):
        key = "e_" + eng
        tok = None
        if signal:
            self.cnt[key] = self.cnt.get(key, 0) + 1
            tok = (key, self.cnt[key])
        self.ops[eng].append((fn, self._dd(tuple(waits) + tuple(self.bar)), key if signal else None, 1))
        return tok

    @staticmethod
    def _dd(waits):
        mx = {}
        for w in waits:
            if w is None:
                continue
            k, v = w
            if mx.get(k, 0) < v:
                mx[k] = v
        return tuple(mx.items())

    def dma(self, eng, out, in_, stream, waits=()):
        key = "d_" + stream
        self.cnt[key] = self.cnt.get(key, 0) + 16
        tok = (key, self.cnt[key])
        self.ops[eng].append((lambda e, o=out, i=in_: e.dma_start(out=o, in_=i),
                              self._dd(tuple(waits) + tuple(self.bar)), key, 16))
        return tok

    def coll(self, fn, name, waits=()):
        key = "c_" + name
        self.cnt[key] = self.cnt.get(key, 0) + 1
        tok = (key, self.cnt[key])
        self.ops["gpsimd"].append((fn, self._dd(tuple(waits) + tuple(self.bar)), key, 1))
        return tok

    def barrier(self):
        self.bar = [(k, v) for k, v in self.cnt.items()]

    def sem_keys(self):
        return sorted(self.cnt.keys())

    def replay(self, eng_name, eng):
        waited = {}
        for fn, waits, key, inc in self.ops[eng_name]:
            for (k, v) in waits:
                if waited.get(k, 0) < v:
                    eng.wait_ge(self.sems[k], v)
                    waited[k] = v
            ins = fn(eng)
            if key is not None:
                ins.then_inc(self.sems[key], inc)

    def final_waits(self, eng_name, eng, toks):
        for (k, v) in toks:
            eng.wait_ge(self.sems[k], v)


def run_sched(nc, sch, final_toks):
    from contextlib import ExitStack
    with ExitStack() as st:
        for k in sch.sem_keys():
            sch.sems[k] = st.enter_context(nc.semaphore(k))
        block = st.enter_context(nc.Block())

        @block.sync
        def _(e):
            sch.replay("sync", e)
            sch.final_waits("sync", e, final_toks)

        @block.tensor
        def _(e):
            sch.replay("tensor", e)

        @block.vector
        def _(e):
            sch.replay("vector", e)

        @block.scalar
        def _(e):
            sch.replay("scalar", e)

        @block.gpsimd
        def _(e):
            sch.replay("gpsimd", e)


def build_fused(lam_inits=(0.0, 0.0), debug_out=False, upto=99):
    from contextlib import ExitStack
    nc = bass.Bass("TRN2", target_bir_lowering=False)

    def ext_in(name, shape, dt=F32):
        return nc.dram_tensor(name, list(shape), dt, kind="ExternalInput").ap()

    def internal(name, shape, dt):
        return nc.dram_tensor(name, list(shape), dt).ap()

    x0 = ext_in("x0", [TOK, D])
    xhalo = ext_in("xhalo", [2, D])
    w_in = ext_in("w_in", [4, D, 4 * D])
    w_out = ext_in("w_out", [4, D, D])
    w_qkv = ext_in("w_qkv", [2, D, 1536])
    g_rep = ext_in("norm_g_rep", [4, 128, D])
    fg_rep = ext_in("final_g_rep", [128, D])
    conv_w = ext_in("conv_w", [2, D, 3])
    ident_in = ext_in("ident", [128, 128])
    Zb = ext_in("Zb", [8, 128, 1024])
    cfar = ext_in("cfar", [128, 8])
    lqk = ext_in("lqk", [2, 128, 4, 64])
    gsub_in = ext_in("gsub", [2, 128, 128])
    sel_in = ext_in("sel", [128, 2])
    hmask_in = ext_in("hmask", [128, 1])
    out_ap = nc.dram_tensor("out", [TOK, D], F32, kind="ExternalOutput").ap()

    X1 = internal("X1", [TOK, D], F32)
    X2 = internal("X2", [TOK, D], F32)
    X3 = internal("X3", [TOK, D], F32)
    HTown = internal("HTown", [8 * 128, TOK], BF16)
    HTall = internal("HTall", [2 * 8 * 128, TOK], BF16)
    ZTd = internal("ZTd", [8, 128, TOK], BF16)
    QTa = internal("QTa", [4, 128, S], BF16)
    KTa = internal("KTa", [4, 128, S], BF16)
    Va = internal("Va", [4, S, 128], BF16)
    RSbuf = internal("RSbuf", [2 * 8 * 128, TOK], BF16)
    ONTr = internal("ONTr", [8 * 128, TOK], BF16)
    TAILown = internal("TAILown", [2, D], F32)
    TAILall = internal("TAILall", [4, D], F32)
    HTown3 = HTown.rearrange("(k p) t -> k p t", p=128)
    HTall5 = HTall.rearrange("(pc r kk p) t -> pc r kk p t", pc=4, r=2, kk=2, p=128)
    RS5 = RSbuf.rearrange("(pc d hh p) t -> pc d hh p t", pc=4, d=2, hh=2, p=128)
    ONTr3 = ONTr.rearrange("(h p) t -> h p t", p=128)
    GROUPS = [[0, 1], [2, 3], [4, 5], [6, 7]]

    sch = Sched()
    est = ExitStack()
    ARENA_BYTES = 206848
    arena = est.enter_context(nc.sbuf_tensor("arena", [128, ARENA_BYTES // 2], BF16))
    psA = est.enter_context(nc.psum_tensor("psA", [128, 2048], F32))
    psB = est.enter_context(nc.psum_tensor("psB", [128, 2048], F32))

    def view(off, shape, dt):
        n = 1
        for s_ in shape[1:]:
            n *= s_
        nb = n * (4 if dt == F32 else 2)
        assert off % 4 == 0 and off + nb <= ARENA_BYTES, (off, nb)
        a = arena[:, off // 2:(off + nb) // 2]
        if dt == F32:
            a = a.bitcast(F32)
        if len(shape) == 3:
            a = a.rearrange("p (a b) -> p a b", a=shape[1])
        return a

    class Alloc:
        def __init__(self, base):
            self.off = base

        def __call__(self, shape, dt):
            n = 1
            for s_ in shape[1:]:
                n *= s_
            nb = n * (4 if dt == F32 else 2)
            nb_al = (nb + 31) // 32 * 32
            v = view(self.off, shape, dt)
            self.off += nb_al
            return v

    ta_ = Alloc(0)
    wstage = [ta_([128, 2048], F32) for _ in range(2)]
    w_in_sb = ta_([128, 8, 4096], BF16)
    w_out_sb = ta_([128, 8, 1024], BF16)
    xtA = [ta_([128, D], F32) for _ in range(4)]
    xtB = [wstage[0][:, 0:1024], wstage[0][:, 1024:2048], wstage[1][:, 0:1024], wstage[1][:, 1024:2048]]
    xt2 = [xtA, xtB]
    xt = xtA
    junk = ta_([128, D], BF16)
    hb = [ta_([128, D], BF16) for _ in range(2)]
    hT = [ta_([128, 8, 512], BF16) for _ in range(2)]
    hT_halo = ta_([128, 8, 128], BF16)
    xh = ta_([128, D], F32)
    ut = ta_([128, 512], F32)
    sz = ta_([128, 512], F32)
    vbuf = [ta_([128, 514], F32) for _ in range(8)]
    t1 = ta_([128, 512], F32)
    t2 = ta_([128, 512], F32)
    yT = ta_([128, 8, 512], BF16)
    ev = [ta_([128, 512], BF16) for _ in range(4)]
    xo = [ta_([128, D], F32) for _ in range(2)]
    tok_end = ta_.off
    onl, ztl = hT[0], hT[1]
    aa_ = Alloc(0)
    QT = [aa_([128, S], BF16) for _ in range(2)]
    KT = [aa_([128, S], BF16) for _ in range(2)]
    VS = [aa_([128, 64, 130], BF16) for _ in range(2)]
    ZC = [aa_([128, 2, 1024], F32) for _ in range(2)]
    NP = 6
    PT = [aa_([128, 2, 512], BF16) for _ in range(NP)]
    SSB = [aa_([128, 2, 512], F32) for _ in range(2)]
    OEV = [aa_([128, 8, 129], F32) for _ in range(2)]
    tA = [aa_([128, 128], F32) for _ in range(4)]
    oo = [aa_([128, 128], F32) for _ in range(4)]
    junkA = aa_([128, 128], F32)
    onb = [aa_([128, 128], BF16) for _ in range(4)]
    onTa = [aa_([128, 512], BF16) for _ in range(2)]
    onTb = [aa_([128, 512], BF16) for _ in range(2)]
    att_end = aa_.off
    pa_ = Alloc(max(tok_end, att_end))
    g_sb = pa_([128, D], F32)
    fg_sb = pa_([128, D], F32)
    ident_f = pa_([128, 128], F32)
    ident = pa_([128, 128], BF16)
    cw_sb = pa_([128, 8, 3], F32)
    ss = pa_([128, 8], F32)
    eps_sb = pa_([128, 1], F32)
    cf_sb = pa_([128, 8], F32)
    lq_sb = pa_([128, 4, 64], F32)
    lprod = pa_([128, 2, 64], F32)
    lsum = pa_([128, 4], F32)
    neglam = pa_([128, 1], F32)
    gs_sb = pa_([128, 128], F32)
    rr = pa_([128, 4, 2], F32)
    r2 = pa_([128, 4], F32)
    q1 = pa_([128, 4], F32)
    sel_sb = pa_([128, 2], F32)
    hmask_sb = pa_([128, 1], F32)
    assert pa_.off <= ARENA_BYTES, pa_.off
    pT = psA[:, 0:1024].rearrange("p (a b) -> p a b", a=8)
    pO = psA[:, 1024:2048]
    pP = [psB[:, i * 512:(i + 1) * 512] for i in range(4)]
    SP = [psA[:, 0:1024].rearrange("p (a b) -> p a b", a=2), psA[:, 1024:2048].rearrange("p (a b) -> p a b", a=2)]
    OA = psB[:, :].rearrange("p (a b) -> p a b", a=8)

    last = {}

    def dep(*names):
        out = []
        for n in names:
            out.extend(last.get(n, []))
        return out

    def setlast(name, *toks):
        last[name] = [t for t in toks if t is not None]

    def addlast(name, *toks):
        last.setdefault(name, []).extend([t for t in toks if t is not None])

    def phase_barrier():
        sch.barrier()
        last.clear()

    def MM(out, lhsT, rhs, start, stop, waits=(), signal=False):
        return sch.op("tensor", lambda e, o=out, l=lhsT, r=rhs, a=start, b=stop:
                      e.matmul(o, l, r, start=a, stop=b, skip_group_check=True), waits, signal)

    def ACT(out, in_, func, waits=(), accum_out=None, scale=None, bias=None):
        kw = {}
        if accum_out is not None:
            kw["accum_out"] = accum_out
        if scale is not None:
            kw["scale"] = scale
        if bias is not None:
            kw["bias"] = bias
        return sch.op("scalar", lambda e, o=out, i=in_, f=func, kw=kw: e.activation(out=o, in_=i, func=f, **kw), waits)

    def TT(eng, out, in0, in1, op, waits=()):
        return sch.op(eng, lambda e, o=out, a=in0, b=in1, p=op: e.tensor_tensor(out=o, in0=a, in1=b, op=p), waits)

    def TS(eng, out, in0, s1, s2, op0, op1=None, waits=()):
        if op1 is None:
            return sch.op(eng, lambda e, o=out, a=in0, x=s1, p=op0: e.tensor_scalar(out=o, in0=a, scalar1=x, scalar2=None, op0=p), waits)
        return sch.op(eng, lambda e, o=out, a=in0, x=s1, y=s2, p=op0, q=op1: e.tensor_scalar(out=o, in0=a, scalar1=x, scalar2=y, op0=p, op1=q), waits)

    def STT(eng, out, in0, scalar, in1, op0, op1, waits=()):
        return sch.op(eng, lambda e, o=out, a=in0, s=scalar, b=in1, p=op0, q=op1:
                      e.scalar_tensor_tensor(out=o, in0=a, scalar=s, in1=b, op0=p, op1=q), waits)

    def CP(eng, out, in_, waits=()):
        if eng == "scalar":
            return ACT(out, in_, AF.Copy, waits)
        return sch.op(eng, lambda e, o=out, i=in_: e.tensor_copy(out=o, in_=i), waits)

    dma_rr = [0]

    def dq():
        return "sync"

    t_id = sch.dma("sync", ident_f, ident_in[:, :], "c0")
    t_idc = ACT(ident, ident_f, AF.Copy, [t_id])
    t_eps = sch.op("vector", lambda e: e.memset(eps_sb, EPS))
    t_cf = sch.dma("sync", cf_sb, cfar[:, :], "c1")
    t_sel = sch.dma("sync", sel_sb, sel_in[:, :], "c2")
    t_hm = sch.dma("sync", hmask_sb, hmask_in[:, :], "c3")
    t_fg = sch.dma("sync", fg_sb, fg_rep[:, :], "c4")
    CONST = [t_idc, t_eps, t_cf, t_sel, t_hm, t_fg]

    final_toks = []

    def load_w(pieces):
        toks = []
        for idx, (src_ap, vfn, dst_ap) in enumerate(pieces):
            s = idx % 2
            td = sch.dma(dq(), vfn(wstage[s]), src_ap, f"w{s}", dep(f"wstage{s}"))
            tc_ = CP("vector" if idx % 2 == 0 else "scalar", dst_ap, vfn(wstage[s]), [td])
            setlast(f"wstage{s}", tc_)
            toks.append(tc_)
        return toks

    def w_in_pieces(layer, c_lo, c_hi):
        ps_ = []
        for k in range(8):
            for c0 in range(c_lo, c_hi, 2048):
                n = min(2048, c_hi - c0)
                ps_.append((w_in[layer, k * 128:(k + 1) * 128, c0:c0 + n], (lambda ws, n=n: ws[:, 0:n]),
                            w_in_sb[:, k, c0:c0 + n]))
        return ps_

    def w_qkv_pieces(j):
        return [(w_qkv[j, k * 128:(k + 1) * 128, :], (lambda ws: ws[:, 0:1536]), w_in_sb[:, k, 0:1536]) for k in range(8)]

    def w_out_pieces(layer):
        return [(w_out[layer, k * 128:(k + 2) * 128, :].rearrange("(a p) c -> p a c", p=128),
                 (lambda ws: ws[:, :].rearrange("p (a c) -> p a c", a=2)), w_out_sb[:, k:k + 2, :]) for k in range(0, 8, 2)]

    def issue_xloads(src_ap, tok0, par):
        toks = []
        for sub in range(4):
            nm = f"xt{par}_{sub}"
            w = dep(nm) + (dep("wstage0", "wstage1") if par == 1 else [])
            toks.append(sch.dma("sync", xt2[par][sub], src_ap[tok0 + sub * 128: tok0 + (sub + 1) * 128, :], f"x_{nm}", w))
        return toks

    def norm_tile(src_ap, row0, col, hslot, xb, name, nrows=128, mask=None, tl=None):
        if tl is None:
            tl = sch.dma(dq(), xb[0:nrows, :], src_ap[row0:row0 + nrows, :], f"x_{name}", dep(name))
        if mask is not None:
            tl = TS("vector", xb[0:nrows, :], xb[0:nrows, :], mask[0:nrows, 0:1], None, ALU.mult, None, [tl] + CONST)
        tsq = ACT(junk, xb, AF.Square, [tl] + CONST + dep(f"ss{col}", "junk"), accum_out=ss[:, col:col + 1])
        setlast("junk", tsq)
        tr1 = ACT(ss[:, col:col + 1], ss[:, col:col + 1], AF.Ln, [tsq], scale=1.0 / D, bias=eps_sb[:, 0:1])
        tr2 = ACT(ss[:, col:col + 1], ss[:, col:col + 1], AF.Exp, [tr1], scale=-0.5)
        th = STT("vector", hb[hslot], xb, ss[:, col:col + 1], g_sb, ALU.mult, ALU.mult,
                 [tr2] + dep(f"hb{hslot}", "g_sb"))
        setlast(f"ss{col}", th)
        setlast(name, th)
        return th

    def transpose_hb(th, hslot):
        w = [th] + dep("pT") + CONST
        tm = None
        for k in range(8):
            tm = MM(pT[:, k, :], hb[hslot][:, k * 128:(k + 1) * 128], ident, True, True,
                    w if k == 0 else [], signal=(k == 7))
        setlast(f"hb{hslot}", tm)
        return tm

    def load_g(layer):
        tg = sch.dma("sync", g_sb, g_rep[layer, :, :], "g", dep("g_sb"))
        setlast("g_sb", tg)

    def norm_to_hT(src, tok0, hs, par=0, xl=None):
        hw = []
        for sub in range(4):
            hslot = sub % 2
            th = norm_tile(src, tok0 + sub * 128, sub, hslot, xt2[par][sub], f"xt{par}_{sub}",
                           tl=(xl[sub] if xl is not None else None))
            tm = transpose_hb(th, hslot)
            tcp = CP("scalar" if sub % 2 == 0 else "vector", hT[hs][:, :, sub * 128:(sub + 1) * 128], pT,
                     [tm] + dep(f"hT{hs}"))
            setlast("pT", tcp)
            hw.append(tcp)
        return hw

    def outproj_residual(src, dst, tok0, yw, load_x, final, par=0, xl=None):
        tmm = None
        for sub in range(4):
            xs = sub % 2
            txl = xl[sub] if xl is not None else None
            for half in range(2):
                for j in range(8):
                    tmm = MM(pO[:, half * 512:(half + 1) * 512], yT[:, j, sub * 128:(sub + 1) * 128],
                             w_out_sb[:, j, half * 512:(half + 1) * 512], j == 0, j == 7,
                             (yw + dep("pO", "w_out_sb")) if (j == 0 and half == 0) else [],
                             signal=(j == 7 and half == 1))
            tad = TT("vector", xo[xs], pO, xt2[par][sub], ALU.add, [tmm, txl] + dep(f"xo{xs}"))
            setlast("pO", tad)
            setlast(f"xt{par}_{sub}", tad)
            if final:
                c_ = 4 + xs
                tsq = ACT(junk, xo[xs], AF.Square, [tad] + CONST + dep(f"ss{c_}", "junk"), accum_out=ss[:, c_:c_ + 1])
                setlast("junk", tsq)
                tr1 = ACT(ss[:, c_:c_ + 1], ss[:, c_:c_ + 1], AF.Ln, [tsq], scale=1.0 / D, bias=eps_sb[:, 0:1])
                tr2 = ACT(ss[:, c_:c_ + 1], ss[:, c_:c_ + 1], AF.Exp, [tr1], scale=-0.5)
                tad = STT("vector", xo[xs], xo[xs], ss[:, c_:c_ + 1], fg_sb, ALU.mult, ALU.mult, [tr2])
                setlast(f"ss{c_}", tad)
            tst = sch.dma(dq(), dst[tok0 + sub * 128: tok0 + (sub + 1) * 128, :], xo[xs], f"xo{xs}", [tad])
            setlast(f"xo{xs}", tst)
            final_toks.append(tst)
        setlast("yT", tmm)
        return tmm

    def conv_stage(layer, cj, src, dst, halo_ap, use_mask):
        load_g(layer)
        tcw = sch.dma("sync", cw_sb, conv_w[cj].rearrange("(j p) k -> p j k", p=128), "cw", dep("cw_sb"))
        setlast("cw_sb", tcw)
        wt = load_w(w_in_pieces(layer, 0, 4096) + w_out_pieces(layer))
        setlast("w_in_sb", *wt)
        setlast("w_out_sb", *wt)
        tz = sch.op("gpsimd", lambda e: e.memset(xh, 0.0), dep("xh"))
        setlast("xh", tz)
        th = norm_tile(halo_ap, 0, 4, 0, xh, "xh", nrows=2, mask=hmask_sb if use_mask else None)
        tm = transpose_hb(th, 0)
        tcp = ACT(hT_halo, pT, AF.Copy, [tm] + dep("hT_halo"))
        setlast("pT", tcp)
        for j in range(8):
            tk = {}
            for gi in (1, 2):
                for k in range(8):
                    tk[gi] = MM(pP[gi][:, 0:2], w_in_sb[:, k, gi * 1024 + j * 128: gi * 1024 + (j + 1) * 128],
                                hT_halo[:, k, 0:2], k == 0, k == 7,
                                ([tcp] + dep(f"pP{gi}", "w_in_sb")) if k == 0 else [], signal=(k == 7))
            tuc = ACT(ut[:, 0:2], pP[2][:, 0:2], AF.Copy, [tk[2]] + dep("ut"))
            tv = TT("vector", vbuf[j][:, 0:2], pP[1][:, 0:2], ut[:, 0:2], ALU.mult, [tuc, tk[1]] + dep(f"vbuf{j}"))
            setlast("ut", tv)
            setlast("pP1", tv)
            setlast("pP2", tuc)
            setlast(f"vbuf{j}", tv)
        setlast("hT_halo", tk[2])
        xl_next = issue_xloads(src, 0, 0)
        for T in range(NT):
            tok0 = T * 512
            hs = T % 2
            par = T % 2
            xl_cur = xl_next
            if T + 1 < NT:
                xl_next = issue_xloads(src, (T + 1) * 512, 1 - par)
            hw = norm_to_hT(src, tok0, hs, par, xl_cur)
            yw = []
            tmm = None
            for j in range(8):
                tg_ = {}
                for g_ in (1, 2, 3, 0):
                    for k in range(8):
                        tmm = MM(pP[g_], w_in_sb[:, k, g_ * 1024 + j * 128: g_ * 1024 + (j + 1) * 128],
                                 hT[hs][:, k, :], k == 0, k == 7,
                                 (hw + dep(f"pP{g_}", "w_in_sb")) if k == 0 else [], signal=(k == 7))
                    tg_[g_] = tmm
                tb, tc1, tu1, tz1 = tg_[0], tg_[1], tg_[2], tg_[3]
                tuc = ACT(ut, pP[2], AF.Copy, [tu1] + dep("ut"))
                setlast("pP2", tuc)
                tsz = ACT(sz, pP[3], AF.Silu, [tz1] + dep("sz"))
                setlast("pP3", tsz)
                tv = TT("vector", vbuf[j][:, 2:514], pP[1], ut, ALU.mult, [tuc, tc1] + dep(f"vbuf{j}"))
                setlast("pP1", tv)
                setlast("ut", tv)
                ta = ACT(t1, vbuf[j][:, 0:512], AF.Copy, [tv] + dep("t1", "cw_sb"), scale=cw_sb[:, j, 0:1])
                tb_ = STT("vector", t1, vbuf[j][:, 1:513], cw_sb[:, j, 1:2], t1, ALU.mult, ALU.add, [ta])
                tc_ = STT("vector", t1, vbuf[j][:, 2:514], cw_sb[:, j, 2:3], t1, ALU.mult, ALU.add, [tb_])
                tcar = CP("vector", vbuf[j][:, 0:2], vbuf[j][:, 512:514], [tc_])
                setlast(f"vbuf{j}", tcar)
                td_ = TT("vector", t2, pP[0], t1, ALU.mult, [tc_, tb] + dep("t2"))
                setlast("pP0", td_)
                setlast("t1", td_)
                te_ = TT("vector", yT[:, j, :], t2, sz, ALU.mult, [td_, tsz] + dep("yT"))
                setlast("sz", te_)
                setlast("t2", te_)
                yw.append(te_)
            setlast(f"hT{hs}", tmm)
            outproj_residual(src, dst, tok0, yw, load_x=False, final=False, par=par)

    def pz_stage(layer, src):
        load_g(layer)
        wt = load_w(w_in_pieces(layer, 3072, 4096))
        setlast("w_in_sb", *wt)
        ring = 0
        xl_next = issue_xloads(src, 0, 0)
        for T in range(NT):
            tok0 = T * 512
            hs = T % 2
            par = T % 2
            xl_cur = xl_next
            if T + 1 < NT:
                xl_next = issue_xloads(src, (T + 1) * 512, 1 - par)
            hw = norm_to_hT(src, tok0, hs, par, xl_cur)
            tsh = sch.dma(dq(), HTown3[:, :, tok0:tok0 + 512].rearrange("k p t -> p k t"), hT[hs], f"hst{hs}", hw)
            final_toks.append(tsh)
            tmm = None
            for j in range(8):
                b_ = ring % 4
                ring += 1
                for k in range(8):
                    tmm = MM(pP[b_], w_in_sb[:, k, 3072 + j * 128: 3072 + (j + 1) * 128], hT[hs][:, k, :], k == 0, k == 7,
                             (hw + dep(f"pP{b_}", "w_in_sb")) if k == 0 else [], signal=(k == 7))
                tev = ACT(ev[b_], pP[b_], AF.Silu, [tmm] + dep(f"ev{b_}"))
                setlast(f"pP{b_}", tev)
                tst = sch.dma(dq(), ZTd[j, :, tok0:tok0 + 512], ev[b_], f"ev{b_}", [tev])
                setlast(f"ev{b_}", tst)
                final_toks.append(tst)
            setlast(f"hT{hs}", tmm, tsh)

    def hproj_stage(j_attn, preloaded=False):
        if not preloaded:
            wt = load_w(w_qkv_pieces(j_attn))
            setlast("w_in_sb", *wt)
        ring = 0
        for T in range(16):
            rk, lt = T // 8, T % 8
            hs = T % 2
            g0 = T * 512
            hw = []
            for pc in range(4):
                tl = sch.dma(dq(), hT[hs][:, 2 * pc:2 * pc + 2, :],
                             HTall5[pc, rk, :, :, lt * 512:(lt + 1) * 512].rearrange("kk p t -> p kk t"),
                             f"hld{hs}", dep(f"hT{hs}"))
                hw.append(tl)
            tmm = None
            for jj in range(8):
                b_ = ring % 4
                ring += 1
                for k in range(8):
                    tmm = MM(pP[b_], w_in_sb[:, k, jj * 128:(jj + 1) * 128], hT[hs][:, k, :], k == 0, k == 7,
                             (hw + dep(f"pP{b_}", "w_in_sb")) if k == 0 else [], signal=(k == 7))
                tev = CP("scalar" if jj % 2 == 0 else "vector", ev[b_], pP[b_], [tmm] + dep(f"ev{b_}"))
                setlast(f"pP{b_}", tev)
                dstT = QTa[jj, :, g0:g0 + 512] if jj < 4 else KTa[jj - 4, :, g0:g0 + 512]
                tst = sch.dma(dq(), dstT, ev[b_], f"ev{b_}", [tev])
                setlast(f"ev{b_}", tst)
                final_toks.append(tst)
            for sub in range(4):
                b_ = ring % 4
                ring += 1
                for k in range(8):
                    tmm = MM(pP[b_], hT[hs][:, k, sub * 128:(sub + 1) * 128], w_in_sb[:, k, 1024:1536], k == 0, k == 7,
                             (hw + dep(f"pP{b_}", "w_in_sb")) if k == 0 else [], signal=(k == 7))
                tev = CP("scalar" if sub % 2 == 0 else "vector", ev[b_], pP[b_], [tmm] + dep(f"ev{b_}"))
                setlast(f"pP{b_}", tev)
                tst = sch.dma(dq(), Va[:, g0 + sub * 128:g0 + (sub + 1) * 128, :].rearrange("h t e -> t h e"),
                              ev[b_][:, :].rearrange("p (h e) -> p h e", h=4), f"ev{b_}", [tev])
                setlast(f"ev{b_}", tst)
                final_toks.append(tst)
            setlast(f"hT{hs}", tmm)

    def aout_stage(layer, src, dst, final, preloaded=False):
        if not preloaded:
            wt = load_w(w_out_pieces(layer))
            setlast("w_out_sb", *wt)
        xl_next = issue_xloads(src, 0, 0)
        for T in range(NT):
            tok0 = T * 512
            par = T % 2
            xl_cur = xl_next
            if T + 1 < NT:
                xl_next = issue_xloads(src, (T + 1) * 512, 1 - par)
            tl1 = sch.dma("sync", onl, ONTr3[:, :, tok0:tok0 + 512].rearrange("j p t -> p j t"), "onl", dep("onl"))
            tl2 = sch.dma("sync", ztl, ZTd[:, :, tok0:tok0 + 512].rearrange("j p t -> p j t"), "ztl", dep("ztl"))
            ty = TT("vector", yT, onl, ztl, ALU.mult, [tl1, tl2] + dep("yT"))
            setlast("onl", ty)
            setlast("ztl", ty)
            outproj_residual(src, dst, tok0, [ty], load_x=True, final=final, par=par, xl=xl_cur)

    def attn_stage(j_attn, lam_init):
        t_lq = sch.dma("sync", lq_sb, lqk[j_attn], "lq")
        t_gs = sch.dma("sync", gs_sb, gsub_in[j_attn], "gs")
        t_gs2 = TS("vector", gs_sb, gs_sb, 1.0 - lam_init, None, ALU.mult, None, [t_gs])
        t_p = TT("vector", lprod, lq_sb[:, 0:4:2, :], lq_sb[:, 1:4:2, :], ALU.mult, [t_lq])
        t_s1 = ACT(junkA[:, 0:64], lprod[:, 0, :], AF.Copy, [t_p] + dep("junkA"), accum_out=lsum[:, 0:1])
        t_s2 = ACT(junkA[:, 0:64], lprod[:, 1, :], AF.Copy, [t_s1], accum_out=lsum[:, 1:2])
        t_e = ACT(lsum[:, 2:4], lsum[:, 0:2], AF.Exp, [t_s2])
        t_l1 = TT("vector", neglam, lsum[:, 3:4], lsum[:, 2:3], ALU.subtract, [t_e])
        t_l2 = TS("vector", neglam, neglam, -lam_init, None, ALU.add, None, [t_l1])
        setlast("junkA", t_s2)
        for i in range(2):
            tv1 = sch.op("gpsimd", lambda e, i=i: e.memset(VS[i][:, :, 128:130], 1.0))
            setlast(f"VS{i}", tv1)

        def load_head(h):
            s = h % 2
            toks = []
            toks.append(sch.dma("sync", QT[s][:, 0:S // 2], QTa[h, :, 0:S // 2], f"hd{s}", dep(f"QT{s}")))
            toks.append(sch.dma("sync", QT[s][:, S // 2:S], QTa[h, :, S // 2:S], f"hd{s}", dep(f"QT{s}")))
            toks.append(sch.dma("sync", KT[s][:, 0:S // 2], KTa[h, :, 0:S // 2], f"hd{s}", dep(f"KT{s}")))
            toks.append(sch.dma("sync", KT[s][:, S // 2:S], KTa[h, :, S // 2:S], f"hd{s}", dep(f"KT{s}")))
            vv = Va[h].rearrange("(c p) e -> p c e", p=128)
            for qd in range(4):
                toks.append(sch.dma("sync", VS[s][:, qd * 16:(qd + 1) * 16, 0:128],
                                    vv[:, qd * 16:(qd + 1) * 16, :], f"hd{s}", dep(f"VS{s}")))
            tz = sch.dma("sync", ZC[s], Zb[2 * h:2 * h + 2].rearrange("m p w -> p m w"), f"hz{s}", dep(f"ZC{s}"))
            tzc = None
            for m in range(2):
                tzc = TS("vector", ZC[s][:, m, :], ZC[s][:, m, :], cf_sb[:, 2 * h + m:2 * h + m + 1], None, ALU.subtract,
                         None, [tz] + CONST)
            toks.append(tzc)
            return toks

        sring = [0]
        pring = [0]
        bring = [0]
        pending_epi = [None]

        def emit_epilogue_pe(info):
            h, j, es, ton = info
            sl = sring[0] % 2
            sring[0] += 1
            w0 = dep(f"SP{sl}") + CONST
            tms = []
            for s4 in range(4):
                tm = MM(SP[sl][:, 0, s4 * 128:(s4 + 1) * 128], onb[s4], ident, True, True,
                        (w0 + [ton[s4]]) if s4 == 0 else [ton[s4]], signal=True)
                tms.append(tm)
                setlast(f"onb{s4}", tm)
            os_ = j % 2
            tca = TS("vector", onTa[os_], SP[sl][:, 0, :], sel_sb[:, 0:1], None, ALU.mult, None,
                     [tms[-1]] + dep(f"onTa{os_}"))
            tcb = TS("vector", onTb[os_], SP[sl][:, 0, :], sel_sb[:, 1:2], None, ALU.mult, None,
                     [tca] + dep(f"onTb{os_}"))
            setlast(f"SP{sl}", tca, tcb)
            d_, c0 = j // 8, (j % 8) * 512
            tsa = sch.dma("sync", RS5[h // 2, d_, h % 2, :, c0:c0 + 512], onTa[os_], f"onta{os_}", [tca])
            tsb = sch.dma("sync", RS5[(4 + h) // 2, d_, h % 2, :, c0:c0 + 512], onTb[os_], f"ontb{os_}", [tcb])
            setlast(f"onTa{os_}", tsa)
            setlast(f"onTb{os_}", tsb)
            final_toks.append(tsa)
            final_toks.append(tsb)

        head_toks = {0: load_head(0)}
        unit_idx = 0
        for h in range(4):
            s = h % 2
            if h + 1 < 4:
                head_toks[h + 1] = load_head(h + 1)
            hw = head_toks[h]
            last_pe_read = None
            for j in range(16):
                nchunks = 4 * j + 4
                q0 = j * 512

                def issue_qk(c):
                    i = c - 4 * j
                    lo = 128 * i if i > 0 else 0
                    sl = sring[0] % 2
                    sring[0] += 1
                    w = dep(f"SP{sl}") + (hw if c == 0 else [])
                    tq = None
                    for m in range(2):
                        tq = MM(SP[sl][:, m, lo:512], KT[s][m * 64:(m + 1) * 64, c * 128:(c + 1) * 128],
                                QT[s][m * 64:(m + 1) * 64, q0 + lo:q0 + 512], True, True, w if m == 0 else [],
                                signal=(m == 1))
                    pl = pring[0] % NP
                    pring[0] += 1
                    if i >= -1:
                        o_idx = 3 - i
                        bl = bring[0] % 2
                        bring[0] += 1
                        tb = STT("vector", SSB[bl][:, :, lo:512], SP[sl][:, :, lo:512], 0.125,
                                 ZC[s][:, :, o_idx * 128 + lo:o_idx * 128 + 512], ALU.mult, ALU.add,
                                 [tq] + dep(f"SSB{bl}") + hw)
                        setlast(f"SP{sl}", tb)
                        te = ACT(PT[pl][:, :, lo:512], SSB[bl][:, :, lo:512], AF.Exp, [tb] + dep(f"PT{pl}"))
                        setlast(f"SSB{bl}", te)
                    else:
                        te = ACT(PT[pl][:, :, lo:512], SP[sl][:, :, lo:512], AF.Exp, [tq] + dep(f"PT{pl}"), scale=0.125)
                        setlast(f"SP{sl}", te)
                    return (c, i, pl, te)

                def issue_av(tile):
                    c, i, pl, te = tile
                    tm = None
                    first = True
                    for s4 in range(4):
                        if s4 < i:
                            continue
                        lastc = 4 * j + s4
                        for m in range(2):
                            a = s4 * 2 + m
                            w = []
                            if first:
                                w = [te] + (dep("OA") if c == 0 else [])
                                first = False
                            tm = MM(OA[:, a, 0:129], PT[pl][:, m, s4 * 128:(s4 + 1) * 128], VS[s][:, c, 0:129],
                                    (c == 0 and m == 0), (c == lastc), w, signal=(s4 == 3 and m == 1))
                    setlast(f"PT{pl}", tm)
                    return tm

                prev = None
                tav = None
                for c in range(nchunks):
                    t_ = issue_qk(c)
                    if prev is not None:
                        tav = issue_av(prev)
                    prev = t_
                    if c == min(6, nchunks - 1) and pending_epi[0] is not None:
                        emit_epilogue_pe(pending_epi[0])
                        pending_epi[0] = None
                tav = issue_av(prev)
                last_pe_read = tav
                es = unit_idx % 2
                unit_idx += 1
                tev = CP("vector", OEV[es], OA[:, :, 0:129], [tav] + dep(f"OEV{es}"))
                setlast("OA", tev)
                trr = sch.op("vector", lambda e, es=es: e.reciprocal(out=rr[:, :, :].rearrange("p s m -> p (s m)"),
                                                                     in_=OEV[es][:, :, 128:129].rearrange("p a o -> p (a o)")),
                             [tev] + dep("rr"))
                tr2 = TS("vector", r2, rr[:, :, 1], neglam[:, 0:1], None, ALU.mult, None, [trr, t_l2] + dep("r2"))
                ton = []
                tlast = None
                tos = []
                tqs = []
                for s4 in range(4):
                    ta = TS("gpsimd", tA[s4], OEV[es][:, 2 * s4 + 1, 0:128], r2[:, s4:s4 + 1], None, ALU.mult, None,
                            [tr2] + dep(f"tA{s4}"))
                    to = STT("vector", oo[s4], OEV[es][:, 2 * s4, 0:128], rr[:, s4, 0:1], tA[s4], ALU.mult, ALU.add,
                             [ta, trr] + dep(f"oo{s4}"))
                    setlast(f"tA{s4}", to)
                    tq_ = ACT(junkA, oo[s4], AF.Square, [to] + CONST + dep("q1", "junkA"), accum_out=q1[:, s4:s4 + 1])
                    setlast("junkA", tq_)
                    tos.append(to)
                    tqs.append(tq_)
                tl_ = ACT(q1[:, 0:4], q1[:, 0:4], AF.Ln, tqs, scale=1.0 / 128, bias=eps_sb[:, 0:1])
                tx_ = ACT(q1[:, 0:4], q1[:, 0:4], AF.Exp, [tl_], scale=-0.5)
                for s4 in range(4):
                    tn_ = STT("vector", onb[s4], oo[s4], q1[:, s4:s4 + 1], gs_sb, ALU.mult, ALU.mult,
                              [tx_, t_gs2] + dep(f"onb{s4}"))
                    setlast(f"oo{s4}", tn_)
                    ton.append(tn_)
                    tlast = tn_
                setlast("q1", tlast)
                setlast("rr", tlast)
                setlast("r2", tlast)
                setlast(f"OEV{es}", tlast)
                if pending_epi[0] is not None:
                    emit_epilogue_pe(pending_epi[0])
                pending_epi[0] = (h, j, es, ton)
            for nm in (f"QT{s}", f"KT{s}", f"VS{s}"):
                addlast(nm, last_pe_read)
            setlast(f"ZC{s}", ("e_vector", sch.cnt.get("e_vector", 0)))
        if pending_epi[0] is not None:
            emit_epilogue_pe(pending_epi[0])

    def collective(kind, op, src2d, dst2d, name):
        return sch.coll(lambda e, k=kind, o=op, a=src2d, b=dst2d: e.collective_compute(
            k, o, replica_groups=GROUPS, ins=[a], outs=[b]), name)

    xs_ = [x0, X1, X2, X3]
    phase_no = [0]

    def want():
        phase_no[0] += 1
        return phase_no[0] <= upto

    for blk in range(2):
        l_conv, l_attn = 2 * blk, 2 * blk + 1
        src = xs_[l_conv]
        mid = xs_[l_conv + 1]
        dst = xs_[l_conv + 2] if blk == 0 else out_ap
        if want():
            if blk == 0:
                conv_stage(l_conv, blk, src, mid, xhalo, use_mask=False)
            else:
                conv_stage(l_conv, blk, src, mid, TAILall[0:2, :], use_mask=True)
            phase_barrier()
        if want():
            pz_stage(l_attn, mid)
            phase_barrier()
        if want():
            for pc in range(4):
                collective("AllGather", ALU.bypass, HTown[pc * 256:(pc + 1) * 256, :], HTall[pc * 512:(pc + 1) * 512, :], f"ag{pc}")
            load_w(w_qkv_pieces(blk))
            phase_barrier()
        if want():
            hproj_stage(blk, preloaded=True)
            phase_barrier()
        if want():
            attn_stage(blk, lam_inits[blk])
            phase_barrier()
        if want():
            for pc in range(4):
                collective("ReduceScatter", ALU.add, RSbuf[pc * 512:(pc + 1) * 512, :], ONTr[pc * 256:(pc + 1) * 256, :], f"rs{pc}")
            load_w(w_out_pieces(l_attn))
            phase_barrier()
        if want():
            aout_stage(l_attn, mid, dst, final=(blk == 1), preloaded=True)
            phase_barrier()
        if blk == 0 and want():
            ttail = sch.dma("sync", TAILown[:, :], X2[TOK - 2:TOK, :], "tail")
            final_toks.append(ttail)
            phase_barrier()
            collective("AllGather", ALU.bypass, TAILown, TAILall, "agt")
            phase_barrier()
    if debug_out:
        dbg = nc.dram_tensor("dbg", [TOK, D], F32, kind="ExternalOutput").ap()
        for q_ in range(4):
            td_ = sch.dma("sync", dbg[q_ * 1024:(q_ + 1) * 1024, :], debug_out[q_ * 1024:(q_ + 1) * 1024, :], "dbg")
            final_toks.append(td_)

    mx = {}
    for (k, v) in final_toks:
        mx[k] = max(mx.get(k, 0), v)
    run_sched(nc, sch, list(mx.items()))
    est.close()
    return nc


BF = ml_dtypes.bfloat16
NUM_BUCKETS, MAX_EXACT, REL_MAX = 32, 16, 128
MASKV = -30000.0

def bucket_table(n):
    d = np.arange(n, dtype=np.int64)
    ds = np.maximum(d, 1).astype(np.float32)
    large = MAX_EXACT + (np.log(ds / np.float32(MAX_EXACT)) / np.float32(math.log(REL_MAX / MAX_EXACT))
                         * np.float32(NUM_BUCKETS - MAX_EXACT)).astype(np.int32)
    large = np.minimum(large, NUM_BUCKETS - 1)
    return np.where(d < MAX_EXACT, d, large).astype(np.int64)

def common_inputs(inp):
    rep = lambda a: np.ascontiguousarray(np.broadcast_to(a[..., None, :], a.shape[:-1] + (128, a.shape[-1]))).astype(np.float32)
    return {
        "w_in": np.ascontiguousarray(inp["w_in"], dtype=np.float32),
        "w_out": np.ascontiguousarray(inp["w_out"], dtype=np.float32),
        "norm_g_rep": rep(np.asarray(inp["norm_g"])),
        "final_g_rep": rep(np.asarray(inp["final_g"])),
        "conv_w": np.ascontiguousarray(inp["conv_w"], dtype=np.float32),
        "ident": np.eye(128, dtype=np.float32),
    }

def attn_consts(inp, layer_j, core_r):
    rb = np.asarray(inp["rel_bias"], dtype=np.float32)
    bk = bucket_table(1024)
    maps = np.arange(8) + 8 * core_r
    p = np.arange(128)[:, None]
    w = np.arange(1024)[None, :]
    d = w - 384 - p
    dd = np.clip(d, 0, 1023)
    Zb = np.empty((8, 128, 1024), np.float32)
    for mi, m in enumerate(maps):
        vals = rb[bk[dd], m]
        Zb[mi] = np.where(d >= 0, vals, np.float32(MASKV))
    cf = np.ascontiguousarray(np.broadcast_to(rb[31, maps][None, :], (128, 8))).astype(np.float32)
    lqk = np.stack([np.asarray(inp[k])[layer_j] for k in ("lambda_q1", "lambda_k1", "lambda_q2", "lambda_k2")], 0)
    lqk = np.ascontiguousarray(np.broadcast_to(lqk[None], (128, 4, 64))).astype(np.float32)
    gs = np.ascontiguousarray(np.broadcast_to(np.asarray(inp["subln_g"])[layer_j][None, :], (128, 128))).astype(np.float32)
    return {"Zb": Zb, "cfar": cf, "lqk": lqk, "gsub": gs, "ident": np.eye(128, dtype=np.float32)}


def fused_inputs(inp):
    com = common_inputs(inp)
    x = inp["x"]
    w_in = inp["w_in"]
    lqk = np.stack([np.stack([np.asarray(inp[k])[j] for k in ("lambda_q1", "lambda_k1", "lambda_q2", "lambda_k2")], 0)
                    for j in range(2)], 0)
    lqk = np.ascontiguousarray(np.broadcast_to(lqk[:, None], (2, 128, 4, 64))).astype(np.float32)
    gs = np.ascontiguousarray(np.broadcast_to(np.asarray(inp["subln_g"])[:, None, :], (2, 128, 128))).astype(np.float32)
    maps = []
    for c in range(8):
        b, r = c // 2, c % 2
        m = dict(com)
        m["x0"] = np.ascontiguousarray(x[b, r * TOK:(r + 1) * TOK])
        m["xhalo"] = np.zeros((2, D), np.float32) if r == 0 else np.ascontiguousarray(x[b, TOK - 2:TOK])
        wq = []
        for j in range(2):
            l = 2 * j + 1
            cols = [w_in[l][:, base + r * 512: base + (r + 1) * 512] for base in (0, 1024, 2048)]
            wq.append(np.concatenate(cols, axis=1))
        m["w_qkv"] = np.ascontiguousarray(np.stack(wq, 0)).astype(np.float32)
        ac = attn_consts(inp, 0, r)
        m["Zb"] = ac["Zb"]
        m["cfar"] = ac["cfar"]
        m["lqk"] = lqk
        m["gsub"] = gs
        sel = np.zeros((128, 2), np.float32)
        sel[:, r] = 1.0
        m["sel"] = sel
        m["hmask"] = np.full((128, 1), float(r), np.float32)
        maps.append(m)
    return maps


_NC = {}


def _lam_init(layer_idx):
    return 0.8 - 0.6 * math.exp(-0.3 * layer_idx)


def kernel(x, norm_g, w_in, w_out, conv_w, lambda_q1, lambda_k1, lambda_q2, lambda_k2,
           subln_g, rel_bias, final_g):
    inp = {"x": np.asarray(x, np.float32), "norm_g": np.asarray(norm_g, np.float32),
           "w_in": np.asarray(w_in, np.float32), "w_out": np.asarray(w_out, np.float32),
           "conv_w": np.asarray(conv_w, np.float32),
           "lambda_q1": np.asarray(lambda_q1, np.float32), "lambda_k1": np.asarray(lambda_k1, np.float32),
           "lambda_q2": np.asarray(lambda_q2, np.float32), "lambda_k2": np.asarray(lambda_k2, np.float32),
           "subln_g": np.asarray(subln_g, np.float32), "rel_bias": np.asarray(rel_bias, np.float32),
           "final_g": np.asarray(final_g, np.float32)}
    if "nc" not in _NC:
        _NC["nc"] = build_fused((_lam_init(1), _lam_init(3)))
    maps = fused_inputs(inp)
    res = run_bass_kernel_spmd(_NC["nc"], maps, core_ids=list(range(8)))
    out = np.empty((4, S, D), np.float32)
    for c in range(8):
        out[c // 2, (c % 2) * TOK:(c % 2 + 1) * TOK] = res.results[c]["out"]
    return out
```

```python
import math
import numpy as np
import ml_dtypes
import concourse.bass as bass
import concourse.mybir as mybir
from concourse.bass_utils import run_bass_kernel_spmd

F32 = mybir.dt.float32
BF16 = mybir.dt.bfloat16
AF = mybir.ActivationFunctionType
ALU = mybir.AluOpType

D = 1024
S = 8192
TOK = 4096
NT = TOK // 512
EPS = 1e-6
MASKV = -30000.0
ENGS = ("tensor", "vector", "scalar", "gpsimd", "sync")


class Sched:
    def __init__(self):
        self.ops = {e: [] for e in ENGS}
        self.cnt = {}
        self.sems = {}
        self.bar = []

    def op(self, eng, fn, waits=(), signal=True):
        key = "e_" + eng
        tok = None
        if signal:
            self.cnt[key] = self.cnt.get(key, 0) + 1
            tok = (key, self.cnt[key])
        self.ops[eng].append((fn, self._dd(tuple(waits) + tuple(self.bar)), key if signal else None, 1))
        return tok

    @staticmethod
    def _dd(waits):
        mx = {}
        for w in waits:
            if w is None:
                continue
            k, v = w
            if mx.get(k, 0) < v:
                mx[k] = v
        return tuple(mx.items())

    def dma(self, eng, out, in_, stream, waits=()):
        key = "d_" + stream
        self.cnt[key] = self.cnt.get(key, 0) + 16
        tok = (key, self.cnt[key])
        self.ops[eng].append((lambda e, o=out, i=in_: e.dma_start(out=o, in_=i),
                              self._dd(tuple(waits) + tuple(self.bar)), key, 16))
        return tok

    def coll(self, fn, name, waits=()):
        key = "c_" + name
        self.cnt[key] = self.cnt.get(key, 0) + 1
        tok = (key, self.cnt[key])
        self.ops["gpsimd"].append((fn, self._dd(tuple(waits) + tuple(self.bar)), key, 1))
        return tok

    def barrier(self):
        self.bar = [(k, v) for k, v in self.cnt.items()]

    def sem_keys(self):
        return sorted(self.cnt.keys())

    def replay(self, eng_name, eng):
        waited = {}
        for fn, waits, key, inc in self.ops[eng_name]:
            for (k, v) in waits:
                if waited.get(k, 0) < v:
                    eng.wait_ge(self.sems[k], v)
                    waited[k] = v
            ins = fn(eng)
            if key is not None:
                ins.then_inc(self.sems[key], inc)

    def final_waits(self, eng_name, eng, toks):
        for (k, v) in toks:
            eng.wait_ge(self.sems[k], v)


def run_sched(nc, sch, final_toks):
    from contextlib import ExitStack
    with ExitStack() as st:
        for k in sch.sem_keys():
            sch.sems[k] = st.enter_context(nc.semaphore(k))
        block = st.enter_context(nc.Block())

        @block.sync
        def _(e):
            sch.replay("sync", e)
            sch.final_waits("sync", e, final_toks)

        @block.tensor
        def _(e):
            sch.replay("tensor", e)

        @block.vector
        def _(e):
            sch.replay("vector", e)

        @block.scalar
        def _(e):
            sch.replay("scalar", e)

        @block.gpsimd
        def _(e):
            sch.replay("gpsimd", e)


def build_fused(lam_inits=(0.0, 0.0), debug_out=False, upto=99):
    from contextlib import ExitStack
    nc = bass.Bass("TRN2", target_bir_lowering=False)

    def ext_in(name, shape, dt=F32):
        return nc.dram_tensor(name, list(shape), dt, kind="ExternalInput").ap()

    def internal(name, shape, dt):
        return nc.dram_tensor(name, list(shape), dt).ap()

    x0 = ext_in("x0", [TOK, D])
    xhalo = ext_in("xhalo", [2, D])
    w_in = ext_in("w_in", [4, D, 4 * D])
    w_out = ext_in("w_out", [4, D, D])
    w_qkv = ext_in("w_qkv", [2, D, 1536])
    g_rep = ext_in("norm_g_rep", [4, 128, D])
    fg_rep = ext_in("final_g_rep", [128, D])
    conv_w = ext_in("conv_w", [2, D, 3])
    ident_in = ext_in("ident", [128, 128])
    Zb = ext_in("Zb", [8, 128, 1024])
    cfar = ext_in("cfar", [128, 8])
    lqk = ext_in("lqk", [2, 128, 4, 64])
    gsub_in = ext_in("gsub", [2, 128, 128])
    sel_in = ext_in("sel", [128, 2])
    hmask_in = ext_in("hmask", [128, 1])
    out_ap = nc.dram_tensor("out", [TOK, D], F32, kind="ExternalOutput").ap()

    X1 = internal("X1", [TOK, D], F32)
    X2 = internal("X2", [TOK, D], F32)
    X3 = internal("X3", [TOK, D], F32)
    HTown = internal("HTown", [8 * 128, TOK], BF16)
    HTall = internal("HTall", [2 * 8 * 128, TOK], BF16)
    ZTd = internal("ZTd", [8, 128, TOK], BF16)
    QTa = internal("QTa", [4, 128, S], BF16)
    KTa = internal("KTa", [4, 128, S], BF16)
    Va = internal("Va", [4, S, 128], BF16)
    RSbuf = internal("RSbuf", [2 * 8 * 128, TOK], BF16)
    ONTr = internal("ONTr", [8 * 128, TOK], BF16)
    TAILown = internal("TAILown", [2, D], F32)
    TAILall = internal("TAILall", [4, D], F32)
    HTown3 = HTown.rearrange("(k p) t -> k p t", p=128)
    HTall5 = HTall.rearrange("(pc r kk p) t -> pc r kk p t", pc=4, r=2, kk=2, p=128)
    RS5 = RSbuf.rearrange("(pc d hh p) t -> pc d hh p t", pc=2, d=2, hh=4, p=128)
    ONTr3 = ONTr.rearrange("(h p) t -> h p t", p=128)
    GROUPS = [[0, 1], [2, 3], [4, 5], [6, 7]]

    sch = Sched()
    est = ExitStack()
    ARENA_BYTES = 206848
    arena = est.enter_context(nc.sbuf_tensor("arena", [128, ARENA_BYTES // 2], BF16))
    psA = est.enter_context(nc.psum_tensor("psA", [128, 2048], F32))
    psB = est.enter_context(nc.psum_tensor("psB", [128, 2048], F32))

    def view(off, shape, dt):
        n = 1
        for s_ in shape[1:]:
            n *= s_
        nb = n * (4 if dt == F32 else 2)
        assert off % 4 == 0 and off + nb <= ARENA_BYTES, (off, nb)
        a = arena[:, off // 2:(off + nb) // 2]
        if dt == F32:
            a = a.bitcast(F32)
        if len(shape) == 3:
            a = a.rearrange("p (a b) -> p a b", a=shape[1])
        return a

    class Alloc:
        def __init__(self, base):
            self.off = base

        def __call__(self, shape, dt):
            n = 1
            for s_ in shape[1:]:
                n *= s_
            nb = n * (4 if dt == F32 else 2)
            nb_al = (nb + 31) // 32 * 32
            v = view(self.off, shape, dt)
            self.off += nb_al
            return v

    ta_ = Alloc(0)
    wstage = [ta_([128, 2048], F32) for _ in range(2)]
    w_in_sb = ta_([128, 8, 4096], BF16)
    w_out_sb = ta_([128, 8, 1024], BF16)
    xtA = [ta_([128, D], F32) for _ in range(4)]
    xtB = [wstage[0][:, 0:1024], wstage[0][:, 1024:2048], wstage[1][:, 0:1024], wstage[1][:, 1024:2048]]
    xt2 = [xtA, xtB]
    xt = xtA
    junk = ta_([128, D], BF16)
    hb = [ta_([128, D], BF16) for _ in range(2)]
    hT = [ta_([128, 8, 512], BF16) for _ in range(2)]
    hT_halo = ta_([128, 8, 128], BF16)
    xh = ta_([128, D], F32)
    ut = ta_([128, 512], F32)
    sz = ta_([128, 512], F32)
    vbuf = [ta_([128, 514], F32) for _ in range(8)]
    t1 = ta_([128, 512], F32)
    t2 = ta_([128, 512], F32)
    yT = ta_([128, 8, 512], BF16)
    ev = [ta_([128, 512], BF16) for _ in range(4)]
    xo = [ta_([128, D], F32) for _ in range(2)]
    tok_end = ta_.off
    onl, ztl = hT[0], hT[1]
    aa_ = Alloc(0)
    QT = [aa_([128, S], BF16) for _ in range(2)]
    KT = [aa_([128, S], BF16) for _ in range(2)]
    VS = [aa_([128, 64, 130], BF16) for _ in range(2)]
    ZC = [aa_([128, 2, 1024], F32) for _ in range(2)]
    NP = 6
    PT = [aa_([128, 2, 512], BF16) for _ in range(NP)]
    SSB = [aa_([128, 2, 512], F32) for _ in range(2)]
    OEV = [aa_([128, 8, 129], F32) for _ in range(2)]
    tA = [aa_([128, 128], F32) for _ in range(4)]
    oo = [aa_([128, 128], F32) for _ in range(4)]
    junkA = aa_([128, 128], F32)
    onb = [aa_([128, 128], BF16) for _ in range(4)]
    onTa = [aa_([128, 512], BF16) for _ in range(2)]
    onTb = [aa_([128, 512], BF16) for _ in range(2)]
    att_end = aa_.off
    pa_ = Alloc(max(tok_end, att_end))
    g_sb = pa_([128, D], F32)
    fg_sb = pa_([128, D], F32)
    ident_f = pa_([128, 128], F32)
    ident = pa_([128, 128], BF16)
    cw_sb = pa_([128, 8, 3], F32)
    ss = pa_([128, 8], F32)
    eps_sb = pa_([128, 1], F32)
    cf_sb = pa_([128, 8], F32)
    lq_sb = pa_([128, 4, 64], F32)
    lprod = pa_([128, 2, 64], F32)
    lsum = pa_([128, 4], F32)
    neglam = pa_([128, 1], F32)
    gs_sb = pa_([128, 128], F32)
    rr = pa_([128, 4, 2], F32)
    r2 = pa_([128, 4], F32)
    q1 = pa_([128, 4], F32)
    sel_sb = pa_([128, 2], F32)
    hmask_sb = pa_([128, 1], F32)
    assert pa_.off <= ARENA_BYTES, pa_.off
    pT = psA[:, 0:1024].rearrange("p (a b) -> p a b", a=8)
    pO = psA[:, 1024:2048]
    pP = [psB[:, i * 512:(i + 1) * 512] for i in range(4)]
    SP = [psA[:, 0:1024].rearrange("p (a b) -> p a b", a=2), psA[:, 1024:2048].rearrange("p (a b) -> p a b", a=2)]
    OA = psB[:, :].rearrange("p (a b) -> p a b", a=8)

    last = {}

    def dep(*names):
        out = []
        for n in names:
            out.extend(last.get(n, []))
        return out

    def setlast(name, *toks):
        last[name] = [t for t in toks if t is not None]

    def addlast(name, *toks):
        last.setdefault(name, []).extend([t for t in toks if t is not None])

    def phase_barrier():
        sch.barrier()
        last.clear()

    def MM(out, lhsT, rhs, start, stop, waits=(), signal=False):
        return sch.op("tensor", lambda e, o=out, l=lhsT, r=rhs, a=start, b=stop:
                      e.matmul(o, l, r, start=a, stop=b, skip_group_check=True), waits, signal)

    def ACT(out, in_, func, waits=(), accum_out=None, scale=None, bias=None):
        kw = {}
        if accum_out is not None:
            kw["accum_out"] = accum_out
        if scale is not None:
            kw["scale"] = scale
        if bias is not None:
            kw["bias"] = bias
        return sch.op("scalar", lambda e, o=out, i=in_, f=func, kw=kw: e.activation(out=o, in_=i, func=f, **kw), waits)

    def TT(eng, out, in0, in1, op, waits=()):
        return sch.op(eng, lambda e, o=out, a=in0, b=in1, p=op: e.tensor_tensor(out=o, in0=a, in1=b, op=p), waits)

    def TS(eng, out, in0, s1, s2, op0, op1=None, waits=()):
        if op1 is None:
            return sch.op(eng, lambda e, o=out, a=in0, x=s1, p=op0: e.tensor_scalar(out=o, in0=a, scalar1=x, scalar2=None, op0=p), waits)
        return sch.op(eng, lambda e, o=out, a=in0, x=s1, y=s2, p=op0, q=op1: e.tensor_scalar(out=o, in0=a, scalar1=x, scalar2=y, op0=p, op1=q), waits)

    def STT(eng, out, in0, scalar, in1, op0, op1, waits=()):
        return sch.op(eng, lambda e, o=out, a=in0, s=scalar, b=in1, p=op0, q=op1:
                      e.scalar_tensor_tensor(out=o, in0=a, scalar=s, in1=b, op0=p, op1=q), waits)

    def CP(eng, out, in_, waits=()):
        if eng == "scalar":
            return ACT(out, in_, AF.Copy, waits)
        return sch.op(eng, lambda e, o=out, i=in_: e.tensor_copy(out=o, in_=i), waits)

    dma_rr = [0]

    def dq():
        return "sync"

    t_id = sch.dma("sync", ident_f, ident_in[:, :], "c0")
    t_idc = ACT(ident, ident_f, AF.Copy, [t_id])
    t_eps = sch.op("vector", lambda e: e.memset(eps_sb, EPS))
    t_cf = sch.dma("sync", cf_sb, cfar[:, :], "c1")
    t_sel = sch.dma("sync", sel_sb, sel_in[:, :], "c2")
    t_hm = sch.dma("sync", hmask_sb, hmask_in[:, :], "c3")
    t_fg = sch.dma("sync", fg_sb, fg_rep[:, :], "c4")
    CONST = [t_idc, t_eps, t_cf, t_sel, t_hm, t_fg]

    final_toks = []

    def load_w(pieces):
        toks = []
        for idx, (src_ap, vfn, dst_ap) in enumerate(pieces):
            s = idx % 2
            td = sch.dma(dq(), vfn(wstage[s]), src_ap, f"w{s}", dep(f"wstage{s}"))
            tc_ = CP("vector" if idx % 2 == 0 else "scalar", dst_ap, vfn(wstage[s]), [td])
            setlast(f"wstage{s}", tc_)
            toks.append(tc_)
        return toks

    def w_in_pieces(layer, c_lo, c_hi):
        ps_ = []
        for k in range(8):
            for c0 in range(c_lo, c_hi, 2048):
                n = min(2048, c_hi - c0)
                ps_.append((w_in[layer, k * 128:(k + 1) * 128, c0:c0 + n], (lambda ws, n=n: ws[:, 0:n]),
                            w_in_sb[:, k, c0:c0 + n]))
        return ps_

    def w_qkv_pieces(j):
        return [(w_qkv[j, k * 128:(k + 1) * 128, :], (lambda ws: ws[:, 0:1536]), w_in_sb[:, k, 0:1536]) for k in range(8)]

    def w_out_pieces(layer):
        return [(w_out[layer, k * 128:(k + 2) * 128, :].rearrange("(a p) c -> p a c", p=128),
                 (lambda ws: ws[:, :].rearrange("p (a c) -> p a c", a=2)), w_out_sb[:, k:k + 2, :]) for k in range(0, 8, 2)]

    def issue_xloads(src_ap, tok0, par):
        toks = []
        for sub in range(4):
            nm = f"xt{par}_{sub}"
            w = dep(nm) + (dep("wstage0", "wstage1") if par == 1 else [])
            toks.append(sch.dma("sync", xt2[par][sub], src_ap[tok0 + sub * 128: tok0 + (sub + 1) * 128, :], f"x_{nm}", w))
        return toks

    def norm_tile(src_ap, row0, col, hslot, xb, name, nrows=128, mask=None, tl=None):
        if tl is None:
            tl = sch.dma(dq(), xb[0:nrows, :], src_ap[row0:row0 + nrows, :], f"x_{name}", dep(name))
        if mask is not None:
            tl = TS("vector", xb[0:nrows, :], xb[0:nrows, :], mask[0:nrows, 0:1], None, ALU.mult, None, [tl] + CONST)
        tsq = ACT(junk, xb, AF.Square, [tl] + CONST + dep(f"ss{col}", "junk"), accum_out=ss[:, col:col + 1])
        setlast("junk", tsq)
        tr1 = ACT(ss[:, col:col + 1], ss[:, col:col + 1], AF.Ln, [tsq], scale=1.0 / D, bias=eps_sb[:, 0:1])
        tr2 = ACT(ss[:, col:col + 1], ss[:, col:col + 1], AF.Exp, [tr1], scale=-0.5)
        th = STT("vector", hb[hslot], xb, ss[:, col:col + 1], g_sb, ALU.mult, ALU.mult,
                 [tr2] + dep(f"hb{hslot}", "g_sb"))
        setlast(f"ss{col}", th)
        setlast(name, th)
        return th

    def transpose_hb(th, hslot):
        w = [th] + dep("pT") + CONST
        tm = None
        for k in range(8):
            tm = MM(pT[:, k, :], hb[hslot][:, k * 128:(k + 1) * 128], ident, True, True,
                    w if k == 0 else [], signal=(k == 7))
        setlast(f"hb{hslot}", tm)
        return tm

    def load_g(layer):
        tg = sch.dma("sync", g_sb, g_rep[layer, :, :], "g", dep("g_sb"))
        setlast("g_sb", tg)

    def norm_to_hT(src, tok0, hs, par=0, xl=None):
        hw = []
        for sub in range(4):
            hslot = sub % 2
            th = norm_tile(src, tok0 + sub * 128, sub, hslot, xt2[par][sub], f"xt{par}_{sub}",
                           tl=(xl[sub] if xl is not None else None))
            tm = transpose_hb(th, hslot)
            tcp = CP("scalar" if sub % 2 == 0 else "vector", hT[hs][:, :, sub * 128:(sub + 1) * 128], pT,
                     [tm] + dep(f"hT{hs}"))
            setlast("pT", tcp)
            hw.append(tcp)
        return hw

    def outproj_residual(src, dst, tok0, yw, load_x, final, par=0, xl=None):
        tmm = None
        for sub in range(4):
            xs = sub % 2
            txl = xl[sub] if xl is not None else None
            for half in range(2):
                for j in range(8):
                    tmm = MM(pO[:, half * 512:(half + 1) * 512], yT[:, j, sub * 128:(sub + 1) * 128],
                             w_out_sb[:, j, half * 512:(half + 1) * 512], j == 0, j == 7,
                             (yw + dep("pO", "w_out_sb")) if (j == 0 and half == 0) else [],
                             signal=(j == 7 and half == 1))
            tad = TT("vector", xo[xs], pO, xt2[par][sub], ALU.add, [tmm, txl] + dep(f"xo{xs}"))
            setlast("pO", tad)
            setlast(f"xt{par}_{sub}", tad)
            if final:
                c_ = 4 + xs
                tsq = ACT(junk, xo[xs], AF.Square, [tad] + CONST + dep(f"ss{c_}", "junk"), accum_out=ss[:, c_:c_ + 1])
                setlast("junk", tsq)
                tr1 = ACT(ss[:, c_:c_ + 1], ss[:, c_:c_ + 1], AF.Ln, [tsq], scale=1.0 / D, bias=eps_sb[:, 0:1])
                tr2 = ACT(ss[:, c_:c_ + 1], ss[:, c_:c_ + 1], AF.Exp, [tr1], scale=-0.5)
                tad = STT("vector", xo[xs], xo[xs], ss[:, c_:c_ + 1], fg_sb, ALU.mult, ALU.mult, [tr2])
                setlast(f"ss{c_}", tad)
            tst = sch.dma(dq(), dst[tok0 + sub * 128: tok0 + (sub + 1) * 128, :], xo[xs], f"xo{xs}", [tad])
            setlast(f"xo{xs}", tst)
            final_toks.append(tst)
        setlast("yT", tmm)
        return tmm

    def conv_stage(layer, cj, src, dst, halo_ap, use_mask):
        load_g(layer)
        tcw = sch.dma("sync", cw_sb, conv_w[cj].rearrange("(j p) k -> p j k", p=128), "cw", dep("cw_sb"))
        setlast("cw_sb", tcw)
        wt = load_w(w_in_pieces(layer, 0, 4096) + w_out_pieces(layer))
        setlast("w_in_sb", *wt)
        setlast("w_out_sb", *wt)
        tz = sch.op("gpsimd", lambda e: e.memset(xh, 0.0), dep("xh"))
        setlast("xh", tz)
        th = norm_tile(halo_ap, 0, 4, 0, xh, "xh", nrows=2, mask=hmask_sb if use_mask else None)
        tm = transpose_hb(th, 0)
        tcp = ACT(hT_halo, pT, AF.Copy, [tm] + dep("hT_halo"))
        setlast("pT", tcp)
        for j in range(8):
            tk = {}
            for gi in (1, 2):
                for k in range(8):
                    tk[gi] = MM(pP[gi][:, 0:2], w_in_sb[:, k, gi * 1024 + j * 128: gi * 1024 + (j + 1) * 128],
                                hT_halo[:, k, 0:2], k == 0, k == 7,
                                ([tcp] + dep(f"pP{gi}", "w_in_sb")) if k == 0 else [], signal=(k == 7))
            tuc = ACT(ut[:, 0:2], pP[2][:, 0:2], AF.Copy, [tk[2]] + dep("ut"))
            tv = TT("vector", vbuf[j][:, 0:2], pP[1][:, 0:2], ut[:, 0:2], ALU.mult, [tuc, tk[1]] + dep(f"vbuf{j}"))
            setlast("ut", tv)
            setlast("pP1", tv)
            setlast("pP2", tuc)
            setlast(f"vbuf{j}", tv)
        setlast("hT_halo", tk[2])
        xl_next = issue_xloads(src, 0, 0)
        for T in range(NT):
            tok0 = T * 512
            hs = T % 2
            par = T % 2
            xl_cur = xl_next
            if T + 1 < NT:
                xl_next = issue_xloads(src, (T + 1) * 512, 1 - par)
            hw = norm_to_hT(src, tok0, hs, par, xl_cur)
            yw = []
            tmm = None
            for j in range(8):
                tg_ = {}
                for g_ in (1, 2, 3, 0):
                    for k in range(8):
                        tmm = MM(pP[g_], w_in_sb[:, k, g_ * 1024 + j * 128: g_ * 1024 + (j + 1) * 128],
                                 hT[hs][:, k, :], k == 0, k == 7,
                                 (hw + dep(f"pP{g_}", "w_in_sb")) if k == 0 else [], signal=(k == 7))
                    tg_[g_] = tmm
                tb, tc1, tu1, tz1 = tg_[0], tg_[1], tg_[2], tg_[3]
                tuc = ACT(ut, pP[2], AF.Copy, [tu1] + dep("ut"))
                setlast("pP2", tuc)
                tsz = ACT(sz, pP[3], AF.Silu, [tz1] + dep("sz"))
                setlast("pP3", tsz)
                tv = TT("vector", vbuf[j][:, 2:514], pP[1], ut, ALU.mult, [tuc, tc1] + dep(f"vbuf{j}"))
                setlast("pP1", tv)
                setlast("ut", tv)
                ta = ACT(t1, vbuf[j][:, 0:512], AF.Copy, [tv] + dep("t1", "cw_sb"), scale=cw_sb[:, j, 0:1])
                tb_ = STT("vector", t1, vbuf[j][:, 1:513], cw_sb[:, j, 1:2], t1, ALU.mult, ALU.add, [ta])
                tc_ = STT("vector", t1, vbuf[j][:, 2:514], cw_sb[:, j, 2:3], t1, ALU.mult, ALU.add, [tb_])
                tcar = CP("vector", vbuf[j][:, 0:2], vbuf[j][:, 512:514], [tc_])
                setlast(f"vbuf{j}", tcar)
                td_ = TT("vector", t2, pP[0], t1, ALU.mult, [tc_, tb] + dep("t2"))
                setlast("pP0", td_)
                setlast("t1", td_)
                te_ = TT("vector", yT[:, j, :], t2, sz, ALU.mult, [td_, tsz] + dep("yT"))
                setlast("sz", te_)
                setlast("t2", te_)
                yw.append(te_)
            setlast(f"hT{hs}", tmm)
            outproj_residual(src, dst, tok0, yw, load_x=False, final=False, par=par)

    def pz_stage(layer, src):
        load_g(layer)
        wt = load_w(w_in_pieces(layer, 3072, 4096))
        setlast("w_in_sb", *wt)
        ring = 0
        xl_next = issue_xloads(src, 0, 0)
        for T in range(NT):
            tok0 = T * 512
            hs = T % 2
            par = T % 2
            xl_cur = xl_next
            if T + 1 < NT:
                xl_next = issue_xloads(src, (T + 1) * 512, 1 - par)
            hw = norm_to_hT(src, tok0, hs, par, xl_cur)
            tsh = sch.dma(dq(), HTown3[:, :, tok0:tok0 + 512].rearrange("k p t -> p k t"), hT[hs], f"hst{hs}", hw)
            final_toks.append(tsh)
            tmm = None
            for j in range(8):
                b_ = ring % 4
                ring += 1
                for k in range(8):
                    tmm = MM(pP[b_], w_in_sb[:, k, 3072 + j * 128: 3072 + (j + 1) * 128], hT[hs][:, k, :], k == 0, k == 7,
                             (hw + dep(f"pP{b_}", "w_in_sb")) if k == 0 else [], signal=(k == 7))
                tev = ACT(ev[b_], pP[b_], AF.Silu, [tmm] + dep(f"ev{b_}"))
                setlast(f"pP{b_}", tev)
                tst = sch.dma(dq(), ZTd[j, :, tok0:tok0 + 512], ev[b_], f"ev{b_}", [tev])
                setlast(f"ev{b_}", tst)
                final_toks.append(tst)
            setlast(f"hT{hs}", tmm, tsh)

    def hproj_stage(j_attn, preloaded=False):
        if not preloaded:
            wt = load_w(w_qkv_pieces(j_attn))
            setlast("w_in_sb", *wt)
        ring = 0
        for T in range(16):
            rk, lt = T // 8, T % 8
            hs = T % 2
            g0 = T * 512
            hw = []
            for pc in range(4):
                tl = sch.dma(dq(), hT[hs][:, 2 * pc:2 * pc + 2, :],
                             HTall5[pc, rk, :, :, lt * 512:(lt + 1) * 512].rearrange("kk p t -> p kk t"),
                             f"hld{hs}", dep(f"hT{hs}"))
                hw.append(tl)
            tmm = None
            for jj in range(8):
                b_ = ring % 4
                ring += 1
                for k in range(8):
                    tmm = MM(pP[b_], w_in_sb[:, k, jj * 128:(jj + 1) * 128], hT[hs][:, k, :], k == 0, k == 7,
                             (hw + dep(f"pP{b_}", "w_in_sb")) if k == 0 else [], signal=(k == 7))
                tev = CP("scalar" if jj % 2 == 0 else "vector", ev[b_], pP[b_], [tmm] + dep(f"ev{b_}"))
                setlast(f"pP{b_}", tev)
                dstT = QTa[jj, :, g0:g0 + 512] if jj < 4 else KTa[jj - 4, :, g0:g0 + 512]
                tst = sch.dma(dq(), dstT, ev[b_], f"ev{b_}", [tev])
                setlast(f"ev{b_}", tst)
                final_toks.append(tst)
            for sub in range(4):
                b_ = ring % 4
                ring += 1
                for k in range(8):
                    tmm = MM(pP[b_], hT[hs][:, k, sub * 128:(sub + 1) * 128], w_in_sb[:, k, 1024:1536], k == 0, k == 7,
                             (hw + dep(f"pP{b_}", "w_in_sb")) if k == 0 else [], signal=(k == 7))
                tev = CP("scalar" if sub % 2 == 0 else "vector", ev[b_], pP[b_], [tmm] + dep(f"ev{b_}"))
                setlast(f"pP{b_}", tev)
                tst = sch.dma(dq(), Va[:, g0 + sub * 128:g0 + (sub + 1) * 128, :].rearrange("h t e -> t h e"),
                              ev[b_][:, :].rearrange("p (h e) -> p h e", h=4), f"ev{b_}", [tev])
                setlast(f"ev{b_}", tst)
                final_toks.append(tst)
            setlast(f"hT{hs}", tmm)

    def aout_stage(layer, src, dst, final, preloaded=False):
        if not preloaded:
            wt = load_w(w_out_pieces(layer))
            setlast("w_out_sb", *wt)
        xl_next = issue_xloads(src, 0, 0)
        for T in range(NT):
            tok0 = T * 512
            par = T % 2
            xl_cur = xl_next
            if T + 1 < NT:
                xl_next = issue_xloads(src, (T + 1) * 512, 1 - par)
            tl1 = sch.dma("sync", onl, ONTr3[:, :, tok0:tok0 + 512].rearrange("j p t -> p j t"), "onl", dep("onl"))
            tl2 = sch.dma("sync", ztl, ZTd[:, :, tok0:tok0 + 512].rearrange("j p t -> p j t"), "ztl", dep("ztl"))
            ty = TT("vector", yT, onl, ztl, ALU.mult, [tl1, tl2] + dep("yT"))
            setlast("onl", ty)
            setlast("ztl", ty)
            outproj_residual(src, dst, tok0, [ty], load_x=True, final=final, par=par, xl=xl_cur)

    def attn_stage(j_attn, lam_init):
        t_lq = sch.dma("sync", lq_sb, lqk[j_attn], "lq")
        t_gs = sch.dma("sync", gs_sb, gsub_in[j_attn], "gs")
        t_gs2 = TS("vector", gs_sb, gs_sb, 1.0 - lam_init, None, ALU.mult, None, [t_gs])
        t_p = TT("vector", lprod, lq_sb[:, 0:4:2, :], lq_sb[:, 1:4:2, :], ALU.mult, [t_lq])
        t_s1 = ACT(junkA[:, 0:64], lprod[:, 0, :], AF.Copy, [t_p] + dep("junkA"), accum_out=lsum[:, 0:1])
        t_s2 = ACT(junkA[:, 0:64], lprod[:, 1, :], AF.Copy, [t_s1], accum_out=lsum[:, 1:2])
        t_e = ACT(lsum[:, 2:4], lsum[:, 0:2], AF.Exp, [t_s2])
        t_l1 = TT("vector", neglam, lsum[:, 3:4], lsum[:, 2:3], ALU.subtract, [t_e])
        t_l2 = TS("vector", neglam, neglam, -lam_init, None, ALU.add, None, [t_l1])
        setlast("junkA", t_s2)
        for i in range(2):
            tv1 = sch.op("gpsimd", lambda e, i=i: e.memset(VS[i][:, :, 128:130], 1.0))
            setlast(f"VS{i}", tv1)

        def load_head(h):
            s = h % 2
            toks = []
            toks.append(sch.dma("sync", QT[s][:, 0:S // 2], QTa[h, :, 0:S // 2], f"hd{s}", dep(f"QT{s}")))
            toks.append(sch.dma("sync", QT[s][:, S // 2:S], QTa[h, :, S // 2:S], f"hd{s}", dep(f"QT{s}")))
            toks.append(sch.dma("sync", KT[s][:, 0:S // 2], KTa[h, :, 0:S // 2], f"hd{s}", dep(f"KT{s}")))
            toks.append(sch.dma("sync", KT[s][:, S // 2:S], KTa[h, :, S // 2:S], f"hd{s}", dep(f"KT{s}")))
            vv = Va[h].rearrange("(c p) e -> p c e", p=128)
            for qd in range(4):
                toks.append(sch.dma("sync", VS[s][:, qd * 16:(qd + 1) * 16, 0:128],
                                    vv[:, qd * 16:(qd + 1) * 16, :], f"hd{s}", dep(f"VS{s}")))
            tz = sch.dma("sync", ZC[s], Zb[2 * h:2 * h + 2].rearrange("m p w -> p m w"), f"hz{s}", dep(f"ZC{s}"))
            tzc = None
            for m in range(2):
                tzc = TS("vector", ZC[s][:, m, :], ZC[s][:, m, :], cf_sb[:, 2 * h + m:2 * h + m + 1], None, ALU.subtract,
                         None, [tz] + CONST)
            toks.append(tzc)
            return toks

        sring = [0]
        pring = [0]
        bring = [0]
        pending_epi = [None]

        def emit_epilogue_pe(info):
            h, j, es, ton = info
            sl = sring[0] % 2
            sring[0] += 1
            w0 = dep(f"SP{sl}") + CONST
            tms = []
            for s4 in range(4):
                tm = MM(SP[sl][:, 0, s4 * 128:(s4 + 1) * 128], onb[s4], ident, True, True,
                        (w0 + [ton[s4]]) if s4 == 0 else [ton[s4]], signal=True)
                tms.append(tm)
                setlast(f"onb{s4}", tm)
            os_ = j % 2
            tca = TS("vector", onTa[os_], SP[sl][:, 0, :], sel_sb[:, 0:1], None, ALU.mult, None,
                     [tms[-1]] + dep(f"onTa{os_}"))
            tcb = TS("vector", onTb[os_], SP[sl][:, 0, :], sel_sb[:, 1:2], None, ALU.mult, None,
                     [tca] + dep(f"onTb{os_}"))
            setlast(f"SP{sl}", tca, tcb)
            d_, c0 = j // 8, (j % 8) * 512
            tsa = sch.dma("sync", RS5[0, d_, h, :, c0:c0 + 512], onTa[os_], f"onta{os_}", [tca])
            tsb = sch.dma("sync", RS5[1, d_, h, :, c0:c0 + 512], onTb[os_], f"ontb{os_}", [tcb])
            setlast(f"onTa{os_}", tsa)
            setlast(f"onTb{os_}", tsb)
            final_toks.append(tsa)
            final_toks.append(tsb)

        head_toks = {0: load_head(0)}
        unit_idx = 0
        for h in range(4):
            s = h % 2
            if h + 1 < 4:
                head_toks[h + 1] = load_head(h + 1)
            hw = head_toks[h]
            last_pe_read = None
            for j in range(16):
                nchunks = 4 * j + 4
                q0 = j * 512

                def issue_qk(c):
                    i = c - 4 * j
                    lo = 128 * i if i > 0 else 0
                    sl = sring[0] % 2
                    sring[0] += 1
                    w = dep(f"SP{sl}") + (hw if c == 0 else [])
                    tq = None
                    for m in range(2):
                        tq = MM(SP[sl][:, m, lo:512], KT[s][m * 64:(m + 1) * 64, c * 128:(c + 1) * 128],
                                QT[s][m * 64:(m + 1) * 64, q0 + lo:q0 + 512], True, True, w if m == 0 else [],
                                signal=(m == 1))
                    pl = pring[0] % NP
                    pring[0] += 1
                    if i >= -1:
                        o_idx = 3 - i
                        bl = bring[0] % 2
                        bring[0] += 1
                        tb = STT("vector", SSB[bl][:, :, lo:512], SP[sl][:, :, lo:512], 0.125,
                                 ZC[s][:, :, o_idx * 128 + lo:o_idx * 128 + 512], ALU.mult, ALU.add,
                                 [tq] + dep(f"SSB{bl}") + hw)
                        setlast(f"SP{sl}", tb)
                        te = ACT(PT[pl][:, :, lo:512], SSB[bl][:, :, lo:512], AF.Exp, [tb] + dep(f"PT{pl}"))
                        setlast(f"SSB{bl}", te)
                    else:
                        te = ACT(PT[pl][:, :, lo:512], SP[sl][:, :, lo:512], AF.Exp, [tq] + dep(f"PT{pl}"), scale=0.125)
                        setlast(f"SP{sl}", te)
                    return (c, i, pl, te)

                def issue_av(tile):
                    c, i, pl, te = tile
                    tm = None
                    first = True
                    for s4 in range(4):
                        if s4 < i:
                            continue
                        lastc = 4 * j + s4
                        for m in range(2):
                            a = s4 * 2 + m
                            w = []
                            if first:
                                w = [te] + (dep("OA") if c == 0 else [])
                                first = False
                            tm = MM(OA[:, a, 0:129], PT[pl][:, m, s4 * 128:(s4 + 1) * 128], VS[s][:, c, 0:129],
                                    (c == 0 and m == 0), (c == lastc), w, signal=(s4 == 3 and m == 1))
                    setlast(f"PT{pl}", tm)
                    return tm

                prev = None
                tav = None
                for c in range(nchunks):
                    t_ = issue_qk(c)
                    if prev is not None:
                        tav = issue_av(prev)
                    prev = t_
                    if c == min(6, nchunks - 1) and pending_epi[0] is not None:
                        emit_epilogue_pe(pending_epi[0])
                        pending_epi[0] = None
                tav = issue_av(prev)
                last_pe_read = tav
                es = unit_idx % 2
                unit_idx += 1
                tev = CP("vector", OEV[es], OA[:, :, 0:129], [tav] + dep(f"OEV{es}"))
                setlast("OA", tev)
                trr = sch.op("vector", lambda e, es=es: e.reciprocal(out=rr[:, :, :].rearrange("p s m -> p (s m)"),
                                                                     in_=OEV[es][:, :, 128:129].rearrange("p a o -> p (a o)")),
                             [tev] + dep("rr"))
                tr2 = TS("vector", r2, rr[:, :, 1], neglam[:, 0:1], None, ALU.mult, None, [trr, t_l2] + dep("r2"))
                ton = []
                tlast = None
                tos = []
                tqs = []
                for s4 in range(4):
                    ta = TS("gpsimd", tA[s4], OEV[es][:, 2 * s4 + 1, 0:128], r2[:, s4:s4 + 1], None, ALU.mult, None,
                            [tr2] + dep(f"tA{s4}"))
                    to = STT("vector", oo[s4], OEV[es][:, 2 * s4, 0:128], rr[:, s4, 0:1], tA[s4], ALU.mult, ALU.add,
                             [ta, trr] + dep(f"oo{s4}"))
                    setlast(f"tA{s4}", to)
                    tq_ = ACT(junkA, oo[s4], AF.Square, [to] + CONST + dep("q1", "junkA"), accum_out=q1[:, s4:s4 + 1])
                    setlast("junkA", tq_)
                    tos.append(to)
                    tqs.append(tq_)
                tl_ = ACT(q1[:, 0:4], q1[:, 0:4], AF.Ln, tqs, scale=1.0 / 128, bias=eps_sb[:, 0:1])
                tx_ = ACT(q1[:, 0:4], q1[:, 0:4], AF.Exp, [tl_], scale=-0.5)
                for s4 in range(4):
                    tn_ = STT("vector", onb[s4], oo[s4], q1[:, s4:s4 + 1], gs_sb, ALU.mult, ALU.mult,
                              [tx_, t_gs2] + dep(f"onb{s4}"))
                    setlast(f"oo{s4}", tn_)
                    ton.append(tn_)
                    tlast = tn_
                setlast("q1", tlast)
                setlast("rr", tlast)
                setlast("r2", tlast)
                setlast(f"OEV{es}", tlast)
                if pending_epi[0] is not None:
                    emit_epilogue_pe(pending_epi[0])
                pending_epi[0] = (h, j, es, ton)
            for nm in (f"QT{s}", f"KT{s}", f"VS{s}"):
                addlast(nm, last_pe_read)
            setlast(f"ZC{s}", ("e_vector", sch.cnt.get("e_vector", 0)))
        if pending_epi[0] is not None:
            emit_epilogue_pe(pending_epi[0])

    def collective(kind, op, src2d, dst2d, name):
        return sch.coll(lambda e, k=kind, o=op, a=src2d, b=dst2d: e.collective_compute(
            k, o, replica_groups=GROUPS, ins=[a], outs=[b]), name)

    xs_ = [x0, X1, X2, X3]
    phase_no = [0]

    def want():
        phase_no[0] += 1
        return phase_no[0] <= upto

    for blk in range(2):
        l_conv, l_attn = 2 * blk, 2 * blk + 1
        src = xs_[l_conv]
        mid = xs_[l_conv + 1]
        dst = xs_[l_conv + 2] if blk == 0 else out_ap
        if want():
            if blk == 0:
                conv_stage(l_conv, blk, src, mid, xhalo, use_mask=False)
            else:
                conv_stage(l_conv, blk, src, mid, TAILall[0:2, :], use_mask=True)
            phase_barrier()
        if want():
            pz_stage(l_attn, mid)
            phase_barrier()
        if want():
            for pc in range(4):
                collective("AllGather", ALU.bypass, HTown[pc * 256:(pc + 1) * 256, :], HTall[pc * 512:(pc + 1) * 512, :], f"ag{pc}")
            load_w(w_qkv_pieces(blk))
            phase_barrier()
        if want():
            hproj_stage(blk, preloaded=True)
            phase_barrier()
        if want():
            attn_stage(blk, lam_inits[blk])
            phase_barrier()
        if want():
            for pc in range(2):
                collective("ReduceScatter", ALU.add, RSbuf[pc * 1024:(pc + 1) * 1024, :], ONTr[pc * 512:(pc + 1) * 512, :], f"rs{pc}")
            load_w(w_out_pieces(l_attn))
            phase_barrier()
        if want():
            aout_stage(l_attn, mid, dst, final=(blk == 1), preloaded=True)
            phase_barrier()
        if blk == 0 and want():
            ttail = sch.dma("sync", TAILown[:, :], X2[TOK - 2:TOK, :], "tail")
            final_toks.append(ttail)
            phase_barrier()
            collective("AllGather", ALU.bypass, TAILown, TAILall, "agt")
            phase_barrier()
    if debug_out:
        dbg = nc.dram_tensor("dbg", [TOK, D], F32, kind="ExternalOutput").ap()
        for q_ in range(4):
            td_ = sch.dma("sync", dbg[q_ * 1024:(q_ + 1) * 1024, :], debug_out[q_ * 1024:(q_ + 1) * 1024, :], "dbg")
            final_toks.append(td_)

    mx = {}
    for (k, v) in final_toks:
        mx[k] = max(mx.get(k, 0), v)
    run_sched(nc, sch, list(mx.items()))
    est.close()
    return nc


BF = ml_dtypes.bfloat16
NUM_BUCKETS, MAX_EXACT, REL_MAX = 32, 16, 128
MASKV = -30000.0

def bucket_table(n):
    d = np.arange(n, dtype=np.int64)
    ds = np.maximum(d, 1).astype(np.float32)
    large = MAX_EXACT + (np.log(ds / np.float32(MAX_EXACT)) / np.float32(math.log(REL_MAX / MAX_EXACT))
                         * np.float32(NUM_BUCKETS - MAX_EXACT)).astype(np.int32)
    large = np.minimum(large, NUM_BUCKETS - 1)
    return np.where(d < MAX_EXACT, d, large).astype(np.int64)

def common_inputs(inp):
    rep = lambda a: np.ascontiguousarray(np.broadcast_to(a[..., None, :], a.shape[:-1] + (128, a.shape[-1]))).astype(np.float32)
    return {
        "w_in": np.ascontiguousarray(inp["w_in"], dtype=np.float32),
        "w_out": np.ascontiguousarray(inp["w_out"], dtype=np.float32),
        "norm_g_rep": rep(np.asarray(inp["norm_g"])),
        "final_g_rep": rep(np.asarray(inp["final_g"])),
        "conv_w": np.ascontiguousarray(inp["conv_w"], dtype=np.float32),
        "ident": np.eye(128, dtype=np.float32),
    }

def attn_consts(inp, layer_j, core_r):
    rb = np.asarray(inp["rel_bias"], dtype=np.float32)
    bk = bucket_table(1024)
    maps = np.arange(8) + 8 * core_r
    p = np.arange(128)[:, None]
    w = np.arange(1024)[None, :]
    d = w - 384 - p
    dd = np.clip(d, 0, 1023)
    Zb = np.empty((8, 128, 1024), np.float32)
    for mi, m in enumerate(maps):
        vals = rb[bk[dd], m]
        Zb[mi] = np.where(d >= 0, vals, np.float32(MASKV))
    cf = np.ascontiguousarray(np.broadcast_to(rb[31, maps][None, :], (128, 8))).astype(np.float32)
    lqk = np.stack([np.asarray(inp[k])[layer_j] for k in ("lambda_q1", "lambda_k1", "lambda_q2", "lambda_k2")], 0)
    lqk = np.ascontiguousarray(np.broadcast_to(lqk[None], (128, 4, 64))).astype(np.float32)
    gs = np.ascontiguousarray(np.broadcast_to(np.asarray(inp["subln_g"])[layer_j][None, :], (128, 128))).astype(np.float32)
    return {"Zb": Zb, "cfar": cf, "lqk": lqk, "gsub": gs, "ident": np.eye(128, dtype=np.float32)}


def fused_inputs(inp):
    com = common_inputs(inp)
    x = inp["x"]
    w_in = inp["w_in"]
    lqk = np.stack([np.stack([np.asarray(inp[k])[j] for k in ("lambda_q1", "lambda_k1", "lambda_q2", "lambda_k2")], 0)
                    for j in range(2)], 0)
    lqk = np.ascontiguousarray(np.broadcast_to(lqk[:, None], (2, 128, 4, 64))).astype(np.float32)
    gs = np.ascontiguousarray(np.broadcast_to(np.asarray(inp["subln_g"])[:, None, :], (2, 128, 128))).astype(np.float32)
    maps = []
    for c in range(8):
        b, r = c // 2, c % 2
        m = dict(com)
        m["x0"] = np.ascontiguousarray(x[b, r * TOK:(r + 1) * TOK])
        m["xhalo"] = np.zeros((2, D), np.float32) if r == 0 else np.ascontiguousarray(x[b, TOK - 2:TOK])
        wq = []
        for j in range(2):
            l = 2 * j + 1
            cols = [w_in[l][:, base + r * 512: base + (r + 1) * 512] for base in (0, 1024, 2048)]
            wq.append(np.concatenate(cols, axis=1))
        m["w_qkv"] = np.ascontiguousarray(np.stack(wq, 0)).astype(np.float32)
        ac = attn_consts(inp, 0, r)
        m["Zb"] = ac["Zb"]
        m["cfar"] = ac["cfar"]
        m["lqk"] = lqk
        m["gsub"] = gs
        sel = np.zeros((128, 2), np.float32)
        sel[:, r] = 1.0
        m["sel"] = sel
        m["hmask"] = np.full((128, 1), float(r), np.float32)
        maps.append(m)
    return maps


_NC = {}


def _lam_init(layer_idx):
    return 0.8 - 0.6 * math.exp(-0.3 * layer_idx)


def kernel(x, norm_g, w_in, w_out, conv_w, lambda_q1, lambda_k1, lambda_q2, lambda_k2,
           subln_g, rel_bias, final_g):
    inp = {"x": np.asarray(x, np.float32), "norm_g": np.asarray(norm_g, np.float32),
           "w_in": np.asarray(w_in, np.float32), "w_out": np.asarray(w_out, np.float32),
           "conv_w": np.asarray(conv_w, np.float32),
           "lambda_q1": np.asarray(lambda_q1, np.float32), "lambda_k1": np.asarray(lambda_k1, np.float32),
           "lambda_q2": np.asarray(lambda_q2, np.float32), "lambda_k2": np.asarray(lambda_k2, np.float32),
           "subln_g": np.asarray(subln_g, np.float32), "rel_bias": np.asarray(rel_bias, np.float32),
           "final_g": np.asarray(final_g, np.float32)}
    if "nc" not in _NC:
        _NC["nc"] = build_fused((_lam_init(1), _lam_init(3)))
    maps = fused_inputs(inp)
    res = run_bass_kernel_spmd(_NC["nc"], maps, core_ids=list(range(8)))
    out = np.empty((4, S, D), np.float32)
    for c in range(8):
        out[c // 2, (c % 2) * TOK:(c % 2 + 1) * TOK] = res.results[c]["out"]
    return out
```

```python
import math
import numpy as np
import ml_dtypes
import concourse.bass as bass
import concourse.mybir as mybir
from concourse.bass_utils import run_bass_kernel_spmd

F32 = mybir.dt.float32
BF16 = mybir.dt.bfloat16
AF = mybir.ActivationFunctionType
ALU = mybir.AluOpType

D = 1024
S = 8192
TOK = 4096
NT = TOK // 512
EPS = 1e-6
MASKV = -30000.0
ENGS = ("tensor", "vector", "scalar", "gpsimd", "sync")


class Sched:
    def __init__(self):
        self.ops = {e: [] for e in ENGS}
        self.cnt = {}
        self.sems = {}
        self.bar = []

    def op(self, eng, fn, waits=(), signal=True):
        key = "e_" + eng
        tok = None
        if signal:
            self.cnt[key] = self.cnt.get(key, 0) + 1
            tok = (key, self.cnt[key])
        self.ops[eng].append((fn, self._dd(tuple(waits) + tuple(self.bar)), key if signal else None, 1))
        return tok

    @staticmethod
    def _dd(waits):
        mx = {}
        for w in waits:
            if w is None:
                continue
            k, v = w
            if mx.get(k, 0) < v:
                mx[k] = v
        return tuple(mx.items())

    def dma(self, eng, out, in_, stream, waits=()):
        key = "d_" + stream
        self.cnt[key] = self.cnt.get(key, 0) + 16
        tok = (key, self.cnt[key])
        self.ops[eng].append((lambda e, o=out, i=in_: e.dma_start(out=o, in_=i),
                              self._dd(tuple(waits) + tuple(self.bar)), key, 16))
        return tok

    def coll(self, fn, name, waits=()):
        key = "c_" + name
        self.cnt[key] = self.cnt.get(key, 0) + 1
        tok = (key, self.cnt[key])
        self.ops["gpsimd"].append((fn, self._dd(tuple(waits) + tuple(self.bar)), key, 1))
        return tok

    def barrier(self):
        self.bar = [(k, v) for k, v in self.cnt.items()]

    def sem_keys(self):
        return sorted(self.cnt.keys())

    def replay(self, eng_name, eng):
        waited = {}
        for fn, waits, key, inc in self.ops[eng_name]:
            for (k, v) in waits:
                if waited.get(k, 0) < v:
                    eng.wait_ge(self.sems[k], v)
                    waited[k] = v
            ins = fn(eng)
            if key is not None:
                ins.then_inc(self.sems[key], inc)

    def final_waits(self, eng_name, eng, toks):
        for (k, v) in toks:
            eng.wait_ge(self.sems[k], v)


def run_sched(nc, sch, final_toks):
    from contextlib import ExitStack
    with ExitStack() as st:
        for k in sch.sem_keys():
            sch.sems[k] = st.enter_context(nc.semaphore(k))
        block = st.enter_context(nc.Block())

        @block.sync
        def _(e):
            sch.replay("sync", e)
            sch.final_waits("sync", e, final_toks)

        @block.tensor
        def _(e):
            sch.replay("tensor", e)

        @block.vector
        def _(e):
            sch.replay("vector", e)

        @block.scalar
        def _(e):
            sch.replay("scalar", e)

        @block.gpsimd
        def _(e):
            sch.replay("gpsimd", e)


def build_fused(lam_inits=(0.0, 0.0), debug_out=False, upto=99):
    from contextlib import ExitStack
    nc = bass.Bass("TRN2", target_bir_lowering=False)

    def ext_in(name, shape, dt=F32):
        return nc.dram_tensor(name, list(shape), dt, kind="ExternalInput").ap()

    def internal(name, shape, dt):
        return nc.dram_tensor(name, list(shape), dt).ap()

    x0 = ext_in("x0", [TOK, D])
    xhalo = ext_in("xhalo", [2, D])
    w_in = ext_in("w_in", [4, D, 4 * D])
    w_out = ext_in("w_out", [4, D, D])
    w_qkv = ext_in("w_qkv", [2, D, 1536])
    g_rep = ext_in("norm_g_rep", [4, 128, D])
    fg_rep = ext_in("final_g_rep", [128, D])
    conv_w = ext_in("conv_w", [2, D, 3])
    ident_in = ext_in("ident", [128, 128])
    Zb = ext_in("Zb", [8, 128, 1024])
    cfar = ext_in("cfar", [128, 8])
    lqk = ext_in("lqk", [2, 128, 4, 64])
    gsub_in = ext_in("gsub", [2, 128, 128])
    sel_in = ext_in("sel", [128, 2])
    hmask_in = ext_in("hmask", [128, 1])
    out_ap = nc.dram_tensor("out", [TOK, D], F32, kind="ExternalOutput").ap()

    X1 = internal("X1", [TOK, D], F32)
    X2 = internal("X2", [TOK, D], F32)
    X3 = internal("X3", [TOK, D], F32)
    HTown = internal("HTown", [4 * 8 * 128, 1024], BF16)
    HTall = internal("HTall", [4 * 2 * 8 * 128, 1024], BF16)
    ZTd = internal("ZTd", [8, 128, TOK], BF16)
    QTa = internal("QTa", [4, 128, S], BF16)
    KTa = internal("KTa", [4, 128, S], BF16)
    Va = internal("Va", [4, S, 128], BF16)
    RSbuf = internal("RSbuf", [2 * 8 * 128, TOK], BF16)
    ONTr = internal("ONTr", [8 * 128, TOK], BF16)
    TAILown = internal("TAILown", [2, D], F32)
    TAILall = internal("TAILall", [4, D], F32)
    HTown4 = HTown.rearrange("(pc k p) t -> pc k p t", pc=4, k=8, p=128)
    HTall5 = HTall.rearrange("(pc r k p) t -> pc r k p t", pc=4, r=2, k=8, p=128)
    RS5 = RSbuf.rearrange("(pc d hh p) t -> pc d hh p t", pc=2, d=2, hh=4, p=128)
    ONTr3 = ONTr.rearrange("(h p) t -> h p t", p=128)
    GROUPS = [[0, 1], [2, 3], [4, 5], [6, 7]]

    sch = Sched()
    est = ExitStack()
    ARENA_BYTES = 206848
    arena = est.enter_context(nc.sbuf_tensor("arena", [128, ARENA_BYTES // 2], BF16))
    psA = est.enter_context(nc.psum_tensor("psA", [128, 2048], F32))
    psB = est.enter_context(nc.psum_tensor("psB", [128, 2048], F32))

    def view(off, shape, dt):
        n = 1
        for s_ in shape[1:]:
            n *= s_
        nb = n * (4 if dt == F32 else 2)
        assert off % 4 == 0 and off + nb <= ARENA_BYTES, (off, nb)
        a = arena[:, off // 2:(off + nb) // 2]
        if dt == F32:
            a = a.bitcast(F32)
        if len(shape) == 3:
            a = a.rearrange("p (a b) -> p a b", a=shape[1])
        return a

    class Alloc:
        def __init__(self, base):
            self.off = base

        def __call__(self, shape, dt):
            n = 1
            for s_ in shape[1:]:
                n *= s_
            nb = n * (4 if dt == F32 else 2)
            nb_al = (nb + 31) // 32 * 32
            v = view(self.off, shape, dt)
            self.off += nb_al
            return v

    ta_ = Alloc(0)
    wstage = [ta_([128, 2048], F32) for _ in range(2)]
    w_in_sb = ta_([128, 8, 4096], BF16)
    w_out_sb = ta_([128, 8, 1024], BF16)
    xtA = [ta_([128, D], F32) for _ in range(4)]
    xtB = [wstage[0][:, 0:1024], wstage[0][:, 1024:2048], wstage[1][:, 0:1024], wstage[1][:, 1024:2048]]
    xt2 = [xtA, xtB]
    xt = xtA
    junk = ta_([128, D], BF16)
    hb = [ta_([128, D], BF16) for _ in range(2)]
    hT = [ta_([128, 8, 512], BF16) for _ in range(2)]
    hT_halo = ta_([128, 8, 128], BF16)
    xh = ta_([128, D], F32)
    ut = ta_([128, 512], F32)
    sz = ta_([128, 512], F32)
    vbuf = [ta_([128, 514], F32) for _ in range(8)]
    t1 = ta_([128, 512], F32)
    t2 = ta_([128, 512], F32)
    yT = ta_([128, 8, 512], BF16)
    ev = [ta_([128, 512], BF16) for _ in range(4)]
    xo = [ta_([128, D], F32) for _ in range(2)]
    tok_end = ta_.off
    onl, ztl = hT[0], hT[1]
    aa_ = Alloc(0)
    QT = [aa_([128, S], BF16) for _ in range(2)]
    KT = [aa_([128, S], BF16) for _ in range(2)]
    VS = [aa_([128, 64, 130], BF16) for _ in range(2)]
    ZC = [aa_([128, 2, 1024], F32) for _ in range(2)]
    NP = 6
    PT = [aa_([128, 2, 512], BF16) for _ in range(NP)]
    SSB = [aa_([128, 2, 512], F32) for _ in range(2)]
    OEV = [aa_([128, 8, 129], F32) for _ in range(2)]
    tA = [aa_([128, 128], F32) for _ in range(4)]
    oo = [aa_([128, 128], F32) for _ in range(4)]
    junkA = aa_([128, 128], F32)
    onb = [aa_([128, 128], BF16) for _ in range(4)]
    onTa = [aa_([128, 512], BF16) for _ in range(2)]
    onTb = [aa_([128, 512], BF16) for _ in range(2)]
    att_end = aa_.off
    pa_ = Alloc(max(tok_end, att_end))
    g_sb = pa_([128, D], F32)
    fg_sb = pa_([128, D], F32)
    ident_f = pa_([128, 128], F32)
    ident = pa_([128, 128], BF16)
    cw_sb = pa_([128, 8, 3], F32)
    ss = pa_([128, 8], F32)
    eps_sb = pa_([128, 1], F32)
    cf_sb = pa_([128, 8], F32)
    lq_sb = pa_([128, 4, 64], F32)
    lprod = pa_([128, 2, 64], F32)
    lsum = pa_([128, 4], F32)
    neglam = pa_([128, 1], F32)
    gs_sb = pa_([128, 128], F32)
    rr = pa_([128, 4, 2], F32)
    r2 = pa_([128, 4], F32)
    q1 = pa_([128, 4], F32)
    sel_sb = pa_([128, 2], F32)
    hmask_sb = pa_([128, 1], F32)
    assert pa_.off <= ARENA_BYTES, pa_.off
    pT = psA[:, 0:1024].rearrange("p (a b) -> p a b", a=8)
    pO = psA[:, 1024:2048]
    pP = [psB[:, i * 512:(i + 1) * 512] for i in range(4)]
    SP = [psA[:, 0:1024].rearrange("p (a b) -> p a b", a=2), psA[:, 1024:2048].rearrange("p (a b) -> p a b", a=2)]
    OA = psB[:, :].rearrange("p (a b) -> p a b", a=8)

    last = {}

    def dep(*names):
        out = []
        for n in names:
            out.extend(last.get(n, []))
        return out

    def setlast(name, *toks):
        last[name] = [t for t in toks if t is not None]

    def addlast(name, *toks):
        last.setdefault(name, []).extend([t for t in toks if t is not None])

    def phase_barrier():
        sch.barrier()
        last.clear()

    def MM(out, lhsT, rhs, start, stop, waits=(), signal=False):
        return sch.op("tensor", lambda e, o=out, l=lhsT, r=rhs, a=start, b=stop:
                      e.matmul(o, l, r, start=a, stop=b, skip_group_check=True), waits, signal)

    def ACT(out, in_, func, waits=(), accum_out=None, scale=None, bias=None):
        kw = {}
        if accum_out is not None:
            kw["accum_out"] = accum_out
        if scale is not None:
            kw["scale"] = scale
        if bias is not None:
            kw["bias"] = bias
        return sch.op("scalar", lambda e, o=out, i=in_, f=func, kw=kw: e.activation(out=o, in_=i, func=f, **kw), waits)

    def TT(eng, out, in0, in1, op, waits=()):
        return sch.op(eng, lambda e, o=out, a=in0, b=in1, p=op: e.tensor_tensor(out=o, in0=a, in1=b, op=p), waits)

    def TS(eng, out, in0, s1, s2, op0, op1=None, waits=()):
        if op1 is None:
            return sch.op(eng, lambda e, o=out, a=in0, x=s1, p=op0: e.tensor_scalar(out=o, in0=a, scalar1=x, scalar2=None, op0=p), waits)
        return sch.op(eng, lambda e, o=out, a=in0, x=s1, y=s2, p=op0, q=op1: e.tensor_scalar(out=o, in0=a, scalar1=x, scalar2=y, op0=p, op1=q), waits)

    def STT(eng, out, in0, scalar, in1, op0, op1, waits=()):
        return sch.op(eng, lambda e, o=out, a=in0, s=scalar, b=in1, p=op0, q=op1:
                      e.scalar_tensor_tensor(out=o, in0=a, scalar=s, in1=b, op0=p, op1=q), waits)

    def CP(eng, out, in_, waits=()):
        if eng == "scalar":
            return ACT(out, in_, AF.Copy, waits)
        return sch.op(eng, lambda e, o=out, i=in_: e.tensor_copy(out=o, in_=i), waits)

    dma_rr = [0]

    def dq():
        return "sync"

    t_id = sch.dma("sync", ident_f, ident_in[:, :], "c0")
    t_idc = ACT(ident, ident_f, AF.Copy, [t_id])
    t_eps = sch.op("vector", lambda e: e.memset(eps_sb, EPS))
    t_cf = sch.dma("sync", cf_sb, cfar[:, :], "c1")
    t_sel = sch.dma("sync", sel_sb, sel_in[:, :], "c2")
    t_hm = sch.dma("sync", hmask_sb, hmask_in[:, :], "c3")
    t_fg = sch.dma("sync", fg_sb, fg_rep[:, :], "c4")
    CONST = [t_idc, t_eps, t_cf, t_sel, t_hm, t_fg]

    final_toks = []

    def load_w(pieces):
        toks = []
        for idx, (src_ap, vfn, dst_ap) in enumerate(pieces):
            s = idx % 2
            td = sch.dma(dq(), vfn(wstage[s]), src_ap, f"w{s}", dep(f"wstage{s}"))
            tc_ = CP("vector" if idx % 2 == 0 else "scalar", dst_ap, vfn(wstage[s]), [td])
            setlast(f"wstage{s}", tc_)
            toks.append(tc_)
        return toks

    def w_in_pieces(layer, c_lo, c_hi):
        ps_ = []
        for k in range(8):
            for c0 in range(c_lo, c_hi, 2048):
                n = min(2048, c_hi - c0)
                ps_.append((w_in[layer, k * 128:(k + 1) * 128, c0:c0 + n], (lambda ws, n=n: ws[:, 0:n]),
                            w_in_sb[:, k, c0:c0 + n]))
        return ps_

    def w_qkv_pieces(j):
        return [(w_qkv[j, k * 128:(k + 1) * 128, :], (lambda ws: ws[:, 0:1536]), w_in_sb[:, k, 0:1536]) for k in range(8)]

    def w_out_pieces(layer):
        return [(w_out[layer, k * 128:(k + 2) * 128, :].rearrange("(a p) c -> p a c", p=128),
                 (lambda ws: ws[:, :].rearrange("p (a c) -> p a c", a=2)), w_out_sb[:, k:k + 2, :]) for k in range(0, 8, 2)]

    def issue_xloads(src_ap, tok0, par):
        toks = []
        for sub in range(4):
            nm = f"xt{par}_{sub}"
            w = dep(nm) + (dep("wstage0", "wstage1") if par == 1 else [])
            toks.append(sch.dma("sync", xt2[par][sub], src_ap[tok0 + sub * 128: tok0 + (sub + 1) * 128, :], f"x_{nm}", w))
        return toks

    def norm_tile(src_ap, row0, col, hslot, xb, name, nrows=128, mask=None, tl=None):
        if tl is None:
            tl = sch.dma(dq(), xb[0:nrows, :], src_ap[row0:row0 + nrows, :], f"x_{name}", dep(name))
        if mask is not None:
            tl = TS("vector", xb[0:nrows, :], xb[0:nrows, :], mask[0:nrows, 0:1], None, ALU.mult, None, [tl] + CONST)
        tsq = ACT(junk, xb, AF.Square, [tl] + CONST + dep(f"ss{col}", "junk"), accum_out=ss[:, col:col + 1])
        setlast("junk", tsq)
        tr1 = ACT(ss[:, col:col + 1], ss[:, col:col + 1], AF.Ln, [tsq], scale=1.0 / D, bias=eps_sb[:, 0:1])
        tr2 = ACT(ss[:, col:col + 1], ss[:, col:col + 1], AF.Exp, [tr1], scale=-0.5)
        th = STT("vector", hb[hslot], xb, ss[:, col:col + 1], g_sb, ALU.mult, ALU.mult,
                 [tr2] + dep(f"hb{hslot}", "g_sb"))
        setlast(f"ss{col}", th)
        setlast(name, th)
        return th

    def transpose_hb(th, hslot):
        w = [th] + dep("pT") + CONST
        tm = None
        for k in range(8):
            tm = MM(pT[:, k, :], hb[hslot][:, k * 128:(k + 1) * 128], ident, True, True,
                    w if k == 0 else [], signal=(k == 7))
        setlast(f"hb{hslot}", tm)
        return tm

    def load_g(layer):
        tg = sch.dma("sync", g_sb, g_rep[layer, :, :], "g", dep("g_sb"))
        setlast("g_sb", tg)

    def norm_to_hT(src, tok0, hs, par=0, xl=None):
        hw = []
        for sub in range(4):
            hslot = sub % 2
            th = norm_tile(src, tok0 + sub * 128, sub, hslot, xt2[par][sub], f"xt{par}_{sub}",
                           tl=(xl[sub] if xl is not None else None))
            tm = transpose_hb(th, hslot)
            tcp = CP("scalar" if sub % 2 == 0 else "vector", hT[hs][:, :, sub * 128:(sub + 1) * 128], pT,
                     [tm] + dep(f"hT{hs}"))
            setlast("pT", tcp)
            hw.append(tcp)
        return hw

    def outproj_residual(src, dst, tok0, yw, load_x, final, par=0, xl=None):
        tmm = None
        for sub in range(4):
            xs = sub % 2
            txl = xl[sub] if xl is not None else None
            for half in range(2):
                for j in range(8):
                    tmm = MM(pO[:, half * 512:(half + 1) * 512], yT[:, j, sub * 128:(sub + 1) * 128],
                             w_out_sb[:, j, half * 512:(half + 1) * 512], j == 0, j == 7,
                             (yw + dep("pO", "w_out_sb")) if (j == 0 and half == 0) else [],
                             signal=(j == 7 and half == 1))
            tad = TT("vector", xo[xs], pO, xt2[par][sub], ALU.add, [tmm, txl] + dep(f"xo{xs}"))
            setlast("pO", tad)
            setlast(f"xt{par}_{sub}", tad)
            if final:
                c_ = 4 + xs
                tsq = ACT(junk, xo[xs], AF.Square, [tad] + CONST + dep(f"ss{c_}", "junk"), accum_out=ss[:, c_:c_ + 1])
                setlast("junk", tsq)
                tr1 = ACT(ss[:, c_:c_ + 1], ss[:, c_:c_ + 1], AF.Ln, [tsq], scale=1.0 / D, bias=eps_sb[:, 0:1])
                tr2 = ACT(ss[:, c_:c_ + 1], ss[:, c_:c_ + 1], AF.Exp, [tr1], scale=-0.5)
                tad = STT("vector", xo[xs], xo[xs], ss[:, c_:c_ + 1], fg_sb, ALU.mult, ALU.mult, [tr2])
                setlast(f"ss{c_}", tad)
            tst = sch.dma(dq(), dst[tok0 + sub * 128: tok0 + (sub + 1) * 128, :], xo[xs], f"xo{xs}", [tad])
            setlast(f"xo{xs}", tst)
            final_toks.append(tst)
        setlast("yT", tmm)
        return tmm

    def conv_stage(layer, cj, src, dst, halo_ap, use_mask):
        load_g(layer)
        tcw = sch.dma("sync", cw_sb, conv_w[cj].rearrange("(j p) k -> p j k", p=128), "cw", dep("cw_sb"))
        setlast("cw_sb", tcw)
        wt = load_w(w_in_pieces(layer, 0, 4096) + w_out_pieces(layer))
        setlast("w_in_sb", *wt)
        setlast("w_out_sb", *wt)
        tz = sch.op("gpsimd", lambda e: e.memset(xh, 0.0), dep("xh"))
        setlast("xh", tz)
        th = norm_tile(halo_ap, 0, 4, 0, xh, "xh", nrows=2, mask=hmask_sb if use_mask else None)
        tm = transpose_hb(th, 0)
        tcp = ACT(hT_halo, pT, AF.Copy, [tm] + dep("hT_halo"))
        setlast("pT", tcp)
        for j in range(8):
            tk = {}
            for gi in (1, 2):
                for k in range(8):
                    tk[gi] = MM(pP[gi][:, 0:2], w_in_sb[:, k, gi * 1024 + j * 128: gi * 1024 + (j + 1) * 128],
                                hT_halo[:, k, 0:2], k == 0, k == 7,
                                ([tcp] + dep(f"pP{gi}", "w_in_sb")) if k == 0 else [], signal=(k == 7))
            tuc = ACT(ut[:, 0:2], pP[2][:, 0:2], AF.Copy, [tk[2]] + dep("ut"))
            tv = TT("vector", vbuf[j][:, 0:2], pP[1][:, 0:2], ut[:, 0:2], ALU.mult, [tuc, tk[1]] + dep(f"vbuf{j}"))
            setlast("ut", tv)
            setlast("pP1", tv)
            setlast("pP2", tuc)
            setlast(f"vbuf{j}", tv)
        setlast("hT_halo", tk[2])
        xl_next = issue_xloads(src, 0, 0)
        for T in range(NT):
            tok0 = T * 512
            hs = T % 2
            par = T % 2
            xl_cur = xl_next
            if T + 1 < NT:
                xl_next = issue_xloads(src, (T + 1) * 512, 1 - par)
            hw = norm_to_hT(src, tok0, hs, par, xl_cur)
            yw = []
            tmm = None
            for j in range(8):
                tg_ = {}
                for g_ in (1, 2, 3, 0):
                    for k in range(8):
                        tmm = MM(pP[g_], w_in_sb[:, k, g_ * 1024 + j * 128: g_ * 1024 + (j + 1) * 128],
                                 hT[hs][:, k, :], k == 0, k == 7,
                                 (hw + dep(f"pP{g_}", "w_in_sb")) if k == 0 else [], signal=(k == 7))
                    tg_[g_] = tmm
                tb, tc1, tu1, tz1 = tg_[0], tg_[1], tg_[2], tg_[3]
                tuc = ACT(ut, pP[2], AF.Copy, [tu1] + dep("ut"))
                setlast("pP2", tuc)
                tsz = ACT(sz, pP[3], AF.Silu, [tz1] + dep("sz"))
                setlast("pP3", tsz)
                tv = TT("vector", vbuf[j][:, 2:514], pP[1], ut, ALU.mult, [tuc, tc1] + dep(f"vbuf{j}"))
                setlast("pP1", tv)
                setlast("ut", tv)
                ta = ACT(t1, vbuf[j][:, 0:512], AF.Copy, [tv] + dep("t1", "cw_sb"), scale=cw_sb[:, j, 0:1])
                tb_ = STT("vector", t1, vbuf[j][:, 1:513], cw_sb[:, j, 1:2], t1, ALU.mult, ALU.add, [ta])
                tc_ = STT("vector", t1, vbuf[j][:, 2:514], cw_sb[:, j, 2:3], t1, ALU.mult, ALU.add, [tb_])
                tcar = CP("vector", vbuf[j][:, 0:2], vbuf[j][:, 512:514], [tc_])
                setlast(f"vbuf{j}", tcar)
                td_ = TT("vector", t2, pP[0], t1, ALU.mult, [tc_, tb] + dep("t2"))
                setlast("pP0", td_)
                setlast("t1", td_)
                te_ = TT("vector", yT[:, j, :], t2, sz, ALU.mult, [td_, tsz] + dep("yT"))
                setlast("sz", te_)
                setlast("t2", te_)
                yw.append(te_)
            setlast(f"hT{hs}", tmm)
            outproj_residual(src, dst, tok0, yw, load_x=False, final=False, par=par)

    def pz_stage(layer, src):
        load_g(layer)
        wt = load_w(w_in_pieces(layer, 3072, 4096))
        setlast("w_in_sb", *wt)
        ring = 0
        xl_next = issue_xloads(src, 0, 0)
        for T in range(NT):
            tok0 = T * 512
            hs = T % 2
            par = T % 2
            xl_cur = xl_next
            if T + 1 < NT:
                xl_next = issue_xloads(src, (T + 1) * 512, 1 - par)
            hw = norm_to_hT(src, tok0, hs, par, xl_cur)
            tsh = sch.dma(dq(), HTown4[T // 2, :, :, (T % 2) * 512:(T % 2) * 512 + 512].rearrange("k p t -> p k t"),
                          hT[hs], f"hst{hs}", hw)
            final_toks.append(tsh)
            if T % 2 == 1:
                pc = T // 2
                sch.coll(lambda e, a=HTown[pc * 1024:(pc + 1) * 1024, :], b=HTall[pc * 2048:(pc + 1) * 2048, :]:
                         e.collective_compute("AllGather", ALU.bypass, replica_groups=GROUPS, ins=[a], outs=[b]),
                         f"ag{pc}", waits=[tsh_prev, tsh])
            tsh_prev = tsh
            tmm = None
            for j in range(8):
                b_ = ring % 4
                ring += 1
                for k in range(8):
                    tmm = MM(pP[b_], w_in_sb[:, k, 3072 + j * 128: 3072 + (j + 1) * 128], hT[hs][:, k, :], k == 0, k == 7,
                             (hw + dep(f"pP{b_}", "w_in_sb")) if k == 0 else [], signal=(k == 7))
                tev = ACT(ev[b_], pP[b_], AF.Silu, [tmm] + dep(f"ev{b_}"))
                setlast(f"pP{b_}", tev)
                tst = sch.dma(dq(), ZTd[j, :, tok0:tok0 + 512], ev[b_], f"ev{b_}", [tev])
                setlast(f"ev{b_}", tst)
                final_toks.append(tst)
            setlast(f"hT{hs}", tmm, tsh)

    def hproj_stage(j_attn, preloaded=False):
        if not preloaded:
            wt = load_w(w_qkv_pieces(j_attn))
            setlast("w_in_sb", *wt)
        ring = 0
        for T in range(16):
            rk, lt = T // 8, T % 8
            hs = T % 2
            g0 = T * 512
            tl = sch.dma(dq(), hT[hs], HTall5[lt // 2, rk, :, :, (lt % 2) * 512:(lt % 2) * 512 + 512].rearrange("k p t -> p k t"),
                         f"hld{hs}", dep(f"hT{hs}"))
            hw = [tl]
            tmm = None
            for jj in range(8):
                b_ = ring % 4
                ring += 1
                for k in range(8):
                    tmm = MM(pP[b_], w_in_sb[:, k, jj * 128:(jj + 1) * 128], hT[hs][:, k, :], k == 0, k == 7,
                             (hw + dep(f"pP{b_}", "w_in_sb")) if k == 0 else [], signal=(k == 7))
                tev = CP("scalar" if jj % 2 == 0 else "vector", ev[b_], pP[b_], [tmm] + dep(f"ev{b_}"))
                setlast(f"pP{b_}", tev)
                dstT = QTa[jj, :, g0:g0 + 512] if jj < 4 else KTa[jj - 4, :, g0:g0 + 512]
                tst = sch.dma(dq(), dstT, ev[b_], f"ev{b_}", [tev])
                setlast(f"ev{b_}", tst)
                final_toks.append(tst)
            for sub in range(4):
                b_ = ring % 4
                ring += 1
                for k in range(8):
                    tmm = MM(pP[b_], hT[hs][:, k, sub * 128:(sub + 1) * 128], w_in_sb[:, k, 1024:1536], k == 0, k == 7,
                             (hw + dep(f"pP{b_}", "w_in_sb")) if k == 0 else [], signal=(k == 7))
                tev = CP("scalar" if sub % 2 == 0 else "vector", ev[b_], pP[b_], [tmm] + dep(f"ev{b_}"))
                setlast(f"pP{b_}", tev)
                tst = sch.dma(dq(), Va[:, g0 + sub * 128:g0 + (sub + 1) * 128, :].rearrange("h t e -> t h e"),
                              ev[b_][:, :].rearrange("p (h e) -> p h e", h=4), f"ev{b_}", [tev])
                setlast(f"ev{b_}", tst)
                final_toks.append(tst)
            setlast(f"hT{hs}", tmm)

    def aout_stage(layer, src, dst, final, preloaded=False):
        if not preloaded:
            wt = load_w(w_out_pieces(layer))
            setlast("w_out_sb", *wt)
        xl_next = issue_xloads(src, 0, 0)
        for T in range(NT):
            tok0 = T * 512
            par = T % 2
            xl_cur = xl_next
            if T + 1 < NT:
                xl_next = issue_xloads(src, (T + 1) * 512, 1 - par)
            tl1 = sch.dma("sync", onl, ONTr3[:, :, tok0:tok0 + 512].rearrange("j p t -> p j t"), "onl", dep("onl"))
            tl2 = sch.dma("sync", ztl, ZTd[:, :, tok0:tok0 + 512].rearrange("j p t -> p j t"), "ztl", dep("ztl"))
            ty = TT("vector", yT, onl, ztl, ALU.mult, [tl1, tl2] + dep("yT"))
            setlast("onl", ty)
            setlast("ztl", ty)
            outproj_residual(src, dst, tok0, [ty], load_x=True, final=final, par=par, xl=xl_cur)

    def attn_stage(j_attn, lam_init):
        t_lq = sch.dma("sync", lq_sb, lqk[j_attn], "lq")
        t_gs = sch.dma("sync", gs_sb, gsub_in[j_attn], "gs")
        t_gs2 = TS("vector", gs_sb, gs_sb, 1.0 - lam_init, None, ALU.mult, None, [t_gs])
        t_p = TT("vector", lprod, lq_sb[:, 0:4:2, :], lq_sb[:, 1:4:2, :], ALU.mult, [t_lq])
        t_s1 = ACT(junkA[:, 0:64], lprod[:, 0, :], AF.Copy, [t_p] + dep("junkA"), accum_out=lsum[:, 0:1])
        t_s2 = ACT(junkA[:, 0:64], lprod[:, 1, :], AF.Copy, [t_s1], accum_out=lsum[:, 1:2])
        t_e = ACT(lsum[:, 2:4], lsum[:, 0:2], AF.Exp, [t_s2])
        t_l1 = TT("vector", neglam, lsum[:, 3:4], lsum[:, 2:3], ALU.subtract, [t_e])
        t_l2 = TS("vector", neglam, neglam, -lam_init, None, ALU.add, None, [t_l1])
        setlast("junkA", t_s2)
        for i in range(2):
            tv1 = sch.op("gpsimd", lambda e, i=i: e.memset(VS[i][:, :, 128:130], 1.0))
            setlast(f"VS{i}", tv1)

        def load_head(h):
            s = h % 2
            toks = []
            toks.append(sch.dma("sync", QT[s][:, 0:S // 2], QTa[h, :, 0:S // 2], f"hd{s}", dep(f"QT{s}")))
            toks.append(sch.dma("sync", QT[s][:, S // 2:S], QTa[h, :, S // 2:S], f"hd{s}", dep(f"QT{s}")))
            toks.append(sch.dma("sync", KT[s][:, 0:S // 2], KTa[h, :, 0:S // 2], f"hd{s}", dep(f"KT{s}")))
            toks.append(sch.dma("sync", KT[s][:, S // 2:S], KTa[h, :, S // 2:S], f"hd{s}", dep(f"KT{s}")))
            vv = Va[h].rearrange("(c p) e -> p c e", p=128)
            for qd in range(4):
                toks.append(sch.dma("sync", VS[s][:, qd * 16:(qd + 1) * 16, 0:128],
                                    vv[:, qd * 16:(qd + 1) * 16, :], f"hd{s}", dep(f"VS{s}")))
            tz = sch.dma("sync", ZC[s], Zb[2 * h:2 * h + 2].rearrange("m p w -> p m w"), f"hz{s}", dep(f"ZC{s}"))
            tzc = None
            for m in range(2):
                tzc = TS("vector", ZC[s][:, m, :], ZC[s][:, m, :], cf_sb[:, 2 * h + m:2 * h + m + 1], None, ALU.subtract,
                         None, [tz] + CONST)
            toks.append(tzc)
            return toks

        sring = [0]
        pring = [0]
        bring = [0]
        pending_epi = [None]

        def emit_epilogue_pe(info):
            h, j, es, ton = info
            sl = sring[0] % 2
            sring[0] += 1
            w0 = dep(f"SP{sl}") + CONST
            tms = []
            for s4 in range(4):
                tm = MM(SP[sl][:, 0, s4 * 128:(s4 + 1) * 128], onb[s4], ident, True, True,
                        (w0 + [ton[s4]]) if s4 == 0 else [ton[s4]], signal=True)
                tms.append(tm)
                setlast(f"onb{s4}", tm)
            os_ = j % 2
            tca = TS("vector", onTa[os_], SP[sl][:, 0, :], sel_sb[:, 0:1], None, ALU.mult, None,
                     [tms[-1]] + dep(f"onTa{os_}"))
            tcb = TS("vector", onTb[os_], SP[sl][:, 0, :], sel_sb[:, 1:2], None, ALU.mult, None,
                     [tca] + dep(f"onTb{os_}"))
            setlast(f"SP{sl}", tca, tcb)
            d_, c0 = j // 8, (j % 8) * 512
            tsa = sch.dma("sync", RS5[0, d_, h, :, c0:c0 + 512], onTa[os_], f"onta{os_}", [tca])
            tsb = sch.dma("sync", RS5[1, d_, h, :, c0:c0 + 512], onTb[os_], f"ontb{os_}", [tcb])
            setlast(f"onTa{os_}", tsa)
            setlast(f"onTb{os_}", tsb)
            final_toks.append(tsa)
            final_toks.append(tsb)

        head_toks = {0: load_head(0)}
        unit_idx = 0
        for h in range(4):
            s = h % 2
            if h + 1 < 4:
                head_toks[h + 1] = load_head(h + 1)
            hw = head_toks[h]
            last_pe_read = None
            for j in range(16):
                nchunks = 4 * j + 4
                q0 = j * 512

                def issue_qk(c):
                    i = c - 4 * j
                    lo = 128 * i if i > 0 else 0
                    sl = sring[0] % 2
                    sring[0] += 1
                    w = dep(f"SP{sl}") + (hw if c == 0 else [])
                    tq = None
                    for m in range(2):
                        tq = MM(SP[sl][:, m, lo:512], KT[s][m * 64:(m + 1) * 64, c * 128:(c + 1) * 128],
                                QT[s][m * 64:(m + 1) * 64, q0 + lo:q0 + 512], True, True, w if m == 0 else [],
                                signal=(m == 1))
                    pl = pring[0] % NP
                    pring[0] += 1
                    if i >= -1:
                        o_idx = 3 - i
                        bl = bring[0] % 2
                        bring[0] += 1
                        tb = STT("vector", SSB[bl][:, :, lo:512], SP[sl][:, :, lo:512], 0.125,
                                 ZC[s][:, :, o_idx * 128 + lo:o_idx * 128 + 512], ALU.mult, ALU.add,
                                 [tq] + dep(f"SSB{bl}") + hw)
                        setlast(f"SP{sl}", tb)
                        te = ACT(PT[pl][:, :, lo:512], SSB[bl][:, :, lo:512], AF.Exp, [tb] + dep(f"PT{pl}"))
                        setlast(f"SSB{bl}", te)
                    else:
                        te = ACT(PT[pl][:, :, lo:512], SP[sl][:, :, lo:512], AF.Exp, [tq] + dep(f"PT{pl}"), scale=0.125)
                        setlast(f"SP{sl}", te)
                    return (c, i, pl, te)

                def issue_av(tile):
                    c, i, pl, te = tile
                    tm = None
                    first = True
                    for s4 in range(4):
                        if s4 < i:
                            continue
                        lastc = 4 * j + s4
                        for m in range(2):
                            a = s4 * 2 + m
                            w = []
                            if first:
                                w = [te] + (dep("OA") if c == 0 else [])
                                first = False
                            tm = MM(OA[:, a, 0:129], PT[pl][:, m, s4 * 128:(s4 + 1) * 128], VS[s][:, c, 0:129],
                                    (c == 0 and m == 0), (c == lastc), w, signal=(s4 == 3 and m == 1))
                    setlast(f"PT{pl}", tm)
                    return tm

                prev = None
                tav = None
                for c in range(nchunks):
                    t_ = issue_qk(c)
                    if prev is not None:
                        tav = issue_av(prev)
                    prev = t_
                    if c == min(6, nchunks - 1) and pending_epi[0] is not None:
                        emit_epilogue_pe(pending_epi[0])
                        pending_epi[0] = None
                tav = issue_av(prev)
                last_pe_read = tav
                es = unit_idx % 2
                unit_idx += 1
                tev = CP("vector", OEV[es], OA[:, :, 0:129], [tav] + dep(f"OEV{es}"))
                setlast("OA", tev)
                trr = sch.op("vector", lambda e, es=es: e.reciprocal(out=rr[:, :, :].rearrange("p s m -> p (s m)"),
                                                                     in_=OEV[es][:, :, 128:129].rearrange("p a o -> p (a o)")),
                             [tev] + dep("rr"))
                tr2 = TS("vector", r2, rr[:, :, 1], neglam[:, 0:1], None, ALU.mult, None, [trr, t_l2] + dep("r2"))
                ton = []
                tlast = None
                tos = []
                tqs = []
                for s4 in range(4):
                    ta = TS("gpsimd", tA[s4], OEV[es][:, 2 * s4 + 1, 0:128], r2[:, s4:s4 + 1], None, ALU.mult, None,
                            [tr2] + dep(f"tA{s4}"))
                    to = STT("vector", oo[s4], OEV[es][:, 2 * s4, 0:128], rr[:, s4, 0:1], tA[s4], ALU.mult, ALU.add,
                             [ta, trr] + dep(f"oo{s4}"))
                    setlast(f"tA{s4}", to)
                    tq_ = ACT(junkA, oo[s4], AF.Square, [to] + CONST + dep("q1", "junkA"), accum_out=q1[:, s4:s4 + 1])
                    setlast("junkA", tq_)
                    tos.append(to)
                    tqs.append(tq_)
                tl_ = ACT(q1[:, 0:4], q1[:, 0:4], AF.Ln, tqs, scale=1.0 / 128, bias=eps_sb[:, 0:1])
                tx_ = ACT(q1[:, 0:4], q1[:, 0:4], AF.Exp, [tl_], scale=-0.5)
                for s4 in range(4):
                    tn_ = STT("vector", onb[s4], oo[s4], q1[:, s4:s4 + 1], gs_sb, ALU.mult, ALU.mult,
                              [tx_, t_gs2] + dep(f"onb{s4}"))
                    setlast(f"oo{s4}", tn_)
                    ton.append(tn_)
                    tlast = tn_
                setlast("q1", tlast)
                setlast("rr", tlast)
                setlast("r2", tlast)
                setlast(f"OEV{es}", tlast)
                if pending_epi[0] is not None:
                    emit_epilogue_pe(pending_epi[0])
                pending_epi[0] = (h, j, es, ton)
            for nm in (f"QT{s}", f"KT{s}", f"VS{s}"):
                addlast(nm, last_pe_read)
            setlast(f"ZC{s}", ("e_vector", sch.cnt.get("e_vector", 0)))
        if pending_epi[0] is not None:
            emit_epilogue_pe(pending_epi[0])

    def collective(kind, op, src2d, dst2d, name):
        return sch.coll(lambda e, k=kind, o=op, a=src2d, b=dst2d: e.collective_compute(
            k, o, replica_groups=GROUPS, ins=[a], outs=[b]), name)

    xs_ = [x0, X1, X2, X3]
    phase_no = [0]

    def want():
        phase_no[0] += 1
        return phase_no[0] <= upto

    for blk in range(2):
        l_conv, l_attn = 2 * blk, 2 * blk + 1
        src = xs_[l_conv]
        mid = xs_[l_conv + 1]
        dst = xs_[l_conv + 2] if blk == 0 else out_ap
        if want():
            if blk == 0:
                conv_stage(l_conv, blk, src, mid, xhalo, use_mask=False)
            else:
                conv_stage(l_conv, blk, src, mid, TAILall[0:2, :], use_mask=True)
            phase_barrier()
        if want():
            pz_stage(l_attn, mid)
            phase_barrier()
        if want():
            load_w(w_qkv_pieces(blk))
            phase_barrier()
        if want():
            hproj_stage(blk, preloaded=True)
            phase_barrier()
        if want():
            attn_stage(blk, lam_inits[blk])
            phase_barrier()
        if want():
            for pc in range(2):
                collective("ReduceScatter", ALU.add, RSbuf[pc * 1024:(pc + 1) * 1024, :], ONTr[pc * 512:(pc + 1) * 512, :], f"rs{pc}")
            load_w(w_out_pieces(l_attn))
            phase_barrier()
        if want():
            aout_stage(l_attn, mid, dst, final=(blk == 1), preloaded=True)
            phase_barrier()
        if blk == 0 and want():
            ttail = sch.dma("sync", TAILown[:, :], X2[TOK - 2:TOK, :], "tail")
            final_toks.append(ttail)
            phase_barrier()
            collective("AllGather", ALU.bypass, TAILown, TAILall, "agt")
            phase_barrier()
    if debug_out:
        dbg = nc.dram_tensor("dbg", [TOK, D], F32, kind="ExternalOutput").ap()
        for q_ in range(4):
            td_ = sch.dma("sync", dbg[q_ * 1024:(q_ + 1) * 1024, :], debug_out[q_ * 1024:(q_ + 1) * 1024, :], "dbg")
            final_toks.append(td_)

    mx = {}
    for (k, v) in final_toks:
        mx[k] = max(mx.get(k, 0), v)
    run_sched(nc, sch, list(mx.items()))
    est.close()
    return nc


BF = ml_dtypes.bfloat16
NUM_BUCKETS, MAX_EXACT, REL_MAX = 32, 16, 128
MASKV = -30000.0

def bucket_table(n):
    d = np.arange(n, dtype=np.int64)
    ds = np.maximum(d, 1).astype(np.float32)
    large = MAX_EXACT + (np.log(ds / np.float32(MAX_EXACT)) / np.float32(math.log(REL_MAX / MAX_EXACT))
                         * np.float32(NUM_BUCKETS - MAX_EXACT)).astype(np.int32)
    large = np.minimum(large, NUM_BUCKETS - 1)
    return np.where(d < MAX_EXACT, d, large).astype(np.int64)

def common_inputs(inp):
    rep = lambda a: np.ascontiguousarray(np.broadcast_to(a[..., None, :], a.shape[:-1] + (128, a.shape[-1]))).astype(np.float32)
    return {
        "w_in": np.ascontiguousarray(inp["w_in"], dtype=np.float32),
        "w_out": np.ascontiguousarray(inp["w_out"], dtype=np.float32),
        "norm_g_rep": rep(np.asarray(inp["norm_g"])),
        "final_g_rep": rep(np.asarray(inp["final_g"])),
        "conv_w": np.ascontiguousarray(inp["conv_w"], dtype=np.float32),
        "ident": np.eye(128, dtype=np.float32),
    }

def attn_consts(inp, layer_j, core_r):
    rb = np.asarray(inp["rel_bias"], dtype=np.float32)
    bk = bucket_table(1024)
    maps = np.arange(8) + 8 * core_r
    p = np.arange(128)[:, None]
    w = np.arange(1024)[None, :]
    d = w - 384 - p
    dd = np.clip(d, 0, 1023)
    Zb = np.empty((8, 128, 1024), np.float32)
    for mi, m in enumerate(maps):
        vals = rb[bk[dd], m]
        Zb[mi] = np.where(d >= 0, vals, np.float32(MASKV))
    cf = np.ascontiguousarray(np.broadcast_to(rb[31, maps][None, :], (128, 8))).astype(np.float32)
    lqk = np.stack([np.asarray(inp[k])[layer_j] for k in ("lambda_q1", "lambda_k1", "lambda_q2", "lambda_k2")], 0)
    lqk = np.ascontiguousarray(np.broadcast_to(lqk[None], (128, 4, 64))).astype(np.float32)
    gs = np.ascontiguousarray(np.broadcast_to(np.asarray(inp["subln_g"])[layer_j][None, :], (128, 128))).astype(np.float32)
    return {"Zb": Zb, "cfar": cf, "lqk": lqk, "gsub": gs, "ident": np.eye(128, dtype=np.float32)}


def fused_inputs(inp):
    com = common_inputs(inp)
    x = inp["x"]
    w_in = inp["w_in"]
    lqk = np.stack([np.stack([np.asarray(inp[k])[j] for k in ("lambda_q1", "lambda_k1", "lambda_q2", "lambda_k2")], 0)
                    for j in range(2)], 0)
    lqk = np.ascontiguousarray(np.broadcast_to(lqk[:, None], (2, 128, 4, 64))).astype(np.float32)
    gs = np.ascontiguousarray(np.broadcast_to(np.asarray(inp["subln_g"])[:, None, :], (2, 128, 128))).astype(np.float32)
    maps = []
    for c in range(8):
        b, r = c // 2, c % 2
        m = dict(com)
        m["x0"] = np.ascontiguousarray(x[b, r * TOK:(r + 1) * TOK])
        m["xhalo"] = np.zeros((2, D), np.float32) if r == 0 else np.ascontiguousarray(x[b, TOK - 2:TOK])
        wq = []
        for j in range(2):
            l = 2 * j + 1
            cols = [w_in[l][:, base + r * 512: base + (r + 1) * 512] for base in (0, 1024, 2048)]
            wq.append(np.concatenate(cols, axis=1))
        m["w_qkv"] = np.ascontiguousarray(np.stack(wq, 0)).astype(np.float32)
        ac = attn_consts(inp, 0, r)
        m["Zb"] = ac["Zb"]
        m["cfar"] = ac["cfar"]
        m["lqk"] = lqk
        m["gsub"] = gs
        sel = np.zeros((128, 2), np.float32)
        sel[:, r] = 1.0
        m["sel"] = sel
        m["hmask"] = np.full((128, 1), float(r), np.float32)
        maps.append(m)
    return maps


_NC = {}


def _lam_init(layer_idx):
    return 0.8 - 0.6 * math.exp(-0.3 * layer_idx)


def kernel(x, norm_g, w_in, w_out, conv_w, lambda_q1, lambda_k1, lambda_q2, lambda_k2,
           subln_g, rel_bias, final_g):
    inp = {"x": np.asarray(x, np.float32), "norm_g": np.asarray(norm_g, np.float32),
           "w_in": np.asarray(w_in, np.float32), "w_out": np.asarray(w_out, np.float32),
           "conv_w": np.asarray(conv_w, np.float32),
           "lambda_q1": np.asarray(lambda_q1, np.float32), "lambda_k1": np.asarray(lambda_k1, np.float32),
           "lambda_q2": np.asarray(lambda_q2, np.float32), "lambda_k2": np.asarray(lambda_k2, np.float32),
           "subln_g": np.asarray(subln_g, np.float32), "rel_bias": np.asarray(rel_bias, np.float32),
           "final_g": np.asarray(final_g, np.float32)}
    if "nc" not in _NC:
        _NC["nc"] = build_fused((_lam_init(1), _lam_init(3)))
    maps = fused_inputs(inp)
    res = run_bass_kernel_spmd(_NC["nc"], maps, core_ids=list(range(8)))
    out = np.empty((4, S, D), np.float32)
    for c in range(8):
        out[c // 2, (c % 2) * TOK:(c % 2 + 1) * TOK] = res.results[c]["out"]
    return out
```

```python
import math
import numpy as np
import ml_dtypes
import concourse.bass as bass
import concourse.mybir as mybir
from concourse.bass_utils import run_bass_kernel_spmd

F32 = mybir.dt.float32
BF16 = mybir.dt.bfloat16
AF = mybir.ActivationFunctionType
ALU = mybir.AluOpType

D = 1024
S = 8192
TOK = 4096
NT = TOK // 512
EPS = 1e-6
MASKV = -30000.0
ENGS = ("tensor", "vector", "scalar", "gpsimd", "sync")


class Sched:
    def __init__(self):
        self.ops = {e: [] for e in ENGS}
        self.cnt = {}
        self.sems = {}
        self.bar = []

    def op(self, eng, fn, waits=(), signal=True):
        key = "e_" + eng
        tok = None
        if signal:
            self.cnt[key] = self.cnt.get(key, 0) + 1
            tok = (key, self.cnt[key])
        self.ops[eng].append((fn, self._dd(tuple(waits) + tuple(self.bar)), key if signal else None, 1))
        return tok

    @staticmethod
    def _dd(waits):
        mx = {}
        for w in waits:
            if w is None:
                continue
            k, v = w
            if mx.get(k, 0) < v:
                mx[k] = v
        return tuple(mx.items())

    def dma(self, eng, out, in_, stream, waits=()):
        key = "d_" + stream
        self.cnt[key] = self.cnt.get(key, 0) + 16
        tok = (key, self.cnt[key])
        self.ops[eng].append((lambda e, o=out, i=in_: e.dma_start(out=o, in_=i),
                              self._dd(tuple(waits) + tuple(self.bar)), key, 16))
        return tok

    def coll(self, fn, name, waits=()):
        key = "c_" + name
        self.cnt[key] = self.cnt.get(key, 0) + 1
        tok = (key, self.cnt[key])
        self.ops["gpsimd"].append((fn, self._dd(tuple(waits) + tuple(self.bar)), key, 1))
        return tok

    def barrier(self):
        self.bar = [(k, v) for k, v in self.cnt.items()]

    def sem_keys(self):
        return sorted(self.cnt.keys())

    def replay(self, eng_name, eng):
        waited = {}
        for fn, waits, key, inc in self.ops[eng_name]:
            for (k, v) in waits:
                if waited.get(k, 0) < v:
                    eng.wait_ge(self.sems[k], v)
                    waited[k] = v
            ins = fn(eng)
            if key is not None:
                ins.then_inc(self.sems[key], inc)

    def final_waits(self, eng_name, eng, toks):
        for (k, v) in toks:
            eng.wait_ge(self.sems[k], v)


def run_sched(nc, sch, final_toks):
    from contextlib import ExitStack
    with ExitStack() as st:
        for k in sch.sem_keys():
            sch.sems[k] = st.enter_context(nc.semaphore(k))
        block = st.enter_context(nc.Block())

        @block.sync
        def _(e):
            sch.replay("sync", e)
            sch.final_waits("sync", e, final_toks)

        @block.tensor
        def _(e):
            sch.replay("tensor", e)

        @block.vector
        def _(e):
            sch.replay("vector", e)

        @block.scalar
        def _(e):
            sch.replay("scalar", e)

        @block.gpsimd
        def _(e):
            sch.replay("gpsimd", e)


def build_fused(lam_inits=(0.0, 0.0), debug_out=False, upto=99):
    from contextlib import ExitStack
    nc = bass.Bass("TRN2", target_bir_lowering=False)

    def ext_in(name, shape, dt=F32):
        return nc.dram_tensor(name, list(shape), dt, kind="ExternalInput").ap()

    def internal(name, shape, dt):
        return nc.dram_tensor(name, list(shape), dt).ap()

    x0 = ext_in("x0", [TOK, D])
    xhalo = ext_in("xhalo", [2, D])
    w_in = ext_in("w_in", [4, D, 4 * D])
    w_out = ext_in("w_out", [4, D, D])
    w_qkv = ext_in("w_qkv", [2, D, 1536])
    g_rep = ext_in("norm_g_rep", [4, 128, D])
    fg_rep = ext_in("final_g_rep", [128, D])
    conv_w = ext_in("conv_w", [2, D, 3])
    ident_in = ext_in("ident", [128, 128])
    Zb = ext_in("Zb", [8, 128, 1024])
    cfar = ext_in("cfar", [128, 8])
    lqk = ext_in("lqk", [2, 128, 4, 64])
    gsub_in = ext_in("gsub", [2, 128, 128])
    sel_in = ext_in("sel", [128, 2])
    hmask_in = ext_in("hmask", [128, 1])
    out_ap = nc.dram_tensor("out", [TOK, D], F32, kind="ExternalOutput").ap()

    X1 = internal("X1", [TOK, D], F32)
    X2 = internal("X2", [TOK, D], F32)
    X3 = internal("X3", [TOK, D], F32)
    HTown = internal("HTown", [8 * 128, TOK], BF16)
    HTall = internal("HTall", [2 * 8 * 128, TOK], BF16)
    ZTd = internal("ZTd", [8, 128, TOK], BF16)
    QTa = internal("QTa", [4, 128, S], BF16)
    KTa = internal("KTa", [4, 128, S], BF16)
    Va = internal("Va", [4, S, 128], BF16)
    RSbuf = internal("RSbuf", [2 * 8 * 128, TOK], BF16)
    ONTr = internal("ONTr", [8 * 128, TOK], BF16)
    TAILown = internal("TAILown", [2, D], F32)
    TAILall = internal("TAILall", [4, D], F32)
    HTown3 = HTown.rearrange("(k p) t -> k p t", p=128)
    HTall5 = HTall.rearrange("(pc r kk p) t -> pc r kk p t", pc=4, r=2, kk=2, p=128)
    RS5 = RSbuf.rearrange("(pc d hh p) t -> pc d hh p t", pc=2, d=2, hh=4, p=128)
    ONTr3 = ONTr.rearrange("(h p) t -> h p t", p=128)
    GROUPS = [[0, 1], [2, 3], [4, 5], [6, 7]]

    sch = Sched()
    est = ExitStack()
    ARENA_BYTES = 206848
    arena = est.enter_context(nc.sbuf_tensor("arena", [128, ARENA_BYTES // 2], BF16))
    psA = est.enter_context(nc.psum_tensor("psA", [128, 2048], F32))
    psB = est.enter_context(nc.psum_tensor("psB", [128, 2048], F32))

    def view(off, shape, dt):
        n = 1
        for s_ in shape[1:]:
            n *= s_
        nb = n * (4 if dt == F32 else 2)
        assert off % 4 == 0 and off + nb <= ARENA_BYTES, (off, nb)
        a = arena[:, off // 2:(off + nb) // 2]
        if dt == F32:
            a = a.bitcast(F32)
        if len(shape) == 3:
            a = a.rearrange("p (a b) -> p a b", a=shape[1])
        return a

    class Alloc:
        def __init__(self, base):
            self.off = base

        def __call__(self, shape, dt):
            n = 1
            for s_ in shape[1:]:
                n *= s_
            nb = n * (4 if dt == F32 else 2)
            nb_al = (nb + 31) // 32 * 32
            v = view(self.off, shape, dt)
            self.off += nb_al
            return v

    ta_ = Alloc(0)
    wstage = [ta_([128, 2048], F32) for _ in range(2)]
    w_in_sb = ta_([128, 8, 4096], BF16)
    w_out_sb = ta_([128, 8, 1024], BF16)
    xtA = [ta_([128, D], F32) for _ in range(4)]
    xtB = [wstage[0][:, 0:1024], wstage[0][:, 1024:2048], wstage[1][:, 0:1024], wstage[1][:, 1024:2048]]
    xt2 = [xtA, xtB]
    xt = xtA
    junk = ta_([128, D], BF16)
    hb = [ta_([128, D], BF16) for _ in range(2)]
    hT = [ta_([128, 8, 512], BF16) for _ in range(2)]
    hT_halo = ta_([128, 8, 128], BF16)
    xh = ta_([128, D], F32)
    ut = ta_([128, 512], F32)
    sz = ta_([128, 512], F32)
    vbuf = [ta_([128, 514], F32) for _ in range(8)]
    t1 = ta_([128, 512], F32)
    t2 = ta_([128, 512], F32)
    yT = ta_([128, 8, 512], BF16)
    ev = [ta_([128, 512], BF16) for _ in range(4)]
    xo = [ta_([128, D], F32) for _ in range(2)]
    tok_end = ta_.off
    onl, ztl = hT[0], hT[1]
    aa_ = Alloc(0)
    QT = [aa_([128, S], BF16) for _ in range(2)]
    KT = [aa_([128, S], BF16) for _ in range(2)]
    VS = [aa_([128, 64, 130], BF16) for _ in range(2)]
    ZC = [aa_([128, 2, 1024], F32) for _ in range(2)]
    NP = 6
    PT = [aa_([128, 2, 512], BF16) for _ in range(NP)]
    SSB = [aa_([128, 2, 512], F32) for _ in range(2)]
    OEV = [aa_([128, 8, 129], F32) for _ in range(2)]
    tA = [aa_([128, 128], F32) for _ in range(4)]
    oo = [aa_([128, 128], F32) for _ in range(4)]
    junkA = aa_([128, 128], F32)
    onb = [aa_([128, 128], BF16) for _ in range(4)]
    onTa = [aa_([128, 512], BF16) for _ in range(2)]
    onTb = [aa_([128, 512], BF16) for _ in range(2)]
    att_end = aa_.off
    pa_ = Alloc(max(tok_end, att_end))
    g_sb = pa_([128, D], F32)
    fg_sb = pa_([128, D], F32)
    ident_f = pa_([128, 128], F32)
    ident = pa_([128, 128], BF16)
    cw_sb = pa_([128, 8, 3], F32)
    ss = pa_([128, 8], F32)
    eps_sb = pa_([128, 1], F32)
    cf_sb = pa_([128, 8], F32)
    lq_sb = pa_([128, 4, 64], F32)
    lprod = pa_([128, 2, 64], F32)
    lsum = pa_([128, 4], F32)
    neglam = pa_([128, 1], F32)
    gs_sb = pa_([128, 128], F32)
    rr = pa_([128, 4, 2], F32)
    r2 = pa_([128, 4], F32)
    q1 = pa_([128, 4], F32)
    sel_sb = pa_([128, 2], F32)
    hmask_sb = pa_([128, 1], F32)
    assert pa_.off <= ARENA_BYTES, pa_.off
    pT = psA[:, 0:1024].rearrange("p (a b) -> p a b", a=8)
    pO = psA[:, 1024:2048]
    pP = [psB[:, i * 512:(i + 1) * 512] for i in range(4)]
    SP = [psA[:, 0:1024].rearrange("p (a b) -> p a b", a=2), psA[:, 1024:2048].rearrange("p (a b) -> p a b", a=2)]
    OA = psB[:, :].rearrange("p (a b) -> p a b", a=8)

    last = {}

    def dep(*names):
        out = []
        for n in names:
            out.extend(last.get(n, []))
        return out

    def setlast(name, *toks):
        last[name] = [t for t in toks if t is not None]

    def addlast(name, *toks):
        last.setdefault(name, []).extend([t for t in toks if t is not None])

    def phase_barrier():
        sch.barrier()
        last.clear()

    def MM(out, lhsT, rhs, start, stop, waits=(), signal=False):
        return sch.op("tensor", lambda e, o=out, l=lhsT, r=rhs, a=start, b=stop:
                      e.matmul(o, l, r, start=a, stop=b, skip_group_check=True), waits, signal)

    def ACT(out, in_, func, waits=(), accum_out=None, scale=None, bias=None):
        kw = {}
        if accum_out is not None:
            kw["accum_out"] = accum_out
        if scale is not None:
            kw["scale"] = scale
        if bias is not None:
            kw["bias"] = bias
        return sch.op("scalar", lambda e, o=out, i=in_, f=func, kw=kw: e.activation(out=o, in_=i, func=f, **kw), waits)

    def TT(eng, out, in0, in1, op, waits=()):
        return sch.op(eng, lambda e, o=out, a=in0, b=in1, p=op: e.tensor_tensor(out=o, in0=a, in1=b, op=p), waits)

    def TS(eng, out, in0, s1, s2, op0, op1=None, waits=()):
        if op1 is None:
            return sch.op(eng, lambda e, o=out, a=in0, x=s1, p=op0: e.tensor_scalar(out=o, in0=a, scalar1=x, scalar2=None, op0=p), waits)
        return sch.op(eng, lambda e, o=out, a=in0, x=s1, y=s2, p=op0, q=op1: e.tensor_scalar(out=o, in0=a, scalar1=x, scalar2=y, op0=p, op1=q), waits)

    def STT(eng, out, in0, scalar, in1, op0, op1, waits=()):
        return sch.op(eng, lambda e, o=out, a=in0, s=scalar, b=in1, p=op0, q=op1:
                      e.scalar_tensor_tensor(out=o, in0=a, scalar=s, in1=b, op0=p, op1=q), waits)

    def CP(eng, out, in_, waits=()):
        if eng == "scalar":
            return ACT(out, in_, AF.Copy, waits)
        return sch.op(eng, lambda e, o=out, i=in_: e.tensor_copy(out=o, in_=i), waits)

    dma_rr = [0]

    def dq():
        return "sync"

    t_id = sch.dma("sync", ident_f, ident_in[:, :], "c0")
    t_idc = ACT(ident, ident_f, AF.Copy, [t_id])
    t_eps = sch.op("vector", lambda e: e.memset(eps_sb, EPS))
    t_cf = sch.dma("sync", cf_sb, cfar[:, :], "c1")
    t_sel = sch.dma("sync", sel_sb, sel_in[:, :], "c2")
    t_hm = sch.dma("sync", hmask_sb, hmask_in[:, :], "c3")
    t_fg = sch.dma("sync", fg_sb, fg_rep[:, :], "c4")
    CONST = [t_idc, t_eps, t_cf, t_sel, t_hm, t_fg]

    final_toks = []

    def load_w(pieces):
        toks = []
        for idx, (src_ap, vfn, dst_ap) in enumerate(pieces):
            s = idx % 2
            td = sch.dma(dq(), vfn(wstage[s]), src_ap, f"w{s}", dep(f"wstage{s}"))
            tc_ = CP("vector" if idx % 2 == 0 else "scalar", dst_ap, vfn(wstage[s]), [td])
            setlast(f"wstage{s}", tc_)
            toks.append(tc_)
        return toks

    def w_in_pieces(layer, c_lo, c_hi):
        ps_ = []
        for k in range(8):
            for c0 in range(c_lo, c_hi, 2048):
                n = min(2048, c_hi - c0)
                ps_.append((w_in[layer, k * 128:(k + 1) * 128, c0:c0 + n], (lambda ws, n=n: ws[:, 0:n]),
                            w_in_sb[:, k, c0:c0 + n]))
        return ps_

    def w_qkv_pieces(j):
        return [(w_qkv[j, k * 128:(k + 1) * 128, :], (lambda ws: ws[:, 0:1536]), w_in_sb[:, k, 0:1536]) for k in range(8)]

    def w_out_pieces(layer):
        return [(w_out[layer, k * 128:(k + 2) * 128, :].rearrange("(a p) c -> p a c", p=128),
                 (lambda ws: ws[:, :].rearrange("p (a c) -> p a c", a=2)), w_out_sb[:, k:k + 2, :]) for k in range(0, 8, 2)]

    def issue_xloads(src_ap, tok0, par):
        toks = []
        for sub in range(4):
            nm = f"xt{par}_{sub}"
            w = dep(nm) + (dep("wstage0", "wstage1") if par == 1 else [])
            toks.append(sch.dma("sync", xt2[par][sub], src_ap[tok0 + sub * 128: tok0 + (sub + 1) * 128, :], f"x_{nm}", w))
        return toks

    def norm_tile(src_ap, row0, col, hslot, xb, name, nrows=128, mask=None, tl=None):
        if tl is None:
            tl = sch.dma(dq(), xb[0:nrows, :], src_ap[row0:row0 + nrows, :], f"x_{name}", dep(name))
        if mask is not None:
            tl = TS("vector", xb[0:nrows, :], xb[0:nrows, :], mask[0:nrows, 0:1], None, ALU.mult, None, [tl] + CONST)
        tsq = ACT(junk, xb, AF.Square, [tl] + CONST + dep(f"ss{col}", "junk"), accum_out=ss[:, col:col + 1])
        setlast("junk", tsq)
        tr1 = ACT(ss[:, col:col + 1], ss[:, col:col + 1], AF.Ln, [tsq], scale=1.0 / D, bias=eps_sb[:, 0:1])
        tr2 = ACT(ss[:, col:col + 1], ss[:, col:col + 1], AF.Exp, [tr1], scale=-0.5)
        th = STT("vector", hb[hslot], xb, ss[:, col:col + 1], g_sb, ALU.mult, ALU.mult,
                 [tr2] + dep(f"hb{hslot}", "g_sb"))
        setlast(f"ss{col}", th)
        setlast(name, th)
        return th

    def transpose_hb(th, hslot):
        w = [th] + dep("pT") + CONST
        tm = None
        for k in range(8):
            tm = MM(pT[:, k, :], hb[hslot][:, k * 128:(k + 1) * 128], ident, True, True,
                    w if k == 0 else [], signal=(k == 7))
        setlast(f"hb{hslot}", tm)
        return tm

    def load_g(layer):
        tg = sch.dma("sync", g_sb, g_rep[layer, :, :], "g", dep("g_sb"))
        setlast("g_sb", tg)

    def norm_to_hT(src, tok0, hs, par=0, xl=None):
        hw = []
        for sub in range(4):
            hslot = sub % 2
            th = norm_tile(src, tok0 + sub * 128, sub, hslot, xt2[par][sub], f"xt{par}_{sub}",
                           tl=(xl[sub] if xl is not None else None))
            tm = transpose_hb(th, hslot)
            tcp = CP("scalar" if sub % 2 == 0 else "vector", hT[hs][:, :, sub * 128:(sub + 1) * 128], pT,
                     [tm] + dep(f"hT{hs}"))
            setlast("pT", tcp)
            hw.append(tcp)
        return hw

    def outproj_residual(src, dst, tok0, yw, load_x, final, par=0, xl=None):
        tmm = None
        for sub in range(4):
            xs = sub % 2
            txl = xl[sub] if xl is not None else None
            for half in range(2):
                for j in range(8):
                    tmm = MM(pO[:, half * 512:(half + 1) * 512], yT[:, j, sub * 128:(sub + 1) * 128],
                             w_out_sb[:, j, half * 512:(half + 1) * 512], j == 0, j == 7,
                             (yw + dep("pO", "w_out_sb")) if (j == 0 and half == 0) else [],
                             signal=(j == 7 and half == 1))
            tad = TT("vector", xo[xs], pO, xt2[par][sub], ALU.add, [tmm, txl] + dep(f"xo{xs}"))
            setlast("pO", tad)
            setlast(f"xt{par}_{sub}", tad)
            if final:
                c_ = 4 + xs
                tsq = ACT(junk, xo[xs], AF.Square, [tad] + CONST + dep(f"ss{c_}", "junk"), accum_out=ss[:, c_:c_ + 1])
                setlast("junk", tsq)
                tr1 = ACT(ss[:, c_:c_ + 1], ss[:, c_:c_ + 1], AF.Ln, [tsq], scale=1.0 / D, bias=eps_sb[:, 0:1])
                tr2 = ACT(ss[:, c_:c_ + 1], ss[:, c_:c_ + 1], AF.Exp, [tr1], scale=-0.5)
                tad = STT("vector", xo[xs], xo[xs], ss[:, c_:c_ + 1], fg_sb, ALU.mult, ALU.mult, [tr2])
                setlast(f"ss{c_}", tad)
            tst = sch.dma(dq(), dst[tok0 + sub * 128: tok0 + (sub + 1) * 128, :], xo[xs], f"xo{xs}", [tad])
            setlast(f"xo{xs}", tst)
            final_toks.append(tst)
        setlast("yT", tmm)
        return tmm

    def conv_stage(layer, cj, src, dst, halo_ap, use_mask):
        load_g(layer)
        tcw = sch.dma("sync", cw_sb, conv_w[cj].rearrange("(j p) k -> p j k", p=128), "cw", dep("cw_sb"))
        setlast("cw_sb", tcw)
        wt = load_w(w_in_pieces(layer, 0, 4096) + w_out_pieces(layer))
        setlast("w_in_sb", *wt)
        setlast("w_out_sb", *wt)
        tz = sch.op("gpsimd", lambda e: e.memset(xh, 0.0), dep("xh"))
        setlast("xh", tz)
        th = norm_tile(halo_ap, 0, 4, 0, xh, "xh", nrows=2, mask=hmask_sb if use_mask else None)
        tm = transpose_hb(th, 0)
        tcp = ACT(hT_halo, pT, AF.Copy, [tm] + dep("hT_halo"))
        setlast("pT", tcp)
        for j in range(8):
            tk = {}
            for gi in (1, 2):
                for k in range(8):
                    tk[gi] = MM(pP[gi][:, 0:2], w_in_sb[:, k, gi * 1024 + j * 128: gi * 1024 + (j + 1) * 128],
                                hT_halo[:, k, 0:2], k == 0, k == 7,
                                ([tcp] + dep(f"pP{gi}", "w_in_sb")) if k == 0 else [], signal=(k == 7))
            tuc = ACT(ut[:, 0:2], pP[2][:, 0:2], AF.Copy, [tk[2]] + dep("ut"))
            tv = TT("vector", vbuf[j][:, 0:2], pP[1][:, 0:2], ut[:, 0:2], ALU.mult, [tuc, tk[1]] + dep(f"vbuf{j}"))
            setlast("ut", tv)
            setlast("pP1", tv)
            setlast("pP2", tuc)
            setlast(f"vbuf{j}", tv)
        setlast("hT_halo", tk[2])
        xl_next = issue_xloads(src, 0, 0)
        for T in range(NT):
            tok0 = T * 512
            hs = T % 2
            par = T % 2
            xl_cur = xl_next
            if T + 1 < NT:
                xl_next = issue_xloads(src, (T + 1) * 512, 1 - par)
            hw = norm_to_hT(src, tok0, hs, par, xl_cur)
            yw = []
            tmm = None
            for j in range(8):
                tg_ = {}
                for g_ in (1, 2, 3, 0):
                    for k in range(8):
                        tmm = MM(pP[g_], w_in_sb[:, k, g_ * 1024 + j * 128: g_ * 1024 + (j + 1) * 128],
                                 hT[hs][:, k, :], k == 0, k == 7,
                                 (hw + dep(f"pP{g_}", "w_in_sb")) if k == 0 else [], signal=(k == 7))
                    tg_[g_] = tmm
                tb, tc1, tu1, tz1 = tg_[0], tg_[1], tg_[2], tg_[3]
                tuc = ACT(ut, pP[2], AF.Copy, [tu1] + dep("ut"))
                setlast("pP2", tuc)
                tsz = ACT(sz, pP[3], AF.Silu, [tz1] + dep("sz"))
                setlast("pP3", tsz)
                tv = TT("vector", vbuf[j][:, 2:514], pP[1], ut, ALU.mult, [tuc, tc1] + dep(f"vbuf{j}"))
                setlast("pP1", tv)
                setlast("ut", tv)
                ta = ACT(t1, vbuf[j][:, 0:512], AF.Copy, [tv] + dep("t1", "cw_sb"), scale=cw_sb[:, j, 0:1])
                tb_ = STT("vector", t1, vbuf[j][:, 1:513], cw_sb[:, j, 1:2], t1, ALU.mult, ALU.add, [ta])
                tc_ = STT("vector", t1, vbuf[j][:, 2:514], cw_sb[:, j, 2:3], t1, ALU.mult, ALU.add, [tb_])
                tcar = CP("vector", vbuf[j][:, 0:2], vbuf[j][:, 512:514], [tc_])
                setlast(f"vbuf{j}", tcar)
                td_ = TT("vector", t2, pP[0], t1, ALU.mult, [tc_, tb] + dep("t2"))
                setlast("pP0", td_)
                setlast("t1", td_)
                te_ = TT("vector", yT[:, j, :], t2, sz, ALU.mult, [td_, tsz] + dep("yT"))
                setlast("sz", te_)
                setlast("t2", te_)
                yw.append(te_)
            setlast(f"hT{hs}", tmm)
            outproj_residual(src, dst, tok0, yw, load_x=False, final=False, par=par)

    def pz_stage(layer, src):
        load_g(layer)
        wt = load_w(w_in_pieces(layer, 3072, 4096))
        setlast("w_in_sb", *wt)
        ring = 0
        xl_next = issue_xloads(src, 0, 0)
        for T in range(NT):
            tok0 = T * 512
            hs = T % 2
            par = T % 2
            xl_cur = xl_next
            if T + 1 < NT:
                xl_next = issue_xloads(src, (T + 1) * 512, 1 - par)
            hw = norm_to_hT(src, tok0, hs, par, xl_cur)
            tsh = sch.dma(dq(), HTown3[:, :, tok0:tok0 + 512].rearrange("k p t -> p k t"), hT[hs], f"hst{hs}", hw)
            final_toks.append(tsh)
            tmm = None
            for j in range(8):
                b_ = ring % 4
                ring += 1
                for k in range(8):
                    tmm = MM(pP[b_], w_in_sb[:, k, 3072 + j * 128: 3072 + (j + 1) * 128], hT[hs][:, k, :], k == 0, k == 7,
                             (hw + dep(f"pP{b_}", "w_in_sb")) if k == 0 else [], signal=(k == 7))
                tev = ACT(ev[b_], pP[b_], AF.Silu, [tmm] + dep(f"ev{b_}"))
                setlast(f"pP{b_}", tev)
                tst = sch.dma(dq(), ZTd[j, :, tok0:tok0 + 512], ev[b_], f"ev{b_}", [tev])
                setlast(f"ev{b_}", tst)
                final_toks.append(tst)
            setlast(f"hT{hs}", tmm, tsh)

    def hproj_stage(j_attn, preloaded=False):
        if not preloaded:
            wt = load_w(w_qkv_pieces(j_attn))
            setlast("w_in_sb", *wt)
        ring = 0
        for T in range(16):
            rk, lt = T // 8, T % 8
            hs = T % 2
            g0 = T * 512
            hw = []
            for pc in range(4):
                tl = sch.dma(dq(), hT[hs][:, 2 * pc:2 * pc + 2, :],
                             HTall5[pc, rk, :, :, lt * 512:(lt + 1) * 512].rearrange("kk p t -> p kk t"),
                             f"hld{hs}", dep(f"hT{hs}"))
                hw.append(tl)
            tmm = None
            for jj in range(8):
                b_ = ring % 4
                ring += 1
                for k in range(8):
                    tmm = MM(pP[b_], w_in_sb[:, k, jj * 128:(jj + 1) * 128], hT[hs][:, k, :], k == 0, k == 7,
                             (hw + dep(f"pP{b_}", "w_in_sb")) if k == 0 else [], signal=(k == 7))
                tev = CP("scalar" if jj % 2 == 0 else "vector", ev[b_], pP[b_], [tmm] + dep(f"ev{b_}"))
                setlast(f"pP{b_}", tev)
                dstT = QTa[jj, :, g0:g0 + 512] if jj < 4 else KTa[jj - 4, :, g0:g0 + 512]
                tst = sch.dma(dq(), dstT, ev[b_], f"ev{b_}", [tev])
                setlast(f"ev{b_}", tst)
                final_toks.append(tst)
            for sub in range(4):
                b_ = ring % 4
                ring += 1
                for k in range(8):
                    tmm = MM(pP[b_], hT[hs][:, k, sub * 128:(sub + 1) * 128], w_in_sb[:, k, 1024:1536], k == 0, k == 7,
                             (hw + dep(f"pP{b_}", "w_in_sb")) if k == 0 else [], signal=(k == 7))
                tev = CP("scalar" if sub % 2 == 0 else "vector", ev[b_], pP[b_], [tmm] + dep(f"ev{b_}"))
                setlast(f"pP{b_}", tev)
                tst = sch.dma(dq(), Va[:, g0 + sub * 128:g0 + (sub + 1) * 128, :].rearrange("h t e -> t h e"),
                              ev[b_][:, :].rearrange("p (h e) -> p h e", h=4), f"ev{b_}", [tev])
                setlast(f"ev{b_}", tst)
                final_toks.append(tst)
            setlast(f"hT{hs}", tmm)

    def aout_stage(layer, src, dst, final, preloaded=False):
        if not preloaded:
            wt = load_w(w_out_pieces(layer))
            setlast("w_out_sb", *wt)
        xl_next = issue_xloads(src, 0, 0)
        for T in range(NT):
            tok0 = T * 512
            par = T % 2
            xl_cur = xl_next
            if T + 1 < NT:
                xl_next = issue_xloads(src, (T + 1) * 512, 1 - par)
            tl1 = sch.dma("sync", onl, ONTr3[:, :, tok0:tok0 + 512].rearrange("j p t -> p j t"), "onl", dep("onl"))
            tl2 = sch.dma("sync", ztl, ZTd[:, :, tok0:tok0 + 512].rearrange("j p t -> p j t"), "ztl", dep("ztl"))
            ty = TT("vector", yT, onl, ztl, ALU.mult, [tl1, tl2] + dep("yT"))
            setlast("onl", ty)
            setlast("ztl", ty)
            outproj_residual(src, dst, tok0, [ty], load_x=True, final=final, par=par, xl=xl_cur)

    def attn_stage(j_attn, lam_init):
        t_lq = sch.dma("sync", lq_sb, lqk[j_attn], "lq")
        t_gs = sch.dma("sync", gs_sb, gsub_in[j_attn], "gs")
        t_gs2 = TS("vector", gs_sb, gs_sb, 1.0 - lam_init, None, ALU.mult, None, [t_gs])
        t_p = TT("vector", lprod, lq_sb[:, 0:4:2, :], lq_sb[:, 1:4:2, :], ALU.mult, [t_lq])
        t_s1 = ACT(junkA[:, 0:64], lprod[:, 0, :], AF.Copy, [t_p] + dep("junkA"), accum_out=lsum[:, 0:1])
        t_s2 = ACT(junkA[:, 0:64], lprod[:, 1, :], AF.Copy, [t_s1], accum_out=lsum[:, 1:2])
        t_e = ACT(lsum[:, 2:4], lsum[:, 0:2], AF.Exp, [t_s2])
        t_l1 = TT("vector", neglam, lsum[:, 3:4], lsum[:, 2:3], ALU.subtract, [t_e])
        t_l2 = TS("vector", neglam, neglam, -lam_init, None, ALU.add, None, [t_l1])
        setlast("junkA", t_s2)
        for i in range(2):
            tv1 = sch.op("gpsimd", lambda e, i=i: e.memset(VS[i][:, :, 128:130], 1.0))
            setlast(f"VS{i}", tv1)

        def load_head(h):
            s = h % 2
            toks = []
            toks.append(sch.dma("sync", QT[s][:, 0:S // 2], QTa[h, :, 0:S // 2], f"hd{s}", dep(f"QT{s}")))
            toks.append(sch.dma("sync", QT[s][:, S // 2:S], QTa[h, :, S // 2:S], f"hd{s}", dep(f"QT{s}")))
            toks.append(sch.dma("sync", KT[s][:, 0:S // 2], KTa[h, :, 0:S // 2], f"hd{s}", dep(f"KT{s}")))
            toks.append(sch.dma("sync", KT[s][:, S // 2:S], KTa[h, :, S // 2:S], f"hd{s}", dep(f"KT{s}")))
            vv = Va[h].rearrange("(c p) e -> p c e", p=128)
            for qd in range(4):
                toks.append(sch.dma("sync", VS[s][:, qd * 16:(qd + 1) * 16, 0:128],
                                    vv[:, qd * 16:(qd + 1) * 16, :], f"hd{s}", dep(f"VS{s}")))
            tz = sch.dma("sync", ZC[s], Zb[2 * h:2 * h + 2].rearrange("m p w -> p m w"), f"hz{s}", dep(f"ZC{s}"))
            tzc = None
            for m in range(2):
                tzc = TS("vector", ZC[s][:, m, :], ZC[s][:, m, :], cf_sb[:, 2 * h + m:2 * h + m + 1], None, ALU.subtract,
                         None, [tz] + CONST)
            toks.append(tzc)
            return toks

        sring = [0]
        pring = [0]
        bring = [0]
        pending_epi = [None]

        def emit_epilogue_pe(info):
            h, j, es, ton = info
            sl = sring[0] % 2
            sring[0] += 1
            w0 = dep(f"SP{sl}") + CONST
            tms = []
            for s4 in range(4):
                tm = MM(SP[sl][:, 0, s4 * 128:(s4 + 1) * 128], onb[s4], ident, True, True,
                        (w0 + [ton[s4]]) if s4 == 0 else [ton[s4]], signal=True)
                tms.append(tm)
                setlast(f"onb{s4}", tm)
            os_ = j % 2
            tca = TS("vector", onTa[os_], SP[sl][:, 0, :], sel_sb[:, 0:1], None, ALU.mult, None,
                     [tms[-1]] + dep(f"onTa{os_}"))
            tcb = TS("vector", onTb[os_], SP[sl][:, 0, :], sel_sb[:, 1:2], None, ALU.mult, None,
                     [tca] + dep(f"onTb{os_}"))
            setlast(f"SP{sl}", tca, tcb)
            d_, c0 = j // 8, (j % 8) * 512
            tsa = sch.dma("sync", RS5[0, d_, h, :, c0:c0 + 512], onTa[os_], f"onta{os_}", [tca])
            tsb = sch.dma("sync", RS5[1, d_, h, :, c0:c0 + 512], onTb[os_], f"ontb{os_}", [tcb])
            setlast(f"onTa{os_}", tsa)
            setlast(f"onTb{os_}", tsb)
            final_toks.append(tsa)
            final_toks.append(tsb)

        head_toks = {0: load_head(0)}
        unit_idx = 0
        for h in range(4):
            s = h % 2
            if h + 1 < 4:
                head_toks[h + 1] = load_head(h + 1)
            hw = head_toks[h]
            last_pe_read = None
            for j in range(16):
                nchunks = 4 * j + 4
                q0 = j * 512

                def issue_qk(c):
                    i = c - 4 * j
                    lo = 128 * i if i > 0 else 0
                    sl = sring[0] % 2
                    sring[0] += 1
                    w = dep(f"SP{sl}") + (hw if c == 0 else [])
                    tq = None
                    for m in range(2):
                        tq = MM(SP[sl][:, m, lo:512], KT[s][m * 64:(m + 1) * 64, c * 128:(c + 1) * 128],
                                QT[s][m * 64:(m + 1) * 64, q0 + lo:q0 + 512], True, True, w if m == 0 else [],
                                signal=(m == 1))
                    pl = pring[0] % NP
                    pring[0] += 1
                    if i >= -1:
                        o_idx = 3 - i
                        bl = bring[0] % 2
                        bring[0] += 1
                        tb = STT("vector", SSB[bl][:, :, lo:512], SP[sl][:, :, lo:512], 0.125,
                                 ZC[s][:, :, o_idx * 128 + lo:o_idx * 128 + 512], ALU.mult, ALU.add,
                                 [tq] + dep(f"SSB{bl}") + hw)
                        setlast(f"SP{sl}", tb)
                        te = ACT(PT[pl][:, :, lo:512], SSB[bl][:, :, lo:512], AF.Exp, [tb] + dep(f"PT{pl}"))
                        setlast(f"SSB{bl}", te)
                    else:
                        te = ACT(PT[pl][:, :, lo:512], SP[sl][:, :, lo:512], AF.Exp, [tq] + dep(f"PT{pl}"), scale=0.125)
                        setlast(f"SP{sl}", te)
                    return (c, i, pl, te)

                def issue_av(tile):
                    c, i, pl, te = tile
                    tm = None
                    first = True
                    for s4 in range(4):
                        if s4 < i:
                            continue
                        lastc = 4 * j + s4
                        for m in range(2):
                            a = s4 * 2 + m
                            w = []
                            if first:
                                w = [te] + (dep("OA") if c == 0 else [])
                                first = False
                            tm = MM(OA[:, a, 0:129], PT[pl][:, m, s4 * 128:(s4 + 1) * 128], VS[s][:, c, 0:129],
                                    (c == 0 and m == 0), (c == lastc), w, signal=(s4 == 3 and m == 1))
                    setlast(f"PT{pl}", tm)
                    return tm

                LAG = 2
                pendq = []
                tav = None
                for c in range(nchunks):
                    pendq.append(issue_qk(c))
                    if len(pendq) > LAG:
                        tav = issue_av(pendq.pop(0))
                    if c == min(6, nchunks - 1) and pending_epi[0] is not None:
                        emit_epilogue_pe(pending_epi[0])
                        pending_epi[0] = None
                while pendq:
                    tav = issue_av(pendq.pop(0))
                last_pe_read = tav
                es = unit_idx % 2
                unit_idx += 1
                tev = CP("vector", OEV[es], OA[:, :, 0:129], [tav] + dep(f"OEV{es}"))
                setlast("OA", tev)
                trr = sch.op("vector", lambda e, es=es: e.reciprocal(out=rr[:, :, :].rearrange("p s m -> p (s m)"),
                                                                     in_=OEV[es][:, :, 128:129].rearrange("p a o -> p (a o)")),
                             [tev] + dep("rr"))
                tr2 = TS("vector", r2, rr[:, :, 1], neglam[:, 0:1], None, ALU.mult, None, [trr, t_l2] + dep("r2"))
                ton = []
                tlast = None
                tos = []
                tqs = []
                for s4 in range(4):
                    ta = TS("gpsimd", tA[s4], OEV[es][:, 2 * s4 + 1, 0:128], r2[:, s4:s4 + 1], None, ALU.mult, None,
                            [tr2] + dep(f"tA{s4}"))
                    to = STT("vector", oo[s4], OEV[es][:, 2 * s4, 0:128], rr[:, s4, 0:1], tA[s4], ALU.mult, ALU.add,
                             [ta, trr] + dep(f"oo{s4}"))
                    setlast(f"tA{s4}", to)
                    tq_ = ACT(junkA, oo[s4], AF.Square, [to] + CONST + dep("q1", "junkA"), accum_out=q1[:, s4:s4 + 1])
                    setlast("junkA", tq_)
                    tos.append(to)
                    tqs.append(tq_)
                tl_ = ACT(q1[:, 0:4], q1[:, 0:4], AF.Ln, tqs, scale=1.0 / 128, bias=eps_sb[:, 0:1])
                tx_ = ACT(q1[:, 0:4], q1[:, 0:4], AF.Exp, [tl_], scale=-0.5)
                for s4 in range(4):
                    tn_ = STT("vector", onb[s4], oo[s4], q1[:, s4:s4 + 1], gs_sb, ALU.mult, ALU.mult,
                              [tx_, t_gs2] + dep(f"onb{s4}"))
                    setlast(f"oo{s4}", tn_)
                    ton.append(tn_)
                    tlast = tn_
                setlast("q1", tlast)
                setlast("rr", tlast)
                setlast("r2", tlast)
                setlast(f"OEV{es}", tlast)
                if pending_epi[0] is not None:
                    emit_epilogue_pe(pending_epi[0])
                pending_epi[0] = (h, j, es, ton)
            for nm in (f"QT{s}", f"KT{s}", f"VS{s}"):
                addlast(nm, last_pe_read)
            setlast(f"ZC{s}", ("e_vector", sch.cnt.get("e_vector", 0)))
        if pending_epi[0] is not None:
            emit_epilogue_pe(pending_epi[0])

    def collective(kind, op, src2d, dst2d, name):
        return sch.coll(lambda e, k=kind, o=op, a=src2d, b=dst2d: e.collective_compute(
            k, o, replica_groups=GROUPS, ins=[a], outs=[b]), name)

    xs_ = [x0, X1, X2, X3]
    phase_no = [0]

    def want():
        phase_no[0] += 1
        return phase_no[0] <= upto

    for blk in range(2):
        l_conv, l_attn = 2 * blk, 2 * blk + 1
        src = xs_[l_conv]
        mid = xs_[l_conv + 1]
        dst = xs_[l_conv + 2] if blk == 0 else out_ap
        if want():
            if blk == 0:
                conv_stage(l_conv, blk, src, mid, xhalo, use_mask=False)
            else:
                conv_stage(l_conv, blk, src, mid, TAILall[0:2, :], use_mask=True)
            phase_barrier()
        if want():
            pz_stage(l_attn, mid)
            phase_barrier()
        if want():
            for pc in range(4):
                collective("AllGather", ALU.bypass, HTown[pc * 256:(pc + 1) * 256, :], HTall[pc * 512:(pc + 1) * 512, :], f"ag{pc}")
            load_w(w_qkv_pieces(blk))
            phase_barrier()
        if want():
            hproj_stage(blk, preloaded=True)
            phase_barrier()
        if want():
            attn_stage(blk, lam_inits[blk])
            phase_barrier()
        if want():
            for pc in range(2):
                collective("ReduceScatter", ALU.add, RSbuf[pc * 1024:(pc + 1) * 1024, :], ONTr[pc * 512:(pc + 1) * 512, :], f"rs{pc}")
            load_w(w_out_pieces(l_attn))
            phase_barrier()
        if want():
            aout_stage(l_attn, mid, dst, final=(blk == 1), preloaded=True)
            phase_barrier()
        if blk == 0 and want():
            ttail = sch.dma("sync", TAILown[:, :], X2[TOK - 2:TOK, :], "tail")
            final_toks.append(ttail)
            phase_barrier()
            collective("AllGather", ALU.bypass, TAILown, TAILall, "agt")
            phase_barrier()
    if debug_out:
        dbg = nc.dram_tensor("dbg", [TOK, D], F32, kind="ExternalOutput").ap()
        for q_ in range(4):
            td_ = sch.dma("sync", dbg[q_ * 1024:(q_ + 1) * 1024, :], debug_out[q_ * 1024:(q_ + 1) * 1024, :], "dbg")
            final_toks.append(td_)

    mx = {}
    for (k, v) in final_toks:
        mx[k] = max(mx.get(k, 0), v)
    run_sched(nc, sch, list(mx.items()))
    est.close()
    return nc


BF = ml_dtypes.bfloat16
NUM_BUCKETS, MAX_EXACT, REL_MAX = 32, 16, 128
MASKV = -30000.0

def bucket_table(n):
    d = np.arange(n, dtype=np.int64)
    ds = np.maximum(d, 1).astype(np.float32)
    large = MAX_EXACT + (np.log(ds / np.float32(MAX_EXACT)) / np.float32(math.log(REL_MAX / MAX_EXACT))
                         * np.float32(NUM_BUCKETS - MAX_EXACT)).astype(np.int32)
    large = np.minimum(large, NUM_BUCKETS - 1)
    return np.where(d < MAX_EXACT, d, large).astype(np.int64)

def common_inputs(inp):
    rep = lambda a: np.ascontiguousarray(np.broadcast_to(a[..., None, :], a.shape[:-1] + (128, a.shape[-1]))).astype(np.float32)
    return {
        "w_in": np.ascontiguousarray(inp["w_in"], dtype=np.float32),
        "w_out": np.ascontiguousarray(inp["w_out"], dtype=np.float32),
        "norm_g_rep": rep(np.asarray(inp["norm_g"])),
        "final_g_rep": rep(np.asarray(inp["final_g"])),
        "conv_w": np.ascontiguousarray(inp["conv_w"], dtype=np.float32),
        "ident": np.eye(128, dtype=np.float32),
    }

def attn_consts(inp, layer_j, core_r):
    rb = np.asarray(inp["rel_bias"], dtype=np.float32)
    bk = bucket_table(1024)
    maps = np.arange(8) + 8 * core_r
    p = np.arange(128)[:, None]
    w = np.arange(1024)[None, :]
    d = w - 384 - p
    dd = np.clip(d, 0, 1023)
    Zb = np.empty((8, 128, 1024), np.float32)
    for mi, m in enumerate(maps):
        vals = rb[bk[dd], m]
        Zb[mi] = np.where(d >= 0, vals, np.float32(MASKV))
    cf = np.ascontiguousarray(np.broadcast_to(rb[31, maps][None, :], (128, 8))).astype(np.float32)
    lqk = np.stack([np.asarray(inp[k])[layer_j] for k in ("lambda_q1", "lambda_k1", "lambda_q2", "lambda_k2")], 0)
    lqk = np.ascontiguousarray(np.broadcast_to(lqk[None], (128, 4, 64))).astype(np.float32)
    gs = np.ascontiguousarray(np.broadcast_to(np.asarray(inp["subln_g"])[layer_j][None, :], (128, 128))).astype(np.float32)
    return {"Zb": Zb, "cfar": cf, "lqk": lqk, "gsub": gs, "ident": np.eye(128, dtype=np.float32)}


def fused_inputs(inp):
    com = common_inputs(inp)
    x = inp["x"]
    w_in = inp["w_in"]
    lqk = np.stack([np.stack([np.asarray(inp[k])[j] for k in ("lambda_q1", "lambda_k1", "lambda_q2", "lambda_k2")], 0)
                    for j in range(2)], 0)
    lqk = np.ascontiguousarray(np.broadcast_to(lqk[:, None], (2, 128, 4, 64))).astype(np.float32)
    gs = np.ascontiguousarray(np.broadcast_to(np.asarray(inp["subln_g"])[:, None, :], (2, 128, 128))).astype(np.float32)
    maps = []
    for c in range(8):
        b, r = c // 2, c % 2
        m = dict(com)
        m["x0"] = np.ascontiguousarray(x[b, r * TOK:(r + 1) * TOK])
        m["xhalo"] = np.zeros((2, D), np.float32) if r == 0 else np.ascontiguousarray(x[b, TOK - 2:TOK])
        wq = []
        for j in range(2):
            l = 2 * j + 1
            cols = [w_in[l][:, base + r * 512: base + (r + 1) * 512] for base in (0, 1024, 2048)]
            wq.append(np.concatenate(cols, axis=1))
        m["w_qkv"] = np.ascontiguousarray(np.stack(wq, 0)).astype(np.float32)
        ac = attn_consts(inp, 0, r)
        m["Zb"] = ac["Zb"]
        m["cfar"] = ac["cfar"]
        m["lqk"] = lqk
        m["gsub"] = gs
        sel = np.zeros((128, 2), np.float32)
        sel[:, r] = 1.0
        m["sel"] = sel
        m["hmask"] = np.full((128, 1), float(r), np.float32)
        maps.append(m)
    return maps


_NC = {}


def _lam_init(layer_idx):
    return 0.8 - 0.6 * math.exp(-0.3 * layer_idx)


def kernel(x, norm_g, w_in, w_out, conv_w, lambda_q1, lambda_k1, lambda_q2, lambda_k2,
           subln_g, rel_bias, final_g):
    inp = {"x": np.asarray(x, np.float32), "norm_g": np.asarray(norm_g, np.float32),
           "w_in": np.asarray(w_in, np.float32), "w_out": np.asarray(w_out, np.float32),
           "conv_w": np.asarray(conv_w, np.float32),
           "lambda_q1": np.asarray(lambda_q1, np.float32), "lambda_k1": np.asarray(lambda_k1, np.float32),
           "lambda_q2": np.asarray(lambda_q2, np.float32), "lambda_k2": np.asarray(lambda_k2, np.float32),
           "subln_g": np.asarray(subln_g, np.float32), "rel_bias": np.asarray(rel_bias, np.float32),
           "final_g": np.asarray(final_g, np.float32)}
    if "nc" not in _NC:
        _NC["nc"] = build_fused((_lam_init(1), _lam_init(3)))
    maps = fused_inputs(inp)
    res = run_bass_kernel_spmd(_NC["nc"], maps, core_ids=list(range(8)))
    out = np.empty((4, S, D), np.float32)
    for c in range(8):
        out[c // 2, (c % 2) * TOK:(c % 2 + 1) * TOK] = res.results[c]["out"]
    return out
```

```python
import math
import numpy as np
import ml_dtypes
import concourse.bass as bass
import concourse.mybir as mybir
from concourse.bass_utils import run_bass_kernel_spmd

F32 = mybir.dt.float32
BF16 = mybir.dt.bfloat16
AF = mybir.ActivationFunctionType
ALU = mybir.AluOpType

D = 1024
S = 8192
TOK = 4096
NT = TOK // 512
EPS = 1e-6
MASKV = -30000.0
ENGS = ("tensor", "vector", "scalar", "gpsimd", "sync")


class Sched:
    def __init__(self):
        self.ops = {e: [] for e in ENGS}
        self.cnt = {}
        self.sems = {}
        self.bar = []

    def op(self, eng, fn, waits=(), signal=True):
        key = "e_" + eng
        tok = None
        if signal:
            self.cnt[key] = self.cnt.get(key, 0) + 1
            tok = (key, self.cnt[key])
        self.ops[eng].append((fn, self._dd(tuple(waits) + tuple(self.bar)), key if signal else None, 1))
        return tok

    @staticmethod
    def _dd(waits):
        mx = {}
        for w in waits:
            if w is None:
                continue
            k, v = w
            if mx.get(k, 0) < v:
                mx[k] = v
        return tuple(mx.items())

    def dma(self, eng, out, in_, stream, waits=()):
        key = "d_" + stream
        self.cnt[key] = self.cnt.get(key, 0) + 16
        tok = (key, self.cnt[key])
        self.ops[eng].append((lambda e, o=out, i=in_: e.dma_start(out=o, in_=i),
                              self._dd(tuple(waits) + tuple(self.bar)), key, 16))
        return tok

    def coll(self, fn, name, waits=()):
        key = "c_" + name
        self.cnt[key] = self.cnt.get(key, 0) + 1
        tok = (key, self.cnt[key])
        self.ops["gpsimd"].append((fn, self._dd(tuple(waits) + tuple(self.bar)), key, 1))
        return tok

    def barrier(self):
        self.bar = [(k, v) for k, v in self.cnt.items()]

    def sem_keys(self):
        return sorted(self.cnt.keys())

    def replay(self, eng_name, eng):
        waited = {}
        for fn, waits, key, inc in self.ops[eng_name]:
            for (k, v) in waits:
                if waited.get(k, 0) < v:
                    eng.wait_ge(self.sems[k], v)
                    waited[k] = v
            ins = fn(eng)
            if key is not None:
                ins.then_inc(self.sems[key], inc)

    def final_waits(self, eng_name, eng, toks):
        for (k, v) in toks:
            eng.wait_ge(self.sems[k], v)


def run_sched(nc, sch, final_toks):
    from contextlib import ExitStack
    with ExitStack() as st:
        for k in sch.sem_keys():
            sch.sems[k] = st.enter_context(nc.semaphore(k))
        block = st.enter_context(nc.Block())

        @block.sync
        def _(e):
            sch.replay("sync", e)
            sch.final_waits("sync", e, final_toks)

        @block.tensor
        def _(e):
            sch.replay("tensor", e)

        @block.vector
        def _(e):
            sch.replay("vector", e)

        @block.scalar
        def _(e):
            sch.replay("scalar", e)

        @block.gpsimd
        def _(e):
            sch.replay("gpsimd", e)


def build_fused(lam_inits=(0.0, 0.0), debug_out=False, upto=99):
    from contextlib import ExitStack
    nc = bass.Bass("TRN2", target_bir_lowering=False)

    def ext_in(name, shape, dt=F32):
        return nc.dram_tensor(name, list(shape), dt, kind="ExternalInput").ap()

    def internal(name, shape, dt):
        return nc.dram_tensor(name, list(shape), dt).ap()

    x0 = ext_in("x0", [TOK, D])
    xhalo = ext_in("xhalo", [2, D])
    w_in = ext_in("w_in", [4, D, 4 * D])
    w_out = ext_in("w_out", [4, D, D])
    w_qkv = ext_in("w_qkv", [2, D, 1536])
    g_rep = ext_in("norm_g_rep", [4, 128, D])
    fg_rep = ext_in("final_g_rep", [128, D])
    conv_w = ext_in("conv_w", [2, D, 3])
    ident_in = ext_in("ident", [128, 128])
    Zb = ext_in("Zb", [8, 128, 1024])
    cfar = ext_in("cfar", [128, 8])
    lqk = ext_in("lqk", [2, 128, 4, 64])
    gsub_in = ext_in("gsub", [2, 128, 128])
    sel_in = ext_in("sel", [128, 2])
    hmask_in = ext_in("hmask", [128, 1])
    out_ap = nc.dram_tensor("out", [TOK, D], F32, kind="ExternalOutput").ap()

    X1 = internal("X1", [TOK, D], F32)
    X2 = internal("X2", [TOK, D], F32)
    X3 = internal("X3", [TOK, D], F32)
    HTown = internal("HTown", [8 * 128, TOK], BF16)
    HTall = internal("HTall", [2 * 8 * 128, TOK], BF16)
    ZTd = internal("ZTd", [8, 128, TOK], BF16)
    QTa = internal("QTa", [4, 128, S], BF16)
    KTa = internal("KTa", [4, 128, S], BF16)
    Va = internal("Va", [4, S, 128], BF16)
    RSbuf = internal("RSbuf", [2 * 8 * 128, TOK], BF16)
    ONTr = internal("ONTr", [8 * 128, TOK], BF16)
    TAILown = internal("TAILown", [2, D], F32)
    TAILall = internal("TAILall", [4, D], F32)
    HTown3 = HTown.rearrange("(k p) t -> k p t", p=128)
    HTall5 = HTall.rearrange("(pc r kk p) t -> pc r kk p t", pc=4, r=2, kk=2, p=128)
    RS5 = RSbuf.rearrange("(pc d hh p) t -> pc d hh p t", pc=2, d=2, hh=4, p=128)
    ONTr3 = ONTr.rearrange("(h p) t -> h p t", p=128)
    GROUPS = [[0, 1], [2, 3], [4, 5], [6, 7]]

    sch = Sched()
    est = ExitStack()
    ARENA_BYTES = 206848
    arena = est.enter_context(nc.sbuf_tensor("arena", [128, ARENA_BYTES // 2], BF16))
    psA = est.enter_context(nc.psum_tensor("psA", [128, 2048], F32))
    psB = est.enter_context(nc.psum_tensor("psB", [128, 2048], F32))

    def view(off, shape, dt):
        n = 1
        for s_ in shape[1:]:
            n *= s_
        nb = n * (4 if dt == F32 else 2)
        assert off % 4 == 0 and off + nb <= ARENA_BYTES, (off, nb)
        a = arena[:, off // 2:(off + nb) // 2]
        if dt == F32:
            a = a.bitcast(F32)
        if len(shape) == 3:
            a = a.rearrange("p (a b) -> p a b", a=shape[1])
        return a

    class Alloc:
        def __init__(self, base):
            self.off = base

        def __call__(self, shape, dt):
            n = 1
            for s_ in shape[1:]:
                n *= s_
            nb = n * (4 if dt == F32 else 2)
            nb_al = (nb + 31) // 32 * 32
            v = view(self.off, shape, dt)
            self.off += nb_al
            return v

    ta_ = Alloc(0)
    wstage = [ta_([128, 2048], F32) for _ in range(2)]
    w_in_sb = ta_([128, 8, 4096], BF16)
    w_out_sb = ta_([128, 8, 1024], BF16)
    xtA = [ta_([128, D], F32) for _ in range(4)]
    xtB = [wstage[0][:, 0:1024], wstage[0][:, 1024:2048], wstage[1][:, 0:1024], wstage[1][:, 1024:2048]]
    xt2 = [xtA, xtB]
    xt = xtA
    junk = ta_([128, D], BF16)
    hb = [ta_([128, D], BF16) for _ in range(2)]
    hT = [ta_([128, 8, 512], BF16) for _ in range(2)]
    hT_halo = ta_([128, 8, 128], BF16)
    xh = ta_([128, D], F32)
    ut = ta_([128, 512], F32)
    sz = ta_([128, 512], F32)
    vbuf = [ta_([128, 514], F32) for _ in range(8)]
    t1 = ta_([128, 512], F32)
    t2 = ta_([128, 512], F32)
    yT = ta_([128, 8, 512], BF16)
    ev = [ta_([128, 512], BF16) for _ in range(4)]
    xo = [ta_([128, D], F32) for _ in range(2)]
    tok_end = ta_.off
    onl, ztl = hT[0], hT[1]
    aa_ = Alloc(0)
    QT = [aa_([128, S], BF16) for _ in range(2)]
    KT = [aa_([128, S], BF16) for _ in range(2)]
    VS = [aa_([128, 64, 130], BF16) for _ in range(2)]
    ZC = [aa_([128, 2, 1024], F32) for _ in range(2)]
    NP = 8
    PT = [aa_([128, 2, 512], BF16) for _ in range(NP)]
    SSB = [aa_([128, 2, 512], F32) for _ in range(2)]
    OEV = [aa_([128, 8, 129], F32) for _ in range(2)]
    tA = [aa_([128, 128], F32) for _ in range(4)]
    oo = [aa_([128, 128], F32) for _ in range(4)]
    junkA = aa_([128, 128], F32)
    onb = [aa_([128, 128], BF16) for _ in range(4)]
    onTa = [aa_([128, 512], BF16) for _ in range(2)]
    onTb = [aa_([128, 512], BF16) for _ in range(2)]
    att_end = aa_.off
    pa_ = Alloc(max(tok_end, att_end))
    g_sb = pa_([128, D], F32)
    fg_sb = pa_([128, D], F32)
    ident_f = pa_([128, 128], F32)
    ident = pa_([128, 128], BF16)
    cw_sb = pa_([128, 8, 3], F32)
    ss = pa_([128, 8], F32)
    eps_sb = pa_([128, 1], F32)
    cf_sb = pa_([128, 8], F32)
    lq_sb = pa_([128, 4, 64], F32)
    lprod = pa_([128, 2, 64], F32)
    lsum = pa_([128, 4], F32)
    neglam = pa_([128, 1], F32)
    gs_sb = pa_([128, 128], F32)
    rr = pa_([128, 4, 2], F32)
    r2 = pa_([128, 4], F32)
    q1 = pa_([128, 4], F32)
    sel_sb = pa_([128, 2], F32)
    hmask_sb = pa_([128, 1], F32)
    assert pa_.off <= ARENA_BYTES, pa_.off
    pT = psA[:, 0:1024].rearrange("p (a b) -> p a b", a=8)
    pO = psA[:, 1024:2048]
    pP = [psB[:, i * 512:(i + 1) * 512] for i in range(4)]
    SP = [psA[:, 0:1024].rearrange("p (a b) -> p a b", a=2), psA[:, 1024:2048].rearrange("p (a b) -> p a b", a=2)]
    OA = psB[:, :].rearrange("p (a b) -> p a b", a=8)

    last = {}

    def dep(*names):
        out = []
        for n in names:
            out.extend(last.get(n, []))
        return out

    def setlast(name, *toks):
        last[name] = [t for t in toks if t is not None]

    def addlast(name, *toks):
        last.setdefault(name, []).extend([t for t in toks if t is not None])

    def phase_barrier():
        sch.barrier()
        last.clear()

    def MM(out, lhsT, rhs, start, stop, waits=(), signal=False):
        return sch.op("tensor", lambda e, o=out, l=lhsT, r=rhs, a=start, b=stop:
                      e.matmul(o, l, r, start=a, stop=b, skip_group_check=True), waits, signal)

    def ACT(out, in_, func, waits=(), accum_out=None, scale=None, bias=None):
        kw = {}
        if accum_out is not None:
            kw["accum_out"] = accum_out
        if scale is not None:
            kw["scale"] = scale
        if bias is not None:
            kw["bias"] = bias
        return sch.op("scalar", lambda e, o=out, i=in_, f=func, kw=kw: e.activation(out=o, in_=i, func=f, **kw), waits)

    def TT(eng, out, in0, in1, op, waits=()):
        return sch.op(eng, lambda e, o=out, a=in0, b=in1, p=op: e.tensor_tensor(out=o, in0=a, in1=b, op=p), waits)

    def TS(eng, out, in0, s1, s2, op0, op1=None, waits=()):
        if op1 is None:
            return sch.op(eng, lambda e, o=out, a=in0, x=s1, p=op0: e.tensor_scalar(out=o, in0=a, scalar1=x, scalar2=None, op0=p), waits)
        return sch.op(eng, lambda e, o=out, a=in0, x=s1, y=s2, p=op0, q=op1: e.tensor_scalar(out=o, in0=a, scalar1=x, scalar2=y, op0=p, op1=q), waits)

    def STT(eng, out, in0, scalar, in1, op0, op1, waits=()):
        return sch.op(eng, lambda e, o=out, a=in0, s=scalar, b=in1, p=op0, q=op1:
                      e.scalar_tensor_tensor(out=o, in0=a, scalar=s, in1=b, op0=p, op1=q), waits)

    def CP(eng, out, in_, waits=()):
        if eng == "scalar":
            return ACT(out, in_, AF.Copy, waits)
        return sch.op(eng, lambda e, o=out, i=in_: e.tensor_copy(out=o, in_=i), waits)

    dma_rr = [0]

    def dq():
        return "sync"

    t_id = sch.dma("sync", ident_f, ident_in[:, :], "c0")
    t_idc = ACT(ident, ident_f, AF.Copy, [t_id])
    t_eps = sch.op("vector", lambda e: e.memset(eps_sb, EPS))
    t_cf = sch.dma("sync", cf_sb, cfar[:, :], "c1")
    t_sel = sch.dma("sync", sel_sb, sel_in[:, :], "c2")
    t_hm = sch.dma("sync", hmask_sb, hmask_in[:, :], "c3")
    t_fg = sch.dma("sync", fg_sb, fg_rep[:, :], "c4")
    CONST = [t_idc, t_eps, t_cf, t_sel, t_hm, t_fg]

    final_toks = []

    def load_w(pieces):
        toks = []
        for idx, (src_ap, vfn, dst_ap) in enumerate(pieces):
            s = idx % 2
            td = sch.dma(dq(), vfn(wstage[s]), src_ap, f"w{s}", dep(f"wstage{s}"))
            tc_ = CP("vector" if idx % 2 == 0 else "scalar", dst_ap, vfn(wstage[s]), [td])
            setlast(f"wstage{s}", tc_)
            toks.append(tc_)
        return toks

    def w_in_pieces(layer, c_lo, c_hi):
        ps_ = []
        for k in range(8):
            for c0 in range(c_lo, c_hi, 2048):
                n = min(2048, c_hi - c0)
                ps_.append((w_in[layer, k * 128:(k + 1) * 128, c0:c0 + n], (lambda ws, n=n: ws[:, 0:n]),
                            w_in_sb[:, k, c0:c0 + n]))
        return ps_

    def w_qkv_pieces(j):
        return [(w_qkv[j, k * 128:(k + 1) * 128, :], (lambda ws: ws[:, 0:1536]), w_in_sb[:, k, 0:1536]) for k in range(8)]

    def w_out_pieces(layer):
        return [(w_out[layer, k * 128:(k + 2) * 128, :].rearrange("(a p) c -> p a c", p=128),
                 (lambda ws: ws[:, :].rearrange("p (a c) -> p a c", a=2)), w_out_sb[:, k:k + 2, :]) for k in range(0, 8, 2)]

    def issue_xloads(src_ap, tok0, par):
        toks = []
        for sub in range(4):
            nm = f"xt{par}_{sub}"
            w = dep(nm) + (dep("wstage0", "wstage1") if par == 1 else [])
            toks.append(sch.dma("sync", xt2[par][sub], src_ap[tok0 + sub * 128: tok0 + (sub + 1) * 128, :], f"x_{nm}", w))
        return toks

    def norm_tile(src_ap, row0, col, hslot, xb, name, nrows=128, mask=None, tl=None):
        if tl is None:
            tl = sch.dma(dq(), xb[0:nrows, :], src_ap[row0:row0 + nrows, :], f"x_{name}", dep(name))
        if mask is not None:
            tl = TS("vector", xb[0:nrows, :], xb[0:nrows, :], mask[0:nrows, 0:1], None, ALU.mult, None, [tl] + CONST)
        tsq = ACT(junk, xb, AF.Square, [tl] + CONST + dep(f"ss{col}", "junk"), accum_out=ss[:, col:col + 1])
        setlast("junk", tsq)
        tr1 = ACT(ss[:, col:col + 1], ss[:, col:col + 1], AF.Ln, [tsq], scale=1.0 / D, bias=eps_sb[:, 0:1])
        tr2 = ACT(ss[:, col:col + 1], ss[:, col:col + 1], AF.Exp, [tr1], scale=-0.5)
        th = STT("vector", hb[hslot], xb, ss[:, col:col + 1], g_sb, ALU.mult, ALU.mult,
                 [tr2] + dep(f"hb{hslot}", "g_sb"))
        setlast(f"ss{col}", th)
        setlast(name, th)
        return th

    def transpose_hb(th, hslot):
        w = [th] + dep("pT") + CONST
        tm = None
        for k in range(8):
            tm = MM(pT[:, k, :], hb[hslot][:, k * 128:(k + 1) * 128], ident, True, True,
                    w if k == 0 else [], signal=(k == 7))
        setlast(f"hb{hslot}", tm)
        return tm

    def load_g(layer):
        tg = sch.dma("sync", g_sb, g_rep[layer, :, :], "g", dep("g_sb"))
        setlast("g_sb", tg)

    def norm_to_hT(src, tok0, hs, par=0, xl=None):
        hw = []
        for sub in range(4):
            hslot = sub % 2
            th = norm_tile(src, tok0 + sub * 128, sub, hslot, xt2[par][sub], f"xt{par}_{sub}",
                           tl=(xl[sub] if xl is not None else None))
            tm = transpose_hb(th, hslot)
            tcp = CP("scalar" if sub % 2 == 0 else "vector", hT[hs][:, :, sub * 128:(sub + 1) * 128], pT,
                     [tm] + dep(f"hT{hs}"))
            setlast("pT", tcp)
            hw.append(tcp)
        return hw

    def outproj_residual(src, dst, tok0, yw, load_x, final, par=0, xl=None):
        tmm = None
        for sub in range(4):
            xs = sub % 2
            txl = xl[sub] if xl is not None else None
            for half in range(2):
                for j in range(8):
                    tmm = MM(pO[:, half * 512:(half + 1) * 512], yT[:, j, sub * 128:(sub + 1) * 128],
                             w_out_sb[:, j, half * 512:(half + 1) * 512], j == 0, j == 7,
                             (yw + dep("pO", "w_out_sb")) if (j == 0 and half == 0) else [],
                             signal=(j == 7 and half == 1))
            tad = TT("vector", xo[xs], pO, xt2[par][sub], ALU.add, [tmm, txl] + dep(f"xo{xs}"))
            setlast("pO", tad)
            setlast(f"xt{par}_{sub}", tad)
            if final:
                c_ = 4 + xs
                tsq = ACT(junk, xo[xs], AF.Square, [tad] + CONST + dep(f"ss{c_}", "junk"), accum_out=ss[:, c_:c_ + 1])
                setlast("junk", tsq)
                tr1 = ACT(ss[:, c_:c_ + 1], ss[:, c_:c_ + 1], AF.Ln, [tsq], scale=1.0 / D, bias=eps_sb[:, 0:1])
                tr2 = ACT(ss[:, c_:c_ + 1], ss[:, c_:c_ + 1], AF.Exp, [tr1], scale=-0.5)
                tad = STT("vector", xo[xs], xo[xs], ss[:, c_:c_ + 1], fg_sb, ALU.mult, ALU.mult, [tr2])
                setlast(f"ss{c_}", tad)
            tst = sch.dma(dq(), dst[tok0 + sub * 128: tok0 + (sub + 1) * 128, :], xo[xs], f"xo{xs}", [tad])
            setlast(f"xo{xs}", tst)
            final_toks.append(tst)
        setlast("yT", tmm)
        return tmm

    def conv_stage(layer, cj, src, dst, halo_ap, use_mask):
        load_g(layer)
        tcw = sch.dma("sync", cw_sb, conv_w[cj].rearrange("(j p) k -> p j k", p=128), "cw", dep("cw_sb"))
        setlast("cw_sb", tcw)
        wt = load_w(w_in_pieces(layer, 0, 4096) + w_out_pieces(layer))
        setlast("w_in_sb", *wt)
        setlast("w_out_sb", *wt)
        tz = sch.op("gpsimd", lambda e: e.memset(xh, 0.0), dep("xh"))
        setlast("xh", tz)
        th = norm_tile(halo_ap, 0, 4, 0, xh, "xh", nrows=2, mask=hmask_sb if use_mask else None)
        tm = transpose_hb(th, 0)
        tcp = ACT(hT_halo, pT, AF.Copy, [tm] + dep("hT_halo"))
        setlast("pT", tcp)
        for j in range(8):
            tk = {}
            for gi in (1, 2):
                for k in range(8):
                    tk[gi] = MM(pP[gi][:, 0:2], w_in_sb[:, k, gi * 1024 + j * 128: gi * 1024 + (j + 1) * 128],
                                hT_halo[:, k, 0:2], k == 0, k == 7,
                                ([tcp] + dep(f"pP{gi}", "w_in_sb")) if k == 0 else [], signal=(k == 7))
            tuc = ACT(ut[:, 0:2], pP[2][:, 0:2], AF.Copy, [tk[2]] + dep("ut"))
            tv = TT("vector", vbuf[j][:, 0:2], pP[1][:, 0:2], ut[:, 0:2], ALU.mult, [tuc, tk[1]] + dep(f"vbuf{j}"))
            setlast("ut", tv)
            setlast("pP1", tv)
            setlast("pP2", tuc)
            setlast(f"vbuf{j}", tv)
        setlast("hT_halo", tk[2])
        xl_next = issue_xloads(src, 0, 0)
        for T in range(NT):
            tok0 = T * 512
            hs = T % 2
            par = T % 2
            xl_cur = xl_next
            if T + 1 < NT:
                xl_next = issue_xloads(src, (T + 1) * 512, 1 - par)
            hw = norm_to_hT(src, tok0, hs, par, xl_cur)
            yw = []
            tmm = None
            for j in range(8):
                tg_ = {}
                for g_ in (1, 2, 3, 0):
                    for k in range(8):
                        tmm = MM(pP[g_], w_in_sb[:, k, g_ * 1024 + j * 128: g_ * 1024 + (j + 1) * 128],
                                 hT[hs][:, k, :], k == 0, k == 7,
                                 (hw + dep(f"pP{g_}", "w_in_sb")) if k == 0 else [], signal=(k == 7))
                    tg_[g_] = tmm
                tb, tc1, tu1, tz1 = tg_[0], tg_[1], tg_[2], tg_[3]
                tuc = ACT(ut, pP[2], AF.Copy, [tu1] + dep("ut"))
                setlast("pP2", tuc)
                tsz = ACT(sz, pP[3], AF.Silu, [tz1] + dep("sz"))
                setlast("pP3", tsz)
                tv = TT("vector", vbuf[j][:, 2:514], pP[1], ut, ALU.mult, [tuc, tc1] + dep(f"vbuf{j}"))
                setlast("pP1", tv)
                setlast("ut", tv)
                ta = ACT(t1, vbuf[j][:, 0:512], AF.Copy, [tv] + dep("t1", "cw_sb"), scale=cw_sb[:, j, 0:1])
                tb_ = STT("vector", t1, vbuf[j][:, 1:513], cw_sb[:, j, 1:2], t1, ALU.mult, ALU.add, [ta])
                tc_ = STT("vector", t1, vbuf[j][:, 2:514], cw_sb[:, j, 2:3], t1, ALU.mult, ALU.add, [tb_])
                tcar = CP("vector", vbuf[j][:, 0:2], vbuf[j][:, 512:514], [tc_])
                setlast(f"vbuf{j}", tcar)
                td_ = TT("vector", t2, pP[0], t1, ALU.mult, [tc_, tb] + dep("t2"))
                setlast("pP0", td_)
                setlast("t1", td_)
                te_ = TT("vector", yT[:, j, :], t2, sz, ALU.mult, [td_, tsz] + dep("yT"))
                setlast("sz", te_)
                setlast("t2", te_)
                yw.append(te_)
            setlast(f"hT{hs}", tmm)
            outproj_residual(src, dst, tok0, yw, load_x=False, final=False, par=par)

    def pz_stage(layer, src):
        load_g(layer)
        wt = load_w(w_in_pieces(layer, 3072, 4096))
        setlast("w_in_sb", *wt)
        ring = 0
        xl_next = issue_xloads(src, 0, 0)
        for T in range(NT):
            tok0 = T * 512
            hs = T % 2
            par = T % 2
            xl_cur = xl_next
            if T + 1 < NT:
                xl_next = issue_xloads(src, (T + 1) * 512, 1 - par)
            hw = norm_to_hT(src, tok0, hs, par, xl_cur)
            tsh = sch.dma(dq(), HTown3[:, :, tok0:tok0 + 512].rearrange("k p t -> p k t"), hT[hs], f"hst{hs}", hw)
            final_toks.append(tsh)
            tmm = None
            for j in range(8):
                b_ = ring % 4
                ring += 1
                for k in range(8):
                    tmm = MM(pP[b_], w_in_sb[:, k, 3072 + j * 128: 3072 + (j + 1) * 128], hT[hs][:, k, :], k == 0, k == 7,
                             (hw + dep(f"pP{b_}", "w_in_sb")) if k == 0 else [], signal=(k == 7))
                tev = ACT(ev[b_], pP[b_], AF.Silu, [tmm] + dep(f"ev{b_}"))
                setlast(f"pP{b_}", tev)
                tst = sch.dma(dq(), ZTd[j, :, tok0:tok0 + 512], ev[b_], f"ev{b_}", [tev])
                setlast(f"ev{b_}", tst)
                final_toks.append(tst)
            setlast(f"hT{hs}", tmm, tsh)

    def hproj_stage(j_attn, preloaded=False):
        if not preloaded:
            wt = load_w(w_qkv_pieces(j_attn))
            setlast("w_in_sb", *wt)
        ring = 0
        for T in range(16):
            rk, lt = T // 8, T % 8
            hs = T % 2
            g0 = T * 512
            hw = []
            for pc in range(4):
                tl = sch.dma(dq(), hT[hs][:, 2 * pc:2 * pc + 2, :],
                             HTall5[pc, rk, :, :, lt * 512:(lt + 1) * 512].rearrange("kk p t -> p kk t"),
                             f"hld{hs}", dep(f"hT{hs}"))
                hw.append(tl)
            tmm = None
            for jj in range(8):
                b_ = ring % 4
                ring += 1
                for k in range(8):
                    tmm = MM(pP[b_], w_in_sb[:, k, jj * 128:(jj + 1) * 128], hT[hs][:, k, :], k == 0, k == 7,
                             (hw + dep(f"pP{b_}", "w_in_sb")) if k == 0 else [], signal=(k == 7))
                tev = CP("scalar" if jj % 2 == 0 else "vector", ev[b_], pP[b_], [tmm] + dep(f"ev{b_}"))
                setlast(f"pP{b_}", tev)
                dstT = QTa[jj, :, g0:g0 + 512] if jj < 4 else KTa[jj - 4, :, g0:g0 + 512]
                tst = sch.dma(dq(), dstT, ev[b_], f"ev{b_}", [tev])
                setlast(f"ev{b_}", tst)
                final_toks.append(tst)
            for sub in range(4):
                b_ = ring % 4
                ring += 1
                for k in range(8):
                    tmm = MM(pP[b_], hT[hs][:, k, sub * 128:(sub + 1) * 128], w_in_sb[:, k, 1024:1536], k == 0, k == 7,
                             (hw + dep(f"pP{b_}", "w_in_sb")) if k == 0 else [], signal=(k == 7))
                tev = CP("scalar" if sub % 2 == 0 else "vector", ev[b_], pP[b_], [tmm] + dep(f"ev{b_}"))
                setlast(f"pP{b_}", tev)
                tst = sch.dma(dq(), Va[:, g0 + sub * 128:g0 + (sub + 1) * 128, :].rearrange("h t e -> t h e"),
                              ev[b_][:, :].rearrange("p (h e) -> p h e", h=4), f"ev{b_}", [tev])
                setlast(f"ev{b_}", tst)
                final_toks.append(tst)
            setlast(f"hT{hs}", tmm)

    def aout_stage(layer, src, dst, final, preloaded=False):
        if not preloaded:
            wt = load_w(w_out_pieces(layer))
            setlast("w_out_sb", *wt)
        xl_next = issue_xloads(src, 0, 0)
        for T in range(NT):
            tok0 = T * 512
            par = T % 2
            xl_cur = xl_next
            if T + 1 < NT:
                xl_next = issue_xloads(src, (T + 1) * 512, 1 - par)
            tl1 = sch.dma("sync", onl, ONTr3[:, :, tok0:tok0 + 512].rearrange("j p t -> p j t"), "onl", dep("onl"))
            tl2 = sch.dma("sync", ztl, ZTd[:, :, tok0:tok0 + 512].rearrange("j p t -> p j t"), "ztl", dep("ztl"))
            ty = TT("vector", yT, onl, ztl, ALU.mult, [tl1, tl2] + dep("yT"))
            setlast("onl", ty)
            setlast("ztl", ty)
            outproj_residual(src, dst, tok0, [ty], load_x=True, final=final, par=par, xl=xl_cur)

    def attn_stage(j_attn, lam_init):
        t_lq = sch.dma("sync", lq_sb, lqk[j_attn], "lq")
        t_gs = sch.dma("sync", gs_sb, gsub_in[j_attn], "gs")
        t_gs2 = TS("vector", gs_sb, gs_sb, 1.0 - lam_init, None, ALU.mult, None, [t_gs])
        t_p = TT("vector", lprod, lq_sb[:, 0:4:2, :], lq_sb[:, 1:4:2, :], ALU.mult, [t_lq])
        t_s1 = ACT(junkA[:, 0:64], lprod[:, 0, :], AF.Copy, [t_p] + dep("junkA"), accum_out=lsum[:, 0:1])
        t_s2 = ACT(junkA[:, 0:64], lprod[:, 1, :], AF.Copy, [t_s1], accum_out=lsum[:, 1:2])
        t_e = ACT(lsum[:, 2:4], lsum[:, 0:2], AF.Exp, [t_s2])
        t_l1 = TT("vector", neglam, lsum[:, 3:4], lsum[:, 2:3], ALU.subtract, [t_e])
        t_l2 = TS("vector", neglam, neglam, -lam_init, None, ALU.add, None, [t_l1])
        setlast("junkA", t_s2)
        for i in range(2):
            tv1 = sch.op("gpsimd", lambda e, i=i: e.memset(VS[i][:, :, 128:130], 1.0))
            setlast(f"VS{i}", tv1)

        def load_head(h):
            s = h % 2
            toks = []
            toks.append(sch.dma("sync", QT[s][:, 0:S // 2], QTa[h, :, 0:S // 2], f"hd{s}", dep(f"QT{s}")))
            toks.append(sch.dma("sync", QT[s][:, S // 2:S], QTa[h, :, S // 2:S], f"hd{s}", dep(f"QT{s}")))
            toks.append(sch.dma("sync", KT[s][:, 0:S // 2], KTa[h, :, 0:S // 2], f"hd{s}", dep(f"KT{s}")))
            toks.append(sch.dma("sync", KT[s][:, S // 2:S], KTa[h, :, S // 2:S], f"hd{s}", dep(f"KT{s}")))
            vv = Va[h].rearrange("(c p) e -> p c e", p=128)
            for qd in range(4):
                toks.append(sch.dma("sync", VS[s][:, qd * 16:(qd + 1) * 16, 0:128],
                                    vv[:, qd * 16:(qd + 1) * 16, :], f"hd{s}", dep(f"VS{s}")))
            tz = sch.dma("sync", ZC[s], Zb[2 * h:2 * h + 2].rearrange("m p w -> p m w"), f"hz{s}", dep(f"ZC{s}"))
            tzc = None
            for m in range(2):
                tzc = TS("vector", ZC[s][:, m, :], ZC[s][:, m, :], cf_sb[:, 2 * h + m:2 * h + m + 1], None, ALU.subtract,
                         None, [tz] + CONST)
            toks.append(tzc)
            return toks

        sring = [0]
        pring = [0]
        bring = [0]
        pending_epi = [None]

        def emit_epilogue_pe(info):
            h, j, es, ton = info
            sl = sring[0] % 2
            sring[0] += 1
            w0 = dep(f"SP{sl}") + CONST
            tms = []
            for s4 in range(4):
                tm = MM(SP[sl][:, 0, s4 * 128:(s4 + 1) * 128], onb[s4], ident, True, True,
                        (w0 + [ton[s4]]) if s4 == 0 else [ton[s4]], signal=True)
                tms.append(tm)
                setlast(f"onb{s4}", tm)
            os_ = j % 2
            tca = TS("vector", onTa[os_], SP[sl][:, 0, :], sel_sb[:, 0:1], None, ALU.mult, None,
                     [tms[-1]] + dep(f"onTa{os_}"))
            tcb = TS("vector", onTb[os_], SP[sl][:, 0, :], sel_sb[:, 1:2], None, ALU.mult, None,
                     [tca] + dep(f"onTb{os_}"))
            setlast(f"SP{sl}", tca, tcb)
            d_, c0 = j // 8, (j % 8) * 512
            tsa = sch.dma("sync", RS5[0, d_, h, :, c0:c0 + 512], onTa[os_], f"onta{os_}", [tca])
            tsb = sch.dma("sync", RS5[1, d_, h, :, c0:c0 + 512], onTb[os_], f"ontb{os_}", [tcb])
            setlast(f"onTa{os_}", tsa)
            setlast(f"onTb{os_}", tsb)
            final_toks.append(tsa)
            final_toks.append(tsb)

        head_toks = {0: load_head(0)}
        unit_idx = 0
        for h in range(4):
            s = h % 2
            if h + 1 < 4:
                head_toks[h + 1] = load_head(h + 1)
            hw = head_toks[h]
            last_pe_read = None
            for j in range(16):
                nchunks = 4 * j + 4
                q0 = j * 512

                def issue_qk(c):
                    i = c - 4 * j
                    lo = 128 * i if i > 0 else 0
                    sl = sring[0] % 2
                    sring[0] += 1
                    w = dep(f"SP{sl}") + (hw if c == 0 else [])
                    tq = None
                    for m in range(2):
                        tq = MM(SP[sl][:, m, lo:512], KT[s][m * 64:(m + 1) * 64, c * 128:(c + 1) * 128],
                                QT[s][m * 64:(m + 1) * 64, q0 + lo:q0 + 512], True, True, w if m == 0 else [],
                                signal=(m == 1))
                    pl = pring[0] % NP
                    pring[0] += 1
                    if i >= -1:
                        o_idx = 3 - i
                        bl = bring[0] % 2
                        bring[0] += 1
                        tb = STT("vector", SSB[bl][:, :, lo:512], SP[sl][:, :, lo:512], 0.125,
                                 ZC[s][:, :, o_idx * 128 + lo:o_idx * 128 + 512], ALU.mult, ALU.add,
                                 [tq] + dep(f"SSB{bl}") + hw)
                        setlast(f"SP{sl}", tb)
                        te = ACT(PT[pl][:, :, lo:512], SSB[bl][:, :, lo:512], AF.Exp, [tb] + dep(f"PT{pl}"))
                        setlast(f"SSB{bl}", te)
                    else:
                        te = ACT(PT[pl][:, :, lo:512], SP[sl][:, :, lo:512], AF.Exp, [tq] + dep(f"PT{pl}"), scale=0.125)
                        setlast(f"SP{sl}", te)
                    return (c, i, pl, te)

                def issue_av(tile):
                    c, i, pl, te = tile
                    tm = None
                    first = True
                    for s4 in range(4):
                        if s4 < i:
                            continue
                        lastc = 4 * j + s4
                        for m in range(2):
                            a = s4 * 2 + m
                            w = []
                            if first:
                                w = [te] + (dep("OA") if c == 0 else [])
                                first = False
                            tm = MM(OA[:, a, 0:129], PT[pl][:, m, s4 * 128:(s4 + 1) * 128], VS[s][:, c, 0:129],
                                    (c == 0 and m == 0), (c == lastc), w, signal=(s4 == 3 and m == 1))
                    setlast(f"PT{pl}", tm)
                    return tm

                LAG = 3
                pendq = []
                tav = None
                for c in range(nchunks):
                    pendq.append(issue_qk(c))
                    if len(pendq) > LAG:
                        tav = issue_av(pendq.pop(0))
                    if c == min(6, nchunks - 1) and pending_epi[0] is not None:
                        emit_epilogue_pe(pending_epi[0])
                        pending_epi[0] = None
                while pendq:
                    tav = issue_av(pendq.pop(0))
                last_pe_read = tav
                es = unit_idx % 2
                unit_idx += 1
                tev = CP("vector", OEV[es], OA[:, :, 0:129], [tav] + dep(f"OEV{es}"))
                setlast("OA", tev)
                trr = sch.op("vector", lambda e, es=es: e.reciprocal(out=rr[:, :, :].rearrange("p s m -> p (s m)"),
                                                                     in_=OEV[es][:, :, 128:129].rearrange("p a o -> p (a o)")),
                             [tev] + dep("rr"))
                tr2 = TS("vector", r2, rr[:, :, 1], neglam[:, 0:1], None, ALU.mult, None, [trr, t_l2] + dep("r2"))
                ton = []
                tlast = None
                tos = []
                tqs = []
                for s4 in range(4):
                    ta = TS("gpsimd", tA[s4], OEV[es][:, 2 * s4 + 1, 0:128], r2[:, s4:s4 + 1], None, ALU.mult, None,
                            [tr2] + dep(f"tA{s4}"))
                    to = STT("vector", oo[s4], OEV[es][:, 2 * s4, 0:128], rr[:, s4, 0:1], tA[s4], ALU.mult, ALU.add,
                             [ta, trr] + dep(f"oo{s4}"))
                    setlast(f"tA{s4}", to)
                    tq_ = ACT(junkA, oo[s4], AF.Square, [to] + CONST + dep("q1", "junkA"), accum_out=q1[:, s4:s4 + 1])
                    setlast("junkA", tq_)
                    tos.append(to)
                    tqs.append(tq_)
                tl_ = ACT(q1[:, 0:4], q1[:, 0:4], AF.Ln, tqs, scale=1.0 / 128, bias=eps_sb[:, 0:1])
                tx_ = ACT(q1[:, 0:4], q1[:, 0:4], AF.Exp, [tl_], scale=-0.5)
                for s4 in range(4):
                    tn_ = STT("vector", onb[s4], oo[s4], q1[:, s4:s4 + 1], gs_sb, ALU.mult, ALU.mult,
                              [tx_, t_gs2] + dep(f"onb{s4}"))
                    setlast(f"oo{s4}", tn_)
                    ton.append(tn_)
                    tlast = tn_
                setlast("q1", tlast)
                setlast("rr", tlast)
                setlast("r2", tlast)
                setlast(f"OEV{es}", tlast)
                if pending_epi[0] is not None:
                    emit_epilogue_pe(pending_epi[0])
                pending_epi[0] = (h, j, es, ton)
            for nm in (f"QT{s}", f"KT{s}", f"VS{s}"):
                addlast(nm, last_pe_read)
            setlast(f"ZC{s}", ("e_vector", sch.cnt.get("e_vector", 0)))
        if pending_epi[0] is not None:
            emit_epilogue_pe(pending_epi[0])

    def collective(kind, op, src2d, dst2d, name):
        return sch.coll(lambda e, k=kind, o=op, a=src2d, b=dst2d: e.collective_compute(
            k, o, replica_groups=GROUPS, ins=[a], outs=[b]), name)

    xs_ = [x0, X1, X2, X3]
    phase_no = [0]

    def want():
        phase_no[0] += 1
        return phase_no[0] <= upto

    for blk in range(2):
        l_conv, l_attn = 2 * blk, 2 * blk + 1
        src = xs_[l_conv]
        mid = xs_[l_conv + 1]
        dst = xs_[l_conv + 2] if blk == 0 else out_ap
        if want():
            if blk == 0:
                conv_stage(l_conv, blk, src, mid, xhalo, use_mask=False)
            else:
                conv_stage(l_conv, blk, src, mid, TAILall[0:2, :], use_mask=True)
            phase_barrier()
        if want():
            pz_stage(l_attn, mid)
            phase_barrier()
        if want():
            for pc in range(4):
                collective("AllGather", ALU.bypass, HTown[pc * 256:(pc + 1) * 256, :], HTall[pc * 512:(pc + 1) * 512, :], f"ag{pc}")
            load_w(w_qkv_pieces(blk))
            phase_barrier()
        if want():
            hproj_stage(blk, preloaded=True)
            phase_barrier()
        if want():
            attn_stage(blk, lam_inits[blk])
            phase_barrier()
        if want():
            for pc in range(2):
                collective("ReduceScatter", ALU.add, RSbuf[pc * 1024:(pc + 1) * 1024, :], ONTr[pc * 512:(pc + 1) * 512, :], f"rs{pc}")
            load_w(w_out_pieces(l_attn))
            phase_barrier()
        if want():
            aout_stage(l_attn, mid, dst, final=(blk == 1), preloaded=True)
            phase_barrier()
        if blk == 0 and want():
            ttail = sch.dma("sync", TAILown[:, :], X2[TOK - 2:TOK, :], "tail")
            final_toks.append(ttail)
            phase_barrier()
            collective("AllGather", ALU.bypass, TAILown, TAILall, "agt")
            phase_barrier()
    if debug_out:
        dbg = nc.dram_tensor("dbg", [TOK, D], F32, kind="ExternalOutput").ap()
        for q_ in range(4):
            td_ = sch.dma("sync", dbg[q_ * 1024:(q_ + 1) * 1024, :], debug_out[q_ * 1024:(q_ + 1) * 1024, :], "dbg")
            final_toks.append(td_)

    mx = {}
    for (k, v) in final_toks:
        mx[k] = max(mx.get(k, 0), v)
    run_sched(nc, sch, list(mx.items()))
    est.close()
    return nc


BF = ml_dtypes.bfloat16
NUM_BUCKETS, MAX_EXACT, REL_MAX = 32, 16, 128
MASKV = -30000.0

def bucket_table(n):
    d = np.arange(n, dtype=np.int64)
    ds = np.maximum(d, 1).astype(np.float32)
    large = MAX_EXACT + (np.log(ds / np.float32(MAX_EXACT)) / np.float32(math.log(REL_MAX / MAX_EXACT))
                         * np.float32(NUM_BUCKETS - MAX_EXACT)).astype(np.int32)
    large = np.minimum(large, NUM_BUCKETS - 1)
    return np.where(d < MAX_EXACT, d, large).astype(np.int64)

def common_inputs(inp):
    rep = lambda a: np.ascontiguousarray(np.broadcast_to(a[..., None, :], a.shape[:-1] + (128, a.shape[-1]))).astype(np.float32)
    return {
        "w_in": np.ascontiguousarray(inp["w_in"], dtype=np.float32),
        "w_out": np.ascontiguousarray(inp["w_out"], dtype=np.float32),
        "norm_g_rep": rep(np.asarray(inp["norm_g"])),
        "final_g_rep": rep(np.asarray(inp["final_g"])),
        "conv_w": np.ascontiguousarray(inp["conv_w"], dtype=np.float32),
        "ident": np.eye(128, dtype=np.float32),
    }

def attn_consts(inp, layer_j, core_r):
    rb = np.asarray(inp["rel_bias"], dtype=np.float32)
    bk = bucket_table(1024)
    maps = np.arange(8) + 8 * core_r
    p = np.arange(128)[:, None]
    w = np.arange(1024)[None, :]
    d = w - 384 - p
    dd = np.clip(d, 0, 1023)
    Zb = np.empty((8, 128, 1024), np.float32)
    for mi, m in enumerate(maps):
        vals = rb[bk[dd], m]
        Zb[mi] = np.where(d >= 0, vals, np.float32(MASKV))
    cf = np.ascontiguousarray(np.broadcast_to(rb[31, maps][None, :], (128, 8))).astype(np.float32)
    lqk = np.stack([np.asarray(inp[k])[layer_j] for k in ("lambda_q1", "lambda_k1", "lambda_q2", "lambda_k2")], 0)
    lqk = np.ascontiguousarray(np.broadcast_to(lqk[None], (128, 4, 64))).astype(np.float32)
    gs = np.ascontiguousarray(np.broadcast_to(np.asarray(inp["subln_g"])[layer_j][None, :], (128, 128))).astype(np.float32)
    return {"Zb": Zb, "cfar": cf, "lqk": lqk, "gsub": gs, "ident": np.eye(128, dtype=np.float32)}


def fused_inputs(inp):
    com = common_inputs(inp)
    x = inp["x"]
    w_in = inp["w_in"]
    lqk = np.stack([np.stack([np.asarray(inp[k])[j] for k in ("lambda_q1", "lambda_k1", "lambda_q2", "lambda_k2")], 0)
                    for j in range(2)], 0)
    lqk = np.ascontiguousarray(np.broadcast_to(lqk[:, None], (2, 128, 4, 64))).astype(np.float32)
    gs = np.ascontiguousarray(np.broadcast_to(np.asarray(inp["subln_g"])[:, None, :], (2, 128, 128))).astype(np.float32)
    maps = []
    for c in range(8):
        b, r = c // 2, c % 2
        m = dict(com)
        m["x0"] = np.ascontiguousarray(x[b, r * TOK:(r + 1) * TOK])
        m["xhalo"] = np.zeros((2, D), np.float32) if r == 0 else np.ascontiguousarray(x[b, TOK - 2:TOK])
        wq = []
        for j in range(2):
            l = 2 * j + 1
            cols = [w_in[l][:, base + r * 512: base + (r + 1) * 512] for base in (0, 1024, 2048)]
            wq.append(np.concatenate(cols, axis=1))
        m["w_qkv"] = np.ascontiguousarray(np.stack(wq, 0)).astype(np.float32)
        ac = attn_consts(inp, 0, r)
        m["Zb"] = ac["Zb"]
        m["cfar"] = ac["cfar"]
        m["lqk"] = lqk
        m["gsub"] = gs
        sel = np.zeros((128, 2), np.float32)
        sel[:, r] = 1.0
        m["sel"] = sel
        m["hmask"] = np.full((128, 1), float(r), np.float32)
        maps.append(m)
    return maps


_NC = {}


def _lam_init(layer_idx):
    return 0.8 - 0.6 * math.exp(-0.3 * layer_idx)


def kernel(x, norm_g, w_in, w_out, conv_w, lambda_q1, lambda_k1, lambda_q2, lambda_k2,
           subln_g, rel_bias, final_g):
    inp = {"x": np.asarray(x, np.float32), "norm_g": np.asarray(norm_g, np.float32),
           "w_in": np.asarray(w_in, np.float32), "w_out": np.asarray(w_out, np.float32),
           "conv_w": np.asarray(conv_w, np.float32),
           "lambda_q1": np.asarray(lambda_q1, np.float32), "lambda_k1": np.asarray(lambda_k1, np.float32),
           "lambda_q2": np.asarray(lambda_q2, np.float32), "lambda_k2": np.asarray(lambda_k2, np.float32),
           "subln_g": np.asarray(subln_g, np.float32), "rel_bias": np.asarray(rel_bias, np.float32),
           "final_g": np.asarray(final_g, np.float32)}
    if "nc" not in _NC:
        _NC["nc"] = build_fused((_lam_init(1), _lam_init(3)))
    maps = fused_inputs(inp)
    res = run_bass_kernel_spmd(_NC["nc"], maps, core_ids=list(range(8)))
    out = np.empty((4, S, D), np.float32)
    for c in range(8):
        out[c // 2, (c % 2) * TOK:(c % 2 + 1) * TOK] = res.results[c]["out"]
    return out
```

```python
import math
import numpy as np
import ml_dtypes
import concourse.bass as bass
import concourse.mybir as mybir
from concourse.bass_utils import run_bass_kernel_spmd

F32 = mybir.dt.float32
BF16 = mybir.dt.bfloat16
AF = mybir.ActivationFunctionType
ALU = mybir.AluOpType

D = 1024
S = 8192
TOK = 4096
NT = TOK // 512
EPS = 1e-6
MASKV = -30000.0
ENGS = ("tensor", "vector", "scalar", "gpsimd", "sync")


class Sched:
    def __init__(self):
        self.ops = {e: [] for e in ENGS}
        self.cnt = {}
        self.sems = {}
        self.bar = []

    def op(self, eng, fn, waits=(), signal=True):
        key = "e_" + eng
        tok = None
        if signal:
            self.cnt[key] = self.cnt.get(key, 0) + 1
            tok = (key, self.cnt[key])
        self.ops[eng].append((fn, self._dd(tuple(waits) + tuple(self.bar)), key if signal else None, 1))
        return tok

    @staticmethod
    def _dd(waits):
        mx = {}
        for w in waits:
            if w is None:
                continue
            k, v = w
            if mx.get(k, 0) < v:
                mx[k] = v
        return tuple(mx.items())

    def dma(self, eng, out, in_, stream, waits=()):
        key = "d_" + stream
        self.cnt[key] = self.cnt.get(key, 0) + 16
        tok = (key, self.cnt[key])
        self.ops[eng].append((lambda e, o=out, i=in_: e.dma_start(out=o, in_=i),
                              self._dd(tuple(waits) + tuple(self.bar)), key, 16))
        return tok

    def coll(self, fn, name, waits=()):
        key = "c_" + name
        self.cnt[key] = self.cnt.get(key, 0) + 1
        tok = (key, self.cnt[key])
        self.ops["gpsimd"].append((fn, self._dd(tuple(waits) + tuple(self.bar)), key, 1))
        return tok

    def barrier(self):
        self.bar = [(k, v) for k, v in self.cnt.items()]

    def sem_keys(self):
        return sorted(self.cnt.keys())

    def replay(self, eng_name, eng):
        waited = {}
        for fn, waits, key, inc in self.ops[eng_name]:
            for (k, v) in waits:
                if waited.get(k, 0) < v:
                    eng.wait_ge(self.sems[k], v)
                    waited[k] = v
            ins = fn(eng)
            if key is not None:
                ins.then_inc(self.sems[key], inc)

    def final_waits(self, eng_name, eng, toks):
        for (k, v) in toks:
            eng.wait_ge(self.sems[k], v)


def run_sched(nc, sch, final_toks):
    from contextlib import ExitStack
    with ExitStack() as st:
        for k in sch.sem_keys():
            sch.sems[k] = st.enter_context(nc.semaphore(k))
        block = st.enter_context(nc.Block())

        @block.sync
        def _(e):
            sch.replay("sync", e)
            sch.final_waits("sync", e, final_toks)

        @block.tensor
        def _(e):
            sch.replay("tensor", e)

        @block.vector
        def _(e):
            sch.replay("vector", e)

        @block.scalar
        def _(e):
            sch.replay("scalar", e)

        @block.gpsimd
        def _(e):
            sch.replay("gpsimd", e)


def build_fused(lam_inits=(0.0, 0.0), debug_out=False, upto=99):
    from contextlib import ExitStack
    nc = bass.Bass("TRN2", target_bir_lowering=False)

    def ext_in(name, shape, dt=F32):
        return nc.dram_tensor(name, list(shape), dt, kind="ExternalInput").ap()

    def internal(name, shape, dt):
        return nc.dram_tensor(name, list(shape), dt).ap()

    x0 = ext_in("x0", [TOK, D])
    xhalo = ext_in("xhalo", [2, D])
    w_in = ext_in("w_in", [4, D, 4 * D])
    w_out = ext_in("w_out", [4, D, D])
    w_qkv = ext_in("w_qkv", [2, D, 1536])
    g_rep = ext_in("norm_g_rep", [4, 128, D])
    fg_rep = ext_in("final_g_rep", [128, D])
    conv_w = ext_in("conv_w", [2, D, 3])
    ident_in = ext_in("ident", [128, 128])
    Zb = ext_in("Zb", [8, 128, 1024])
    cfar = ext_in("cfar", [128, 8])
    lqk = ext_in("lqk", [2, 128, 4, 64])
    gsub_in = ext_in("gsub", [2, 128, 128])
    sel_in = ext_in("sel", [128, 2])
    hmask_in = ext_in("hmask", [128, 1])
    out_ap = nc.dram_tensor("out", [TOK, D], F32, kind="ExternalOutput").ap()

    X1 = internal("X1", [TOK, D], F32)
    X2 = internal("X2", [TOK, D], F32)
    X3 = internal("X3", [TOK, D], F32)
    HTown = internal("HTown", [8 * 128, TOK], BF16)
    HTall = internal("HTall", [2 * 8 * 128, TOK], BF16)
    ZTd = internal("ZTd", [8, 128, TOK], BF16)
    QTa = internal("QTa", [4, 128, S], BF16)
    KTa = internal("KTa", [4, 128, S], BF16)
    Va = internal("Va", [4, S, 128], BF16)
    RSbuf = internal("RSbuf", [2 * 8 * 128, TOK], BF16)
    ONTr = internal("ONTr", [8 * 128, TOK], BF16)
    TAILown = internal("TAILown", [2, D], F32)
    TAILall = internal("TAILall", [4, D], F32)
    HTown3 = HTown.rearrange("(k p) t -> k p t", p=128)
    HTall5 = HTall.rearrange("(pc r kk p) t -> pc r kk p t", pc=4, r=2, kk=2, p=128)
    RS5 = RSbuf.rearrange("(pc d hh p) t -> pc d hh p t", pc=2, d=2, hh=4, p=128)
    ONTr3 = ONTr.rearrange("(h p) t -> h p t", p=128)
    GROUPS = [[0, 1], [2, 3], [4, 5], [6, 7]]

    sch = Sched()
    est = ExitStack()
    ARENA_BYTES = 206848
    arena = est.enter_context(nc.sbuf_tensor("arena", [128, ARENA_BYTES // 2], BF16))
    psA = est.enter_context(nc.psum_tensor("psA", [128, 2048], F32))
    psB = est.enter_context(nc.psum_tensor("psB", [128, 2048], F32))

    def view(off, shape, dt):
        n = 1
        for s_ in shape[1:]:
            n *= s_
        nb = n * (4 if dt == F32 else 2)
        assert off % 4 == 0 and off + nb <= ARENA_BYTES, (off, nb)
        a = arena[:, off // 2:(off + nb) // 2]
        if dt == F32:
            a = a.bitcast(F32)
        if len(shape) == 3:
            a = a.rearrange("p (a b) -> p a b", a=shape[1])
        return a

    class Alloc:
        def __init__(self, base):
            self.off = base

        def __call__(self, shape, dt):
            n = 1
            for s_ in shape[1:]:
                n *= s_
            nb = n * (4 if dt == F32 else 2)
            nb_al = (nb + 31) // 32 * 32
            v = view(self.off, shape, dt)
            self.off += nb_al
            return v

    ta_ = Alloc(0)
    wstage = [ta_([128, 2048], F32) for _ in range(2)]
    w_in_sb = ta_([128, 8, 4096], BF16)
    w_out_sb = ta_([128, 8, 1024], BF16)
    xtA = [ta_([128, D], F32) for _ in range(4)]
    xtB = [wstage[0][:, 0:1024], wstage[0][:, 1024:2048], wstage[1][:, 0:1024], wstage[1][:, 1024:2048]]
    xt2 = [xtA, xtB]
    xt = xtA
    junk = ta_([128, D], BF16)
    hb = [ta_([128, D], BF16) for _ in range(2)]
    hT = [ta_([128, 8, 512], BF16) for _ in range(2)]
    hT_halo = ta_([128, 8, 128], BF16)
    xh = ta_([128, D], F32)
    ut = ta_([128, 512], F32)
    sz = ta_([128, 512], F32)
    vbuf = [ta_([128, 514], F32) for _ in range(8)]
    t1 = ta_([128, 512], F32)
    t2 = ta_([128, 512], F32)
    yT = ta_([128, 8, 512], BF16)
    ev = [ta_([128, 512], BF16) for _ in range(4)]
    xo = [ta_([128, D], F32) for _ in range(2)]
    tok_end = ta_.off
    onl, ztl = hT[0], hT[1]
    aa_ = Alloc(0)
    QT = [aa_([128, S], BF16) for _ in range(2)]
    KT = [aa_([128, S], BF16) for _ in range(2)]
    VS = [aa_([128, 64, 130], BF16) for _ in range(2)]
    ZC = [aa_([128, 2, 1024], F32) for _ in range(2)]
    NP = 10
    PT = [aa_([128, 2, 512], BF16) for _ in range(NP)]
    SSB = [aa_([128, 2, 512], F32) for _ in range(2)]
    OEV = [aa_([128, 8, 129], F32) for _ in range(2)]
    tA = [aa_([128, 128], F32) for _ in range(4)]
    oo = [aa_([128, 128], F32) for _ in range(4)]
    junkA = aa_([128, 128], F32)
    onb = [aa_([128, 128], BF16) for _ in range(4)]
    onTa = [aa_([128, 512], BF16) for _ in range(2)]
    onTb = [aa_([128, 512], BF16) for _ in range(2)]
    att_end = aa_.off
    pa_ = Alloc(max(tok_end, att_end))
    g_sb = pa_([128, D], F32)
    fg_sb = pa_([128, D], F32)
    ident_f = pa_([128, 128], F32)
    ident = pa_([128, 128], BF16)
    cw_sb = pa_([128, 8, 3], F32)
    ss = pa_([128, 8], F32)
    eps_sb = pa_([128, 1], F32)
    cf_sb = pa_([128, 8], F32)
    lq_sb = pa_([128, 4, 64], F32)
    lprod = pa_([128, 2, 64], F32)
    lsum = pa_([128, 4], F32)
    neglam = pa_([128, 1], F32)
    gs_sb = pa_([128, 128], F32)
    rr = pa_([128, 4, 2], F32)
    r2 = pa_([128, 4], F32)
    q1 = pa_([128, 4], F32)
    sel_sb = pa_([128, 2], F32)
    hmask_sb = pa_([128, 1], F32)
    assert pa_.off <= ARENA_BYTES, pa_.off
    pT = psA[:, 0:1024].rearrange("p (a b) -> p a b", a=8)
    pO = psA[:, 1024:2048]
    pP = [psB[:, i * 512:(i + 1) * 512] for i in range(4)]
    SP = [psA[:, 0:1024].rearrange("p (a b) -> p a b", a=2), psA[:, 1024:2048].rearrange("p (a b) -> p a b", a=2)]
    OA = psB[:, :].rearrange("p (a b) -> p a b", a=8)

    last = {}

    def dep(*names):
        out = []
        for n in names:
            out.extend(last.get(n, []))
        return out

    def setlast(name, *toks):
        last[name] = [t for t in toks if t is not None]

    def addlast(name, *toks):
        last.setdefault(name, []).extend([t for t in toks if t is not None])

    def phase_barrier():
        sch.barrier()
        last.clear()

    def MM(out, lhsT, rhs, start, stop, waits=(), signal=False):
        return sch.op("tensor", lambda e, o=out, l=lhsT, r=rhs, a=start, b=stop:
                      e.matmul(o, l, r, start=a, stop=b, skip_group_check=True), waits, signal)

    def ACT(out, in_, func, waits=(), accum_out=None, scale=None, bias=None):
        kw = {}
        if accum_out is not None:
            kw["accum_out"] = accum_out
        if scale is not None:
            kw["scale"] = scale
        if bias is not None:
            kw["bias"] = bias
        return sch.op("scalar", lambda e, o=out, i=in_, f=func, kw=kw: e.activation(out=o, in_=i, func=f, **kw), waits)

    def TT(eng, out, in0, in1, op, waits=()):
        return sch.op(eng, lambda e, o=out, a=in0, b=in1, p=op: e.tensor_tensor(out=o, in0=a, in1=b, op=p), waits)

    def TS(eng, out, in0, s1, s2, op0, op1=None, waits=()):
        if op1 is None:
            return sch.op(eng, lambda e, o=out, a=in0, x=s1, p=op0: e.tensor_scalar(out=o, in0=a, scalar1=x, scalar2=None, op0=p), waits)
        return sch.op(eng, lambda e, o=out, a=in0, x=s1, y=s2, p=op0, q=op1: e.tensor_scalar(out=o, in0=a, scalar1=x, scalar2=y, op0=p, op1=q), waits)

    def STT(eng, out, in0, scalar, in1, op0, op1, waits=()):
        return sch.op(eng, lambda e, o=out, a=in0, s=scalar, b=in1, p=op0, q=op1:
                      e.scalar_tensor_tensor(out=o, in0=a, scalar=s, in1=b, op0=p, op1=q), waits)

    def CP(eng, out, in_, waits=()):
        if eng == "scalar":
            return ACT(out, in_, AF.Copy, waits)
        return sch.op(eng, lambda e, o=out, i=in_: e.tensor_copy(out=o, in_=i), waits)

    dma_rr = [0]

    def dq():
        return "sync"

    t_id = sch.dma("sync", ident_f, ident_in[:, :], "c0")
    t_idc = ACT(ident, ident_f, AF.Copy, [t_id])
    t_eps = sch.op("vector", lambda e: e.memset(eps_sb, EPS))
    t_cf = sch.dma("sync", cf_sb, cfar[:, :], "c1")
    t_sel = sch.dma("sync", sel_sb, sel_in[:, :], "c2")
    t_hm = sch.dma("sync", hmask_sb, hmask_in[:, :], "c3")
    t_fg = sch.dma("sync", fg_sb, fg_rep[:, :], "c4")
    CONST = [t_idc, t_eps, t_cf, t_sel, t_hm, t_fg]

    final_toks = []

    def load_w(pieces):
        toks = []
        for idx, (src_ap, vfn, dst_ap) in enumerate(pieces):
            s = idx % 2
            td = sch.dma(dq(), vfn(wstage[s]), src_ap, f"w{s}", dep(f"wstage{s}"))
            tc_ = CP("vector" if idx % 2 == 0 else "scalar", dst_ap, vfn(wstage[s]), [td])
            setlast(f"wstage{s}", tc_)
            toks.append(tc_)
        return toks

    def w_in_pieces(layer, c_lo, c_hi):
        ps_ = []
        for k in range(8):
            for c0 in range(c_lo, c_hi, 2048):
                n = min(2048, c_hi - c0)
                ps_.append((w_in[layer, k * 128:(k + 1) * 128, c0:c0 + n], (lambda ws, n=n: ws[:, 0:n]),
                            w_in_sb[:, k, c0:c0 + n]))
        return ps_

    def w_qkv_pieces(j):
        return [(w_qkv[j, k * 128:(k + 1) * 128, :], (lambda ws: ws[:, 0:1536]), w_in_sb[:, k, 0:1536]) for k in range(8)]

    def w_out_pieces(layer):
        return [(w_out[layer, k * 128:(k + 2) * 128, :].rearrange("(a p) c -> p a c", p=128),
                 (lambda ws: ws[:, :].rearrange("p (a c) -> p a c", a=2)), w_out_sb[:, k:k + 2, :]) for k in range(0, 8, 2)]

    def issue_xloads(src_ap, tok0, par):
        toks = []
        for sub in range(4):
            nm = f"xt{par}_{sub}"
            w = dep(nm) + (dep("wstage0", "wstage1") if par == 1 else [])
            toks.append(sch.dma("sync", xt2[par][sub], src_ap[tok0 + sub * 128: tok0 + (sub + 1) * 128, :], f"x_{nm}", w))
        return toks

    def norm_tile(src_ap, row0, col, hslot, xb, name, nrows=128, mask=None, tl=None):
        if tl is None:
            tl = sch.dma(dq(), xb[0:nrows, :], src_ap[row0:row0 + nrows, :], f"x_{name}", dep(name))
        if mask is not None:
            tl = TS("vector", xb[0:nrows, :], xb[0:nrows, :], mask[0:nrows, 0:1], None, ALU.mult, None, [tl] + CONST)
        tsq = ACT(junk, xb, AF.Square, [tl] + CONST + dep(f"ss{col}", "junk"), accum_out=ss[:, col:col + 1])
        setlast("junk", tsq)
        tr1 = ACT(ss[:, col:col + 1], ss[:, col:col + 1], AF.Ln, [tsq], scale=1.0 / D, bias=eps_sb[:, 0:1])
        tr2 = ACT(ss[:, col:col + 1], ss[:, col:col + 1], AF.Exp, [tr1], scale=-0.5)
        th = STT("vector", hb[hslot], xb, ss[:, col:col + 1], g_sb, ALU.mult, ALU.mult,
                 [tr2] + dep(f"hb{hslot}", "g_sb"))
        setlast(f"ss{col}", th)
        setlast(name, th)
        return th

    def transpose_hb(th, hslot):
        w = [th] + dep("pT") + CONST
        tm = None
        for k in range(8):
            tm = MM(pT[:, k, :], hb[hslot][:, k * 128:(k + 1) * 128], ident, True, True,
                    w if k == 0 else [], signal=(k == 7))
        setlast(f"hb{hslot}", tm)
        return tm

    def load_g(layer):
        tg = sch.dma("sync", g_sb, g_rep[layer, :, :], "g", dep("g_sb"))
        setlast("g_sb", tg)

    def norm_to_hT(src, tok0, hs, par=0, xl=None):
        hw = []
        for sub in range(4):
            hslot = sub % 2
            th = norm_tile(src, tok0 + sub * 128, sub, hslot, xt2[par][sub], f"xt{par}_{sub}",
                           tl=(xl[sub] if xl is not None else None))
            tm = transpose_hb(th, hslot)
            tcp = CP("scalar" if sub % 2 == 0 else "vector", hT[hs][:, :, sub * 128:(sub + 1) * 128], pT,
                     [tm] + dep(f"hT{hs}"))
            setlast("pT", tcp)
            hw.append(tcp)
        return hw

    def outproj_residual(src, dst, tok0, yw, load_x, final, par=0, xl=None):
        tmm = None
        for sub in range(4):
            xs = sub % 2
            txl = xl[sub] if xl is not None else None
            for half in range(2):
                for j in range(8):
                    tmm = MM(pO[:, half * 512:(half + 1) * 512], yT[:, j, sub * 128:(sub + 1) * 128],
                             w_out_sb[:, j, half * 512:(half + 1) * 512], j == 0, j == 7,
                             (yw + dep("pO", "w_out_sb")) if (j == 0 and half == 0) else [],
                             signal=(j == 7 and half == 1))
            tad = TT("vector", xo[xs], pO, xt2[par][sub], ALU.add, [tmm, txl] + dep(f"xo{xs}"))
            setlast("pO", tad)
            setlast(f"xt{par}_{sub}", tad)
            if final:
                c_ = 4 + xs
                tsq = ACT(junk, xo[xs], AF.Square, [tad] + CONST + dep(f"ss{c_}", "junk"), accum_out=ss[:, c_:c_ + 1])
                setlast("junk", tsq)
                tr1 = ACT(ss[:, c_:c_ + 1], ss[:, c_:c_ + 1], AF.Ln, [tsq], scale=1.0 / D, bias=eps_sb[:, 0:1])
                tr2 = ACT(ss[:, c_:c_ + 1], ss[:, c_:c_ + 1], AF.Exp, [tr1], scale=-0.5)
                tad = STT("vector", xo[xs], xo[xs], ss[:, c_:c_ + 1], fg_sb, ALU.mult, ALU.mult, [tr2])
                setlast(f"ss{c_}", tad)
            tst = sch.dma(dq(), dst[tok0 + sub * 128: tok0 + (sub + 1) * 128, :], xo[xs], f"xo{xs}", [tad])
            setlast(f"xo{xs}", tst)
            final_toks.append(tst)
        setlast("yT", tmm)
        return tmm

    def conv_stage(layer, cj, src, dst, halo_ap, use_mask):
        load_g(layer)
        tcw = sch.dma("sync", cw_sb, conv_w[cj].rearrange("(j p) k -> p j k", p=128), "cw", dep("cw_sb"))
        setlast("cw_sb", tcw)
        wt = load_w(w_in_pieces(layer, 0, 4096) + w_out_pieces(layer))
        setlast("w_in_sb", *wt)
        setlast("w_out_sb", *wt)
        tz = sch.op("gpsimd", lambda e: e.memset(xh, 0.0), dep("xh"))
        setlast("xh", tz)
        th = norm_tile(halo_ap, 0, 4, 0, xh, "xh", nrows=2, mask=hmask_sb if use_mask else None)
        tm = transpose_hb(th, 0)
        tcp = ACT(hT_halo, pT, AF.Copy, [tm] + dep("hT_halo"))
        setlast("pT", tcp)
        for j in range(8):
            tk = {}
            for gi in (1, 2):
                for k in range(8):
                    tk[gi] = MM(pP[gi][:, 0:2], w_in_sb[:, k, gi * 1024 + j * 128: gi * 1024 + (j + 1) * 128],
                                hT_halo[:, k, 0:2], k == 0, k == 7,
                                ([tcp] + dep(f"pP{gi}", "w_in_sb")) if k == 0 else [], signal=(k == 7))
            tuc = ACT(ut[:, 0:2], pP[2][:, 0:2], AF.Copy, [tk[2]] + dep("ut"))
            tv = TT("vector", vbuf[j][:, 0:2], pP[1][:, 0:2], ut[:, 0:2], ALU.mult, [tuc, tk[1]] + dep(f"vbuf{j}"))
            setlast("ut", tv)
            setlast("pP1", tv)
            setlast("pP2", tuc)
            setlast(f"vbuf{j}", tv)
        setlast("hT_halo", tk[2])
        xl_next = issue_xloads(src, 0, 0)
        for T in range(NT):
            tok0 = T * 512
            hs = T % 2
            par = T % 2
            xl_cur = xl_next
            if T + 1 < NT:
                xl_next = issue_xloads(src, (T + 1) * 512, 1 - par)
            hw = norm_to_hT(src, tok0, hs, par, xl_cur)
            yw = []
            tmm = None
            for j in range(8):
                tg_ = {}
                for g_ in (1, 2, 3, 0):
                    for k in range(8):
                        tmm = MM(pP[g_], w_in_sb[:, k, g_ * 1024 + j * 128: g_ * 1024 + (j + 1) * 128],
                                 hT[hs][:, k, :], k == 0, k == 7,
                                 (hw + dep(f"pP{g_}", "w_in_sb")) if k == 0 else [], signal=(k == 7))
                    tg_[g_] = tmm
                tb, tc1, tu1, tz1 = tg_[0], tg_[1], tg_[2], tg_[3]
                tuc = ACT(ut, pP[2], AF.Copy, [tu1] + dep("ut"))
                setlast("pP2", tuc)
                tsz = ACT(sz, pP[3], AF.Silu, [tz1] + dep("sz"))
                setlast("pP3", tsz)
                tv = TT("vector", vbuf[j][:, 2:514], pP[1], ut, ALU.mult, [tuc, tc1] + dep(f"vbuf{j}"))
                setlast("pP1", tv)
                setlast("ut", tv)
                ta = ACT(t1, vbuf[j][:, 0:512], AF.Copy, [tv] + dep("t1", "cw_sb"), scale=cw_sb[:, j, 0:1])
                tb_ = STT("vector", t1, vbuf[j][:, 1:513], cw_sb[:, j, 1:2], t1, ALU.mult, ALU.add, [ta])
                tc_ = STT("vector", t1, vbuf[j][:, 2:514], cw_sb[:, j, 2:3], t1, ALU.mult, ALU.add, [tb_])
                tcar = CP("vector", vbuf[j][:, 0:2], vbuf[j][:, 512:514], [tc_])
                setlast(f"vbuf{j}", tcar)
                td_ = TT("vector", t2, pP[0], t1, ALU.mult, [tc_, tb] + dep("t2"))
                setlast("pP0", td_)
                setlast("t1", td_)
                te_ = TT("vector", yT[:, j, :], t2, sz, ALU.mult, [td_, tsz] + dep("yT"))
                setlast("sz", te_)
                setlast("t2", te_)
                yw.append(te_)
            setlast(f"hT{hs}", tmm)
            outproj_residual(src, dst, tok0, yw, load_x=False, final=False, par=par)

    def pz_stage(layer, src):
        load_g(layer)
        wt = load_w(w_in_pieces(layer, 3072, 4096))
        setlast("w_in_sb", *wt)
        ring = 0
        xl_next = issue_xloads(src, 0, 0)
        for T in range(NT):
            tok0 = T * 512
            hs = T % 2
            par = T % 2
            xl_cur = xl_next
            if T + 1 < NT:
                xl_next = issue_xloads(src, (T + 1) * 512, 1 - par)
            hw = norm_to_hT(src, tok0, hs, par, xl_cur)
            tsh = sch.dma(dq(), HTown3[:, :, tok0:tok0 + 512].rearrange("k p t -> p k t"), hT[hs], f"hst{hs}", hw)
            final_toks.append(tsh)
            tmm = None
            for j in range(8):
                b_ = ring % 4
                ring += 1
                for k in range(8):
                    tmm = MM(pP[b_], w_in_sb[:, k, 3072 + j * 128: 3072 + (j + 1) * 128], hT[hs][:, k, :], k == 0, k == 7,
                             (hw + dep(f"pP{b_}", "w_in_sb")) if k == 0 else [], signal=(k == 7))
                tev = ACT(ev[b_], pP[b_], AF.Silu, [tmm] + dep(f"ev{b_}"))
                setlast(f"pP{b_}", tev)
                tst = sch.dma(dq(), ZTd[j, :, tok0:tok0 + 512], ev[b_], f"ev{b_}", [tev])
                setlast(f"ev{b_}", tst)
                final_toks.append(tst)
            setlast(f"hT{hs}", tmm, tsh)

    def hproj_stage(j_attn, preloaded=False):
        if not preloaded:
            wt = load_w(w_qkv_pieces(j_attn))
            setlast("w_in_sb", *wt)
        ring = 0
        for T in range(16):
            rk, lt = T // 8, T % 8
            hs = T % 2
            g0 = T * 512
            hw = []
            for pc in range(4):
                tl = sch.dma(dq(), hT[hs][:, 2 * pc:2 * pc + 2, :],
                             HTall5[pc, rk, :, :, lt * 512:(lt + 1) * 512].rearrange("kk p t -> p kk t"),
                             f"hld{hs}", dep(f"hT{hs}"))
                hw.append(tl)
            tmm = None
            for jj in range(8):
                b_ = ring % 4
                ring += 1
                for k in range(8):
                    tmm = MM(pP[b_], w_in_sb[:, k, jj * 128:(jj + 1) * 128], hT[hs][:, k, :], k == 0, k == 7,
                             (hw + dep(f"pP{b_}", "w_in_sb")) if k == 0 else [], signal=(k == 7))
                tev = CP("scalar" if jj % 2 == 0 else "vector", ev[b_], pP[b_], [tmm] + dep(f"ev{b_}"))
                setlast(f"pP{b_}", tev)
                dstT = QTa[jj, :, g0:g0 + 512] if jj < 4 else KTa[jj - 4, :, g0:g0 + 512]
                tst = sch.dma(dq(), dstT, ev[b_], f"ev{b_}", [tev])
                setlast(f"ev{b_}", tst)
                final_toks.append(tst)
            for sub in range(4):
                b_ = ring % 4
                ring += 1
                for k in range(8):
                    tmm = MM(pP[b_], hT[hs][:, k, sub * 128:(sub + 1) * 128], w_in_sb[:, k, 1024:1536], k == 0, k == 7,
                             (hw + dep(f"pP{b_}", "w_in_sb")) if k == 0 else [], signal=(k == 7))
                tev = CP("scalar" if sub % 2 == 0 else "vector", ev[b_], pP[b_], [tmm] + dep(f"ev{b_}"))
                setlast(f"pP{b_}", tev)
                tst = sch.dma(dq(), Va[:, g0 + sub * 128:g0 + (sub + 1) * 128, :].rearrange("h t e -> t h e"),
                              ev[b_][:, :].rearrange("p (h e) -> p h e", h=4), f"ev{b_}", [tev])
                setlast(f"ev{b_}", tst)
                final_toks.append(tst)
            setlast(f"hT{hs}", tmm)

    def aout_stage(layer, src, dst, final, preloaded=False):
        if not preloaded:
            wt = load_w(w_out_pieces(layer))
            setlast("w_out_sb", *wt)
        xl_next = issue_xloads(src, 0, 0)
        for T in range(NT):
            tok0 = T * 512
            par = T % 2
            xl_cur = xl_next
            if T + 1 < NT:
                xl_next = issue_xloads(src, (T + 1) * 512, 1 - par)
            tl1 = sch.dma("sync", onl, ONTr3[:, :, tok0:tok0 + 512].rearrange("j p t -> p j t"), "onl", dep("onl"))
            tl2 = sch.dma("sync", ztl, ZTd[:, :, tok0:tok0 + 512].rearrange("j p t -> p j t"), "ztl", dep("ztl"))
            ty = TT("vector", yT, onl, ztl, ALU.mult, [tl1, tl2] + dep("yT"))
            setlast("onl", ty)
            setlast("ztl", ty)
            outproj_residual(src, dst, tok0, [ty], load_x=True, final=final, par=par, xl=xl_cur)

    def attn_stage(j_attn, lam_init):
        t_lq = sch.dma("sync", lq_sb, lqk[j_attn], "lq")
        t_gs = sch.dma("sync", gs_sb, gsub_in[j_attn], "gs")
        t_gs2 = TS("vector", gs_sb, gs_sb, 1.0 - lam_init, None, ALU.mult, None, [t_gs])
        t_p = TT("vector", lprod, lq_sb[:, 0:4:2, :], lq_sb[:, 1:4:2, :], ALU.mult, [t_lq])
        t_s1 = ACT(junkA[:, 0:64], lprod[:, 0, :], AF.Copy, [t_p] + dep("junkA"), accum_out=lsum[:, 0:1])
        t_s2 = ACT(junkA[:, 0:64], lprod[:, 1, :], AF.Copy, [t_s1], accum_out=lsum[:, 1:2])
        t_e = ACT(lsum[:, 2:4], lsum[:, 0:2], AF.Exp, [t_s2])
        t_l1 = TT("vector", neglam, lsum[:, 3:4], lsum[:, 2:3], ALU.subtract, [t_e])
        t_l2 = TS("vector", neglam, neglam, -lam_init, None, ALU.add, None, [t_l1])
        setlast("junkA", t_s2)
        for i in range(2):
            tv1 = sch.op("gpsimd", lambda e, i=i: e.memset(VS[i][:, :, 128:130], 1.0))
            setlast(f"VS{i}", tv1)

        def load_head(h):
            s = h % 2
            toks = []
            toks.append(sch.dma("sync", QT[s][:, 0:S // 2], QTa[h, :, 0:S // 2], f"hd{s}", dep(f"QT{s}")))
            toks.append(sch.dma("sync", QT[s][:, S // 2:S], QTa[h, :, S // 2:S], f"hd{s}", dep(f"QT{s}")))
            toks.append(sch.dma("sync", KT[s][:, 0:S // 2], KTa[h, :, 0:S // 2], f"hd{s}", dep(f"KT{s}")))
            toks.append(sch.dma("sync", KT[s][:, S // 2:S], KTa[h, :, S // 2:S], f"hd{s}", dep(f"KT{s}")))
            vv = Va[h].rearrange("(c p) e -> p c e", p=128)
            for qd in range(4):
                toks.append(sch.dma("sync", VS[s][:, qd * 16:(qd + 1) * 16, 0:128],
                                    vv[:, qd * 16:(qd + 1) * 16, :], f"hd{s}", dep(f"VS{s}")))
            tz = sch.dma("sync", ZC[s], Zb[2 * h:2 * h + 2].rearrange("m p w -> p m w"), f"hz{s}", dep(f"ZC{s}"))
            tzc = None
            for m in range(2):
                tzc = TS("vector", ZC[s][:, m, :], ZC[s][:, m, :], cf_sb[:, 2 * h + m:2 * h + m + 1], None, ALU.subtract,
                         None, [tz] + CONST)
            toks.append(tzc)
            return toks

        sring = [0]
        pring = [0]
        bring = [0]
        pending_epi = [None]

        def emit_epilogue_pe(info):
            h, j, es, ton = info
            sl = sring[0] % 2
            sring[0] += 1
            w0 = dep(f"SP{sl}") + CONST
            tms = []
            for s4 in range(4):
                tm = MM(SP[sl][:, 0, s4 * 128:(s4 + 1) * 128], onb[s4], ident, True, True,
                        (w0 + [ton[s4]]) if s4 == 0 else [ton[s4]], signal=True)
                tms.append(tm)
                setlast(f"onb{s4}", tm)
            os_ = j % 2
            tca = TS("vector", onTa[os_], SP[sl][:, 0, :], sel_sb[:, 0:1], None, ALU.mult, None,
                     [tms[-1]] + dep(f"onTa{os_}"))
            tcb = TS("vector", onTb[os_], SP[sl][:, 0, :], sel_sb[:, 1:2], None, ALU.mult, None,
                     [tca] + dep(f"onTb{os_}"))
            setlast(f"SP{sl}", tca, tcb)
            d_, c0 = j // 8, (j % 8) * 512
            tsa = sch.dma("sync", RS5[0, d_, h, :, c0:c0 + 512], onTa[os_], f"onta{os_}", [tca])
            tsb = sch.dma("sync", RS5[1, d_, h, :, c0:c0 + 512], onTb[os_], f"ontb{os_}", [tcb])
            setlast(f"onTa{os_}", tsa)
            setlast(f"onTb{os_}", tsb)
            final_toks.append(tsa)
            final_toks.append(tsb)

        head_toks = {0: load_head(0)}
        unit_idx = 0
        for h in range(4):
            s = h % 2
            if h + 1 < 4:
                head_toks[h + 1] = load_head(h + 1)
            hw = head_toks[h]
            last_pe_read = None
            for j in range(16):
                nchunks = 4 * j + 4
                q0 = j * 512

                def issue_qk(c):
                    i = c - 4 * j
                    lo = 128 * i if i > 0 else 0
                    sl = sring[0] % 2
                    sring[0] += 1
                    w = dep(f"SP{sl}") + (hw if c == 0 else [])
                    tq = None
                    for m in range(2):
                        tq = MM(SP[sl][:, m, lo:512], KT[s][m * 64:(m + 1) * 64, c * 128:(c + 1) * 128],
                                QT[s][m * 64:(m + 1) * 64, q0 + lo:q0 + 512], True, True, w if m == 0 else [],
                                signal=(m == 1))
                    pl = pring[0] % NP
                    pring[0] += 1
                    if i >= -1:
                        o_idx = 3 - i
                        bl = bring[0] % 2
                        bring[0] += 1
                        tb = STT("vector", SSB[bl][:, :, lo:512], SP[sl][:, :, lo:512], 0.125,
                                 ZC[s][:, :, o_idx * 128 + lo:o_idx * 128 + 512], ALU.mult, ALU.add,
                                 [tq] + dep(f"SSB{bl}") + hw)
                        setlast(f"SP{sl}", tb)
                        te = ACT(PT[pl][:, :, lo:512], SSB[bl][:, :, lo:512], AF.Exp, [tb] + dep(f"PT{pl}"))
                        setlast(f"SSB{bl}", te)
                    else:
                        te = ACT(PT[pl][:, :, lo:512], SP[sl][:, :, lo:512], AF.Exp, [tq] + dep(f"PT{pl}"), scale=0.125)
                        setlast(f"SP{sl}", te)
                    return (c, i, pl, te)

                def issue_av(tile):
                    c, i, pl, te = tile
                    tm = None
                    first = True
                    for s4 in range(4):
                        if s4 < i:
                            continue
                        lastc = 4 * j + s4
                        for m in range(2):
                            a = s4 * 2 + m
                            w = []
                            if first:
                                w = [te] + (dep("OA") if c == 0 else [])
                                first = False
                            tm = MM(OA[:, a, 0:129], PT[pl][:, m, s4 * 128:(s4 + 1) * 128], VS[s][:, c, 0:129],
                                    (c == 0 and m == 0), (c == lastc), w, signal=(s4 == 3 and m == 1))
                    setlast(f"PT{pl}", tm)
                    return tm

                LAG = 4
                pendq = []
                tav = None
                for c in range(nchunks):
                    pendq.append(issue_qk(c))
                    if len(pendq) > LAG:
                        tav = issue_av(pendq.pop(0))
                    if c == min(6, nchunks - 1) and pending_epi[0] is not None:
                        emit_epilogue_pe(pending_epi[0])
                        pending_epi[0] = None
                while pendq:
                    tav = issue_av(pendq.pop(0))
                last_pe_read = tav
                es = unit_idx % 2
                unit_idx += 1
                tev = CP("vector", OEV[es], OA[:, :, 0:129], [tav] + dep(f"OEV{es}"))
                setlast("OA", tev)
                trr = sch.op("vector", lambda e, es=es: e.reciprocal(out=rr[:, :, :].rearrange("p s m -> p (s m)"),
                                                                     in_=OEV[es][:, :, 128:129].rearrange("p a o -> p (a o)")),
                             [tev] + dep("rr"))
                tr2 = TS("vector", r2, rr[:, :, 1], neglam[:, 0:1], None, ALU.mult, None, [trr, t_l2] + dep("r2"))
                ton = []
                tlast = None
                tos = []
                tqs = []
                for s4 in range(4):
                    ta = TS("gpsimd", tA[s4], OEV[es][:, 2 * s4 + 1, 0:128], r2[:, s4:s4 + 1], None, ALU.mult, None,
                            [tr2] + dep(f"tA{s4}"))
                    to = STT("vector", oo[s4], OEV[es][:, 2 * s4, 0:128], rr[:, s4, 0:1], tA[s4], ALU.mult, ALU.add,
                             [ta, trr] + dep(f"oo{s4}"))
                    setlast(f"tA{s4}", to)
                    tq_ = ACT(junkA, oo[s4], AF.Square, [to] + CONST + dep("q1", "junkA"), accum_out=q1[:, s4:s4 + 1])
                    setlast("junkA", tq_)
                    tos.append(to)
                    tqs.append(tq_)
                tl_ = ACT(q1[:, 0:4], q1[:, 0:4], AF.Ln, tqs, scale=1.0 / 128, bias=eps_sb[:, 0:1])
                tx_ = ACT(q1[:, 0:4], q1[:, 0:4], AF.Exp, [tl_], scale=-0.5)
                for s4 in range(4):
                    tn_ = STT("vector", onb[s4], oo[s4], q1[:, s4:s4 + 1], gs_sb, ALU.mult, ALU.mult,
                              [tx_, t_gs2] + dep(f"onb{s4}"))
                    setlast(f"oo{s4}", tn_)
                    ton.append(tn_)
                    tlast = tn_
                setlast("q1", tlast)
                setlast("rr", tlast)
                setlast("r2", tlast)
                setlast(f"OEV{es}", tlast)
                if pending_epi[0] is not None:
                    emit_epilogue_pe(pending_epi[0])
                pending_epi[0] = (h, j, es, ton)
            for nm in (f"QT{s}", f"KT{s}", f"VS{s}"):
                addlast(nm, last_pe_read)
            setlast(f"ZC{s}", ("e_vector", sch.cnt.get("e_vector", 0)))
        if pending_epi[0] is not None:
            emit_epilogue_pe(pending_epi[0])

    def collective(kind, op, src2d, dst2d, name):
        return sch.coll(lambda e, k=kind, o=op, a=src2d, b=dst2d: e.collective_compute(
            k, o, replica_groups=GROUPS, ins=[a], outs=[b]), name)

    xs_ = [x0, X1, X2, X3]
    phase_no = [0]

    def want():
        phase_no[0] += 1
        return phase_no[0] <= upto

    for blk in range(2):
        l_conv, l_attn = 2 * blk, 2 * blk + 1
        src = xs_[l_conv]
        mid = xs_[l_conv + 1]
        dst = xs_[l_conv + 2] if blk == 0 else out_ap
        if want():
            if blk == 0:
                conv_stage(l_conv, blk, src, mid, xhalo, use_mask=False)
            else:
                conv_stage(l_conv, blk, src, mid, TAILall[0:2, :], use_mask=True)
            phase_barrier()
        if want():
            pz_stage(l_attn, mid)
            phase_barrier()
        if want():
            for pc in range(4):
                collective("AllGather", ALU.bypass, HTown[pc * 256:(pc + 1) * 256, :], HTall[pc * 512:(pc + 1) * 512, :], f"ag{pc}")
            load_w(w_qkv_pieces(blk))
            phase_barrier()
        if want():
            hproj_stage(blk, preloaded=True)
            phase_barrier()
        if want():
            attn_stage(blk, lam_inits[blk])
            phase_barrier()
        if want():
            for pc in range(2):
                collective("ReduceScatter", ALU.add, RSbuf[pc * 1024:(pc + 1) * 1024, :], ONTr[pc * 512:(pc + 1) * 512, :], f"rs{pc}")
            load_w(w_out_pieces(l_attn))
            phase_barrier()
        if want():
            aout_stage(l_attn, mid, dst, final=(blk == 1), preloaded=True)
            phase_barrier()
        if blk == 0 and want():
            ttail = sch.dma("sync", TAILown[:, :], X2[TOK - 2:TOK, :], "tail")
            final_toks.append(ttail)
            phase_barrier()
            collective("AllGather", ALU.bypass, TAILown, TAILall, "agt")
            phase_barrier()
    if debug_out:
        dbg = nc.dram_tensor("dbg", [TOK, D], F32, kind="ExternalOutput").ap()
        for q_ in range(4):
            td_ = sch.dma("sync", dbg[q_ * 1024:(q_ + 1) * 1024, :], debug_out[q_ * 1024:(q_ + 1) * 1024, :], "dbg")
            final_toks.append(td_)

    mx = {}
    for (k, v) in final_toks:
        mx[k] = max(mx.get(k, 0), v)
    run_sched(nc, sch, list(mx.items()))
    est.close()
    return nc


BF = ml_dtypes.bfloat16
NUM_BUCKETS, MAX_EXACT, REL_MAX = 32, 16, 128
MASKV = -30000.0

def bucket_table(n):
    d = np.arange(n, dtype=np.int64)
    ds = np.maximum(d, 1).astype(np.float32)
    large = MAX_EXACT + (np.log(ds / np.float32(MAX_EXACT)) / np.float32(math.log(REL_MAX / MAX_EXACT))
                         * np.float32(NUM_BUCKETS - MAX_EXACT)).astype(np.int32)
    large = np.minimum(large, NUM_BUCKETS - 1)
    return np.where(d < MAX_EXACT, d, large).astype(np.int64)

def common_inputs(inp):
    rep = lambda a: np.ascontiguousarray(np.broadcast_to(a[..., None, :], a.shape[:-1] + (128, a.shape[-1]))).astype(np.float32)
    return {
        "w_in": np.ascontiguousarray(inp["w_in"], dtype=np.float32),
        "w_out": np.ascontiguousarray(inp["w_out"], dtype=np.float32),
        "norm_g_rep": rep(np.asarray(inp["norm_g"])),
        "final_g_rep": rep(np.asarray(inp["final_g"])),
        "conv_w": np.ascontiguousarray(inp["conv_w"], dtype=np.float32),
        "ident": np.eye(128, dtype=np.float32),
    }

def attn_consts(inp, layer_j, core_r):
    rb = np.asarray(inp["rel_bias"], dtype=np.float32)
    bk = bucket_table(1024)
    maps = np.arange(8) + 8 * core_r
    p = np.arange(128)[:, None]
    w = np.arange(1024)[None, :]
    d = w - 384 - p
    dd = np.clip(d, 0, 1023)
    Zb = np.empty((8, 128, 1024), np.float32)
    for mi, m in enumerate(maps):
        vals = rb[bk[dd], m]
        Zb[mi] = np.where(d >= 0, vals, np.float32(MASKV))
    cf = np.ascontiguousarray(np.broadcast_to(rb[31, maps][None, :], (128, 8))).astype(np.float32)
    lqk = np.stack([np.asarray(inp[k])[layer_j] for k in ("lambda_q1", "lambda_k1", "lambda_q2", "lambda_k2")], 0)
    lqk = np.ascontiguousarray(np.broadcast_to(lqk[None], (128, 4, 64))).astype(np.float32)
    gs = np.ascontiguousarray(np.broadcast_to(np.asarray(inp["subln_g"])[layer_j][None, :], (128, 128))).astype(np.float32)
    return {"Zb": Zb, "cfar": cf, "lqk": lqk, "gsub": gs, "ident": np.eye(128, dtype=np.float32)}


def fused_inputs(inp):
    com = common_inputs(inp)
    x = inp["x"]
    w_in = inp["w_in"]
    lqk = np.stack([np.stack([np.asarray(inp[k])[j] for k in ("lambda_q1", "lambda_k1", "lambda_q2", "lambda_k2")], 0)
                    for j in range(2)], 0)
    lqk = np.ascontiguousarray(np.broadcast_to(lqk[:, None], (2, 128, 4, 64))).astype(np.float32)
    gs = np.ascontiguousarray(np.broadcast_to(np.asarray(inp["subln_g"])[:, None, :], (2, 128, 128))).astype(np.float32)
    maps = []
    for c in range(8):
        b, r = c // 2, c % 2
        m = dict(com)
        m["x0"] = np.ascontiguousarray(x[b, r * TOK:(r + 1) * TOK])
        m["xhalo"] = np.zeros((2, D), np.float32) if r == 0 else np.ascontiguousarray(x[b, TOK - 2:TOK])
        wq = []
        for j in range(2):
            l = 2 * j + 1
            cols = [w_in[l][:, base + r * 512: base + (r + 1) * 512] for base in (0, 1024, 2048)]
            wq.append(np.concatenate(cols, axis=1))
        m["w_qkv"] = np.ascontiguousarray(np.stack(wq, 0)).astype(np.float32)
        ac = attn_consts(inp, 0, r)
        m["Zb"] = ac["Zb"]
        m["cfar"] = ac["cfar"]
        m["lqk"] = lqk
        m["gsub"] = gs
        sel = np.zeros((128, 2), np.float32)
        sel[:, r] = 1.0
        m["sel"] = sel
        m["hmask"] = np.full((128, 1), float(r), np.float32)
        maps.append(m)
    return maps


_NC = {}


def _lam_init(layer_idx):
    return 0.8 - 0.6 * math.exp(-0.3 * layer_idx)


def kernel(x, norm_g, w_in, w_out, conv_w, lambda_q1, lambda_k1, lambda_q2, lambda_k2,
           subln_g, rel_bias, final_g):
    inp = {"x": np.asarray(x, np.float32), "norm_g": np.asarray(norm_g, np.float32),
           "w_in": np.asarray(w_in, np.float32), "w_out": np.asarray(w_out, np.float32),
           "conv_w": np.asarray(conv_w, np.float32),
           "lambda_q1": np.asarray(lambda_q1, np.float32), "lambda_k1": np.asarray(lambda_k1, np.float32),
           "lambda_q2": np.asarray(lambda_q2, np.float32), "lambda_k2": np.asarray(lambda_k2, np.float32),
           "subln_g": np.asarray(subln_g, np.float32), "rel_bias": np.asarray(rel_bias, np.float32),
           "final_g": np.asarray(final_g, np.float32)}
    if "nc" not in _NC:
        _NC["nc"] = build_fused((_lam_init(1), _lam_init(3)))
    maps = fused_inputs(inp)
    res = run_bass_kernel_spmd(_NC["nc"], maps, core_ids=list(range(8)))
    out = np.empty((4, S, D), np.float32)
    for c in range(8):
        out[c // 2, (c % 2) * TOK:(c % 2 + 1) * TOK] = res.results[c]["out"]
    return out
```
